# Optimizing a Trainium2 kernel written in Bass

```python
import jax, jax.numpy as jnp
from jax import lax
import numpy as np

D_MODEL = 1024
BATCH = 8
SEQ = 2048
DEPTH = 2
DEC_BATCH = 128
DEC_SEQ = 8
PAST_LEN = 2048
PAGE_SIZE = 128

CONV_DIM = D_MODEL // 2
CONV_WIDTH = 31
SB_HEADS = 8
SB_HEAD_DIM = 64
SB_DIM = SB_HEADS * SB_HEAD_DIM
SB_BIAS_INIT = -6.0
Q_BLOCK = 128
SC_DIM = D_MODEL // 2
SC_WIDTH = 3
POOL_DIM = D_MODEL // 2
POOL_WINDOWS = (2, 4, 8, 16)
POOL_GROUP = POOL_DIM // len(POOL_WINDOWS)
POOL_MAX = max(POOL_WINDOWS)
FFN_DIM = ((-(-8 * D_MODEL // 3) + 255) // 256) * 256
N_EVEN = (DEPTH + 1) // 2
N_ODD = DEPTH // 2
EPS = 1e-6

kernel_name = "hybrid_conformer_stickbreak_shortconv_pool_decode_step"


def rms_norm(x, g):
    xf = x.astype(jnp.float32)
    y = xf * lax.rsqrt(jnp.mean(xf * xf, axis=-1, keepdims=True) + EPS)
    return (y * g.astype(jnp.float32)).astype(x.dtype)


def layer_norm(x, g, b):
    xf = x.astype(jnp.float32)
    mu = jnp.mean(xf, axis=-1, keepdims=True)
    var = jnp.mean(jnp.square(xf - mu), axis=-1, keepdims=True)
    y = (xf - mu) * lax.rsqrt(var + EPS) * g.astype(jnp.float32) + b.astype(jnp.float32)
    return y.astype(x.dtype)


def causal_depthwise_conv(ext, w):
    c = ext.shape[-1]
    return lax.conv_general_dilated(ext, w[:, None, :].astype(ext.dtype), window_strides=(1,), padding='VALID',
                                    dimension_numbers=('NWC', 'WIO', 'NWC'), feature_group_count=c)


def stick_breaking_attention(q, k, v, bias, q_pos, k_pos):
    n, lq, h, dh = q.shape
    qb = min(Q_BLOCK, lq)
    nb = lq // qb
    scale = dh ** -0.5
    qs = q.reshape(n, nb, qb, h, dh).transpose(1, 0, 2, 3, 4)
    ps = q_pos.reshape(nb, qb)
    bias_f = bias.astype(jnp.float32)[None, :, None, None]

    def block(args):
        qblk, pblk = args
        z = jnp.einsum('nqhd,nkhd->nhqk', qblk, k, preferred_element_type=jnp.float32) * scale + bias_f
        mask = k_pos[None, :] < pblk[:, None]
        log_1m = jnp.where(mask, jax.nn.log_sigmoid(-z), 0.0)
        after = lax.cumsum(log_1m, axis=3, reverse=True) - log_1m
        logw = jnp.where(mask, jax.nn.log_sigmoid(z) + after, -jnp.inf)
        wgt = jnp.exp(logw)
        return jnp.einsum('nhqk,nkhd->nqhd', wgt.astype(v.dtype), v)

    out = lax.map(block, (qs, ps))
    return out.transpose(1, 0, 2, 3, 4).reshape(n, lq, h * dh)


def conformer_sb_mixer(hn, conv_prev, k_past, v_past, start, w_in, conv_w, conv_b, ln_g, ln_b, sb_bias, w_out):
    n, L, _ = hn.shape
    proj = hn @ w_in
    a_val, a_gate, q, k, v = jnp.split(
        proj, [CONV_DIM, 2 * CONV_DIM, 2 * CONV_DIM + SB_DIM, 2 * CONV_DIM + 2 * SB_DIM], axis=-1)
    glu = a_val * jax.nn.sigmoid(a_gate)
    ext = jnp.concatenate([conv_prev, glu], axis=1)
    a = causal_depthwise_conv(ext, conv_w) + conv_b
    a = jax.nn.silu(layer_norm(a, ln_g, ln_b))
    new_conv = ext[:, -(CONV_WIDTH - 1):]
    q = q.reshape(n, L, SB_HEADS, SB_HEAD_DIM)
    k = k.reshape(n, L, SB_HEADS, SB_HEAD_DIM)
    v = v.reshape(n, L, SB_HEADS, SB_HEAD_DIM)
    k_all = jnp.concatenate([k_past, k], axis=1)
    v_all = jnp.concatenate([v_past, v], axis=1)
    q_pos = start + jnp.arange(L, dtype=jnp.int32)
    k_pos = jnp.arange(k_all.shape[1], dtype=jnp.int32)
    b = stick_breaking_attention(q, k_all, v_all, sb_bias, q_pos, k_pos)
    out = jnp.concatenate([a, b], axis=-1) @ w_out
    return out, new_conv, k, v


def shortconv_pool_mixer(hn, sc_prev, pool_prev, start, w_in, sc_w, pool_w, pool_scale, w_out):
    n, L, _ = hn.shape
    proj = hn @ w_in
    gb, gc, xv, u = jnp.split(proj, [SC_DIM, 2 * SC_DIM, 3 * SC_DIM], axis=-1)
    ext = jnp.concatenate([sc_prev, gc * xv], axis=1)
    c = gb * causal_depthwise_conv(ext, sc_w)
    new_sc = ext[:, -(SC_WIDTH - 1):]
    p = POOL_MAX - 1
    pext = jnp.concatenate([pool_prev, u], axis=1)
    cs = jnp.pad(jnp.cumsum(pext.astype(jnp.float32), axis=1), ((0, 0), (1, 0), (0, 0)))
    pos = start + jnp.arange(L, dtype=jnp.int32)
    groups = []
    for gi, w in enumerate(POOL_WINDOWS):
        lo, hi = gi * POOL_GROUP, (gi + 1) * POOL_GROUP
        wsum = cs[:, p + 1:p + 1 + L, lo:hi] - cs[:, p + 1 - w:p + 1 - w + L, lo:hi]
        cnt = jnp.minimum(w, pos + 1).astype(jnp.float32)[None, :, None]
        groups.append(wsum / cnt)
    pooled = jnp.concatenate(groups, axis=-1).astype(u.dtype) - u
    pooled = pooled.reshape(n, L, len(POOL_WINDOWS), POOL_GROUP)
    d = jnp.einsum('nlgc,gcd->nlgd', pooled, pool_w).reshape(n, L, POOL_DIM) * pool_scale
    new_pool = pext[:, -p:]
    out = jnp.concatenate([c, d], axis=-1) @ w_out
    return out, new_sc, new_pool


def swiglu(h, w_in, w_out):
    g, u = jnp.split(h @ w_in, 2, axis=-1)
    return (jax.nn.silu(g) * u) @ w_out


def setup_inputs(seed: int = 0) -> dict:
    key = jax.random.key(seed)
    ks = jax.random.split(key, 32)
    n_pages = PAST_LEN // PAGE_SIZE
    n_pool = (5 * DEC_BATCH * n_pages + 3) // 4
    f32 = jnp.float32
    nrm = lambda k, shape, s: (jax.random.normal(k, shape, f32) * s).astype(f32)
    perm = jax.random.permutation(ks[7], n_pool)
    page_table = perm[:DEC_BATCH * n_pages].reshape(DEC_BATCH, n_pages).astype(jnp.int32)
    e_in = 2 * CONV_DIM + 3 * SB_DIM
    o_in = 3 * SC_DIM + POOL_DIM
    return {
        "x_prompt": nrm(ks[0], (BATCH, SEQ, D_MODEL), 1.0),
        "x_sample": nrm(ks[1], (DEC_BATCH, DEC_SEQ, D_MODEL), 1.0),
        "cache_k": nrm(ks[2], (N_EVEN, n_pool, PAGE_SIZE, SB_HEADS, SB_HEAD_DIM), 1.0),
        "cache_v": nrm(ks[3], (N_EVEN, n_pool, PAGE_SIZE, SB_HEADS, SB_HEAD_DIM), 1.0),
        "state_conformer": nrm(ks[4], (N_EVEN, DEC_BATCH, CONV_WIDTH - 1, CONV_DIM), 0.5),
        "state_shortconv": nrm(ks[5], (N_ODD, DEC_BATCH, SC_WIDTH - 1, SC_DIM), 1.0),
        "state_pool": nrm(ks[6], (N_ODD, DEC_BATCH, POOL_MAX - 1, POOL_DIM), 1.0),
        "page_table": page_table,
        "norm_mix_g": 1.0 + nrm(ks[8], (DEPTH, D_MODEL), 0.02),
        "norm_ffn_g": 1.0 + nrm(ks[9], (DEPTH, D_MODEL), 0.02),
        "norm_final_g": 1.0 + nrm(ks[10], (D_MODEL,), 0.02),
        "w_in_even": nrm(ks[11], (N_EVEN, D_MODEL, e_in), D_MODEL ** -0.5),
        "conv_a_w": nrm(ks[12], (N_EVEN, CONV_WIDTH, CONV_DIM), CONV_WIDTH ** -0.5),
        "conv_a_b": nrm(ks[13], (N_EVEN, CONV_DIM), 0.01),
        "ln_a_g": 1.0 + nrm(ks[14], (N_EVEN, CONV_DIM), 0.02),
        "ln_a_b": nrm(ks[15], (N_EVEN, CONV_DIM), 0.01),
        "sb_bias": SB_BIAS_INIT + nrm(ks[24], (N_EVEN, SB_HEADS), 0.5),
        "w_out_even": nrm(ks[16], (N_EVEN, CONV_DIM + SB_DIM, D_MODEL), (CONV_DIM + SB_DIM) ** -0.5),
        "w_in_odd": nrm(ks[17], (N_ODD, D_MODEL, o_in), D_MODEL ** -0.5),
        "conv_c_w": nrm(ks[18], (N_ODD, SC_WIDTH, SC_DIM), SC_WIDTH ** -0.5),
        "pool_w": nrm(ks[19], (N_ODD, len(POOL_WINDOWS), POOL_GROUP, POOL_GROUP), POOL_GROUP ** -0.5),
        "pool_scale": 1.0 + nrm(ks[20], (N_ODD, POOL_DIM), 0.1),
        "w_out_odd": nrm(ks[21], (N_ODD, SC_DIM + POOL_DIM, D_MODEL), (SC_DIM + POOL_DIM) ** -0.5),
        "w_ffn_in": nrm(ks[22], (DEPTH, D_MODEL, 2 * FFN_DIM), D_MODEL ** -0.5),
        "w_ffn_out": nrm(ks[23], (DEPTH, FFN_DIM, D_MODEL), FFN_DIM ** -0.5),
    }


def reference(x_prompt, x_sample, cache_k, cache_v, state_conformer, state_shortconv, state_pool, page_table,
              norm_mix_g, norm_ffn_g, norm_final_g, w_in_even, conv_a_w, conv_a_b, ln_a_g, ln_a_b, sb_bias,
              w_out_even, w_in_odd, conv_c_w, pool_w, pool_scale, w_out_odd, w_ffn_in, w_ffn_out):

    def trunk(x, start, conv_prev, k_past, v_past, sc_prev, pool_prev):
        h = x
        new_conv, new_k, new_v, new_sc, new_pool = [], [], [], [], []
        for layer in range(DEPTH):
            i = layer // 2
            hn = rms_norm(h, norm_mix_g[layer])
            if layer % 2 == 0:
                mix, c_new, k_new, v_new = conformer_sb_mixer(
                    hn, conv_prev(i), k_past(i), v_past(i), start, w_in_even[i], conv_a_w[i], conv_a_b[i],
                    ln_a_g[i], ln_a_b[i], sb_bias[i], w_out_even[i])
                new_conv.append(c_new)
                new_k.append(k_new)
                new_v.append(v_new)
            else:
                mix, s_new, p_new = shortconv_pool_mixer(
                    hn, sc_prev(i), pool_prev(i), start, w_in_odd[i], conv_c_w[i], pool_w[i], pool_scale[i],
                    w_out_odd[i])
                new_sc.append(s_new)
                new_pool.append(p_new)
            h = h + mix
            h = h + swiglu(rms_norm(h, norm_ffn_g[layer]), w_ffn_in[layer], w_ffn_out[layer])
        y = rms_norm(h, norm_final_g)
        return (y, jnp.stack(new_k), jnp.stack(new_v), jnp.stack(new_conv), jnp.stack(new_sc), jnp.stack(new_pool))

    bp, dt = x_prompt.shape[0], x_prompt.dtype
    y_p, k_p, v_p, conv_p, sc_p, pool_p = trunk(
        x_prompt, 0,
        lambda i: jnp.zeros((bp, CONV_WIDTH - 1, CONV_DIM), dt),
        lambda i: jnp.zeros((bp, 0, SB_HEADS, SB_HEAD_DIM), dt),
        lambda i: jnp.zeros((bp, 0, SB_HEADS, SB_HEAD_DIM), dt),
        lambda i: jnp.zeros((bp, SC_WIDTH - 1, SC_DIM), dt),
        lambda i: jnp.zeros((bp, POOL_MAX - 1, POOL_DIM), dt))

    bs = x_sample.shape[0]
    past_len = page_table.shape[1] * cache_k.shape[2]
    gather = lambda cache, i: cache[i][page_table].reshape(bs, past_len, SB_HEADS, SB_HEAD_DIM)
    y_s, k_s, v_s, conv_s, sc_s, pool_s = trunk(
        x_sample, past_len,
        lambda i: state_conformer[i],
        lambda i: gather(cache_k, i),
        lambda i: gather(cache_v, i),
        lambda i: state_shortconv[i],
        lambda i: state_pool[i])

    return (y_p, y_s, k_p, v_p, k_s, v_s, conv_p, conv_s, sc_p, sc_s, pool_p, pool_s)
```

```python
import numpy as np
from contextlib import ExitStack
import concourse.bass as bass
import concourse.mybir as mybir
from concourse.bass_utils import run_bass_kernel_spmd

F32, BF16, I32 = mybir.dt.float32, mybir.dt.bfloat16, mybir.dt.int32
AF = mybir.ActivationFunctionType
ALU = mybir.AluOpType
ESZ = {F32: 4, BF16: 2, I32: 4}

NCORES = 8
D = 1024
SEQ = 2048
NSEQ = 16
DSEQ = 8
NPAGE = 16
FFN = 2816
NEG = -240000.0
EPS = 1e-6


class V:
    __slots__ = ("ap", "key", "lo", "hi")

    def __init__(self, ap):
        self.ap = ap
        dims = ap.ap
        pstep = dims[0][0]
        off = ap.offset % pstep if pstep > 0 else ap.offset
        ext = 1
        for st, cnt in dims[1:]:
            ext += (cnt - 1) * abs(st)
        es = ESZ[ap.dtype]
        self.key = ap.tensor.name
        self.lo = off * es
        self.hi = (off + ext) * es
        if str(ap.space) == "PSUM":
            self.lo, self.hi = 0, 2048


class DSem:
    def __init__(self, sem, group=False):
        self.sem = sem
        self.count = 0
        self.group = group


class Op:
    __slots__ = ("eng", "fn", "deps", "sig", "cnt", "isdma", "dsem", "dval", "idx")


class Prog:
    ENGS = ("pe", "act", "dve", "pool", "sp")

    def __init__(self, nc, es):
        self.nc = nc
        self.es = es
        self.q = {e: [] for e in self.ENGS}
        self.recs = {}
        self.esem = {e: es.enter_context(nc.semaphore("sem_" + e)) for e in ("pe", "act", "dve", "pool")}
        self.nsem = 0
        self.out_dsems = []

    def dsem(self, group=False, out=False):
        self.nsem += 1
        d = DSem(self.es.enter_context(self.nc.semaphore("dsem%d" % self.nsem)), group)
        self.out_dsems.append(d)
        return d

    def _access(self, v, op, is_write):
        L = self.recs.get(v.key)
        if L is None:
            L = self.recs[v.key] = []
        raw, other = [], []
        newL = []
        lo, hi = v.lo, v.hi
        cov = []
        for rec in L:
            rlo, rhi, w, rd = rec
            if rhi <= lo or rlo >= hi:
                newL.append(rec)
                continue
            if w is not None:
                (other if is_write else raw).append(w)
            if is_write:
                other.extend(rd.values())
            if rlo < lo:
                newL.append([rlo, lo, w, dict(rd)])
            if rhi > hi:
                newL.append([hi, rhi, w, dict(rd)])
            if not is_write:
                nrd = dict(rd)
                nrd[("d", id(op)) if op.isdma else op.eng] = op
                a, b = max(rlo, lo), min(rhi, hi)
                newL.append([a, b, w, nrd])
                cov.append((a, b))
        if is_write:
            newL.append([lo, hi, op, {}])
        else:
            cov.sort()
            cur = lo
            k = ("d", id(op)) if op.isdma else op.eng
            for a, b in cov:
                if a > cur:
                    newL.append([cur, a, None, {k: op}])
                cur = max(cur, b)
            if cur < hi:
                newL.append([cur, hi, None, {k: op}])
        self.recs[v.key] = newL
        return raw, other

    def add(self, eng, fn, reads=(), writes=(), dsem=None):
        op = Op()
        op.eng = eng
        op.fn = fn
        op.isdma = dsem is not None
        op.sig = False
        op.cnt = 0
        op.idx = len(self.q[eng])
        raw, other = [], []
        for v in reads:
            r, o = self._access(v, op, v.key.startswith("ps"))
            raw += r
            other += o
        for v in writes:
            r, o = self._access(v, op, True)
            raw += r
            other += o
        deps = {}
        for lst, is_raw in ((raw, True), (other, False)):
            for d in lst:
                if d is op:
                    continue
                if d.isdma:
                    deps[("dsem", id(d.dsem))] = (d, d.dsem.count)
                    continue
                if (not op.isdma) and d.eng == eng:
                    if eng == "pe":
                        continue
                k = d.eng
                if k not in deps or deps[k].idx < d.idx:
                    deps[k] = d
        op.deps = [x if isinstance(x, tuple) else (x, None) for x in deps.values()]
        for d, _ in op.deps:
            if not d.isdma:
                d.sig = True
        if op.isdma:
            dsem.count += 16
            op.dsem = dsem
            op.dval = dsem.count
        self.q[eng].append(op)
        return op

    def emit(self):
        nc = self.nc
        for eng, L in self.q.items():
            c = 0
            for op in L:
                if (not op.isdma) and op.sig:
                    c += 1
                    op.cnt = c
        engobj = {"pe": "tensor", "act": "scalar", "dve": "vector", "pool": "gpsimd", "sp": "sync"}
        with nc.Block() as block:
            for eng in self.ENGS:
                def body(e, eng=eng):
                    waited = {}
                    for op in self.q[eng]:
                        for d, dv in op.deps:
                            if d.isdma:
                                sem = d.dsem.sem
                                val = d.dsem.count if d.dsem.group else dv
                            else:
                                sem = self.esem[d.eng]
                                val = d.cnt
                            if waited.get(sem.num, 0) < val:
                                e.wait_ge(sem, val)
                                waited[sem.num] = val
                        ins = op.fn(e)
                        if op.isdma:
                            ins.then_inc(op.dsem.sem, 16)
                        elif op.sig:
                            ins.then_inc(self.esem[eng], 1)
                    if eng == "sp":
                        for d in self.out_dsems:
                            if d.count > 0:
                                e.wait_ge(d.sem, d.count)
                getattr(block, engobj[eng])(body)


class _Stop(Exception):
    pass


def build_nc(npool, limit=None):
    nc = bass.Bass("TRN2", target_bir_lowering=False)
    es = ExitStack()
    P = Prog(nc, es)

    def din(name, shape, dt=F32):
        return nc.dram_tensor(name, list(shape), dt, kind="ExternalInput").ap()

    def dout(name, shape, dt=F32):
        return nc.dram_tensor(name, list(shape), dt, kind="ExternalOutput").ap()

    xp = din("xp", [SEQ, D])
    xs = din("xs", [128, D])
    ck = din("ck", [npool * 128, 512])
    cv = din("cv", [npool * 128, 512])
    st_conf = din("st_conf", [NSEQ * 30, 512])
    st_sc = din("st_sc", [NSEQ * 2, 512])
    st_pool = din("st_pool", [NSEQ * 15, 512])
    pt = din("pt", [NSEQ, NPAGE], I32)
    gvec = din("gvec", [5, D])
    chvd = din("chv", [38, 512])
    sbb = din("sbb", [1, 8])
    poolwd = din("poolw", [512, 128])
    w_in_e = din("w_in_e", [D, 2560])
    w_out_e = din("w_out_e", [D, D])
    w_in_o = din("w_in_o", [D, 2048])
    w_out_o = din("w_out_o", [D, D])
    w_ffn_in = din("w_ffn_in", [2 * D, 2 * FFN])
    w_ffn_out = din("w_ffn_out", [2 * FFN, D])

    yp = dout("yp", [SEQ, D])
    ys = dout("ys", [128, D])
    kp = dout("kp", [SEQ, 512])
    vp = dout("vp", [SEQ, 512])
    ks = dout("ks", [128, 512])
    vs = dout("vs", [128, 512])
    conv_p = dout("conv_p", [30, 512])
    conv_s = dout("conv_s", [NSEQ, 30, 512])
    sc_p = dout("sc_p", [2, 512])
    sc_s = dout("sc_s", [NSEQ, 2, 512])
    pool_p = dout("pool_p", [15, 512])
    pool_s = dout("pool_s", [NSEQ, 15, 512])

    def sb(name, shape, dt):
        return es.enter_context(nc.sbuf_tensor(name, list(shape), dt))

    def ps(name):
        return es.enter_context(nc.psum_tensor(name, [128, 512], F32))

    TMAX = 1152
    h = sb("h", [128, 9, D], F32)
    actA = sb("actA", [128, 8, TMAX], BF16)
    KT = sb("KT", [128, 4, SEQ + 128], BF16)
    Vall = sb("Vall", [128, 16, 512], BF16)
    wslot = sb("wslot", [128, 2, 6144], BF16)
    gb = sb("gb", [128, 2, D], F32)
    stage = sb("stage", [128, 2, D], F32)
    xn = sb("xn", [128, 2, D], BF16)
    identb = sb("identb", [128, 128], BF16)
    identf = sb("identf", [128, 128], F32)
    tinc = sb("tinc", [128, 128], BF16)
    onesb = sb("onesb", [128, 128], BF16)
    negm = sb("negm", [128, 128], BF16)
    onesf = sb("onesf", [128, 128], F32)
    chv = sb("chvs", [128, 4, 38], F32)
    chraw = sb("chraw", [38, 512], F32)
    sbias = sb("sbias", [128, 8], F32)
    biasd = sb("biasd", [128, 64], F32)
    epst = sb("epst", [128, 1], F32)
    poolw = sb("poolws", [128, 4, 128], BF16)
    rc = sb("rc", [128, 4, 16], F32)
    stats = sb("stats", [128, 64], F32)
    ptb = sb("ptb", [128, NSEQ * NPAGE], I32)
    pidx = sb("pidx", [128, NSEQ * NPAGE], I32)
    iot = sb("iot", [128, 1], F32)
    hist_a = sb("hist_a", [128, 4, 30], BF16)
    hist_c = sb("hist_c", [128, 4, 2], BF16)
    hist_u = sb("hist_u", [128, 4, 15], BF16)
    MIXB = 58 * 1024
    mix = sb("mix", [128, MIXB // 2], BF16)
    banks = [ps("ps%d" % i) for i in range(8)]

    maskN = sb("maskN", [128, 128], BF16)
    ebl = sb("ebl", [16, 128], F32)
    eblb = sb("eblb", [16, 128], BF16)

    def mixv(off_bytes, nelem, dt):
        assert off_bytes % 4 == 0
        assert off_bytes + nelem * ESZ[dt] <= MIXB, (off_bytes, nelem, MIXB)
        a = mix[:, off_bytes // 2: off_bytes // 2 + nelem * ESZ[dt] // 2]
        return a if dt == BF16 else a.bitcast(dt)

    def rv(x):
        return x if isinstance(x, V) else V(x)

    def act(out, in_, func, bias=0.0, scale=1.0, accum=None):
        out, in_ = rv(out), rv(in_)
        rd = [in_]
        wr = [out]
        kw = {}
        if hasattr(bias, "ap"):
            bias = rv(bias)
            rd.append(bias)
            kw["bias"] = bias.ap
        else:
            kw["bias"] = float(bias)
        if hasattr(scale, "ap"):
            scale = rv(scale)
            rd.append(scale)
            kw["scale"] = scale.ap
        else:
            kw["scale"] = float(scale)
        if accum is not None:
            accum = rv(accum)
            wr.append(accum)
            kw["accum_out"] = accum.ap
        return P.add("act", lambda e: e.activation(out=out.ap, in_=in_.ap, func=func, **kw), rd, wr)

    def tt(eng, out, in0, in1, op):
        out, in0, in1 = rv(out), rv(in0), rv(in1)
        return P.add(eng, lambda e: e.tensor_tensor(out=out.ap, in0=in0.ap, in1=in1.ap, op=op), [in0, in1], [out])

    def ts(eng, out, in0, s1, op0, s2=None, op1=None):
        out, in0 = rv(out), rv(in0)
        rd = [in0]
        a1 = s1
        if hasattr(s1, "ap"):
            s1 = rv(s1)
            rd.append(s1)
            a1 = s1.ap
        a2 = s2
        if s2 is not None and hasattr(s2, "ap"):
            s2 = rv(s2)
            rd.append(s2)
            a2 = s2.ap
        if op1 is None:
            return P.add(eng, lambda e: e.tensor_scalar(out=out.ap, in0=in0.ap, scalar1=a1, scalar2=None, op0=op0), rd, [out])
        return P.add(eng, lambda e: e.tensor_scalar(out=out.ap, in0=in0.ap, scalar1=a1, scalar2=a2, op0=op0, op1=op1), rd, [out])

    def stt(eng, out, in0, s, in1, op0, op1):
        out, in0, in1 = rv(out), rv(in0), rv(in1)
        rd = [in0, in1]
        a = s
        if hasattr(s, "ap"):
            s = rv(s)
            rd.append(s)
            a = s.ap
        return P.add(eng, lambda e: e.scalar_tensor_tensor(out=out.ap, in0=in0.ap, scalar=a, in1=in1.ap, op0=op0, op1=op1), rd, [out])

    def cp(eng, out, in_):
        out, in_ = rv(out), rv(in_)
        if eng == "act":
            return P.add("act", lambda e: e.copy(out=out.ap, in_=in_.ap), [in_], [out])
        return P.add(eng, lambda e: e.tensor_copy(out=out.ap, in_=in_.ap), [in_], [out])

    def mset(eng, out, val):
        out = rv(out)
        return P.add(eng, lambda e: e.memset(out.ap, val), [], [out])

    def asel(out, in_, pattern, op, base, cm):
        out, in_ = rv(out), rv(in_)
        return P.add("pool", lambda e: e.affine_select(out=out.ap, in_=in_.ap, pattern=pattern, compare_op=op, fill=0.0,
                                                       base=base, channel_multiplier=cm), [in_], [out])

    def mm(out, lhsT, rhs, start, stop, tp=None, sgc=False):
        out, lhsT, rhs = rv(out), rv(lhsT), rv(rhs)
        kw = {}
        if tp is not None:
            kw["tile_position"] = tp
        if sgc:
            kw["skip_group_check"] = True
        rd = [lhsT, rhs] + ([] if start else [out])
        return P.add("pe", lambda e: e.matmul(out.ap, lhsT=lhsT.ap, rhs=rhs.ap, start=start, stop=stop, **kw), rd, [out])

    def tr(out, in_, ident):
        out, in_, ident = rv(out), rv(in_), rv(ident)
        return P.add("pe", lambda e: e.transpose(out.ap, in_.ap, ident.ap), [in_, ident], [out])

    def dma(q, out, in_, dsem, rd=(), wr=()):
        return P.add(q, lambda e: e.dma_start(out=out, in_=in_), [rv(x) for x in rd], [rv(x) for x in wr], dsem=dsem)

    ckstate = {"n": 0}

    def chk(name):
        ckstate["n"] += 1
        if limit is not None and ckstate["n"] > limit:
            raise _Stop(name)

    cdsem = P.dsem(group=True)

    def cload(out_ap, in_ap):
        dma("sp", out_ap, in_ap, cdsem, wr=[out_ap])

    def lload(out_ap, in_ap):
        dma("sp", out_ap, in_ap, P.dsem(), wr=[out_ap])

    cload(chraw[0:38, :], chvd)
    cload(sbias[:], sbb.rearrange("a b -> (a b)").partition_broadcast(128))
    cload(ptb[:], pt.rearrange("a b -> (a b)").partition_broadcast(128))
    pw32 = mixv(0, 512, F32).rearrange("p (g d) -> p g d", g=4)
    cload(pw32, poolwd.rearrange("(g c) d -> c g d", c=128))
    cp("dve", poolw[:], pw32)

    mset("pool", onesf[:], 1.0)
    mset("pool", epst[:], EPS)
    asel(identf[:], onesf[:], [[-1, 128]], ALU.is_equal, 0, 1)
    tincf = mixv(4096, 128, F32)
    asel(tincf, onesf[:], [[-1, 128]], ALU.is_ge, 0, 1)
    cp("pool", identb[:], identf[:])
    cp("pool", tinc[:], tincf)
    cp("pool", onesb[:], onesf[:])
    ts("pool", negm[:], tincf, NEG, ALU.mult)
    P.add("pool", lambda e: e.iota(iot[:], [[0, 1]], base=0, channel_multiplier=1, allow_small_or_imprecise_dtypes=True),
          [], [V(iot[:])])
    ts("dve", pidx[:], ptb[:], 128.0, ALU.mult, iot[:, 0:1], ALU.add)
    cp("dve", biasd[:].rearrange("p (h q) -> p h q", q=8), sbias[:].unsqueeze(2).broadcast_to([128, 8, 8]))
    for c in range(4):
        tr(banks[0][0:128, c * 64: c * 64 + 38], chraw[0:38, c * 128:(c + 1) * 128], identf[0:38, 0:38])
    cp("dve", chv[:], banks[0][:, 0:256].rearrange("p (c r) -> p c r", c=4)[:, :, 0:38])
    rcf = mixv(8192, 16, F32)
    P.add("pool", lambda e: e.iota(rcf, [[1, 16]], base=1, channel_multiplier=0, allow_small_or_imprecise_dtypes=True),
          [], [V(rcf)])
    for g, w in enumerate((2, 4, 8, 16)):
        ts("dve", rc[:, g, :], rcf, float(w), ALU.min)
    P.add("dve", lambda e: e.reciprocal(out=rc[:], in_=rc[:]), [V(rc[:])], [V(rc[:])])
    mset("pool", ebl[:], 1.0)
    asel(ebl[:], ebl[:], [[1, 128]], ALU.is_ge, 0, -8)
    asel(ebl[:], ebl[:], [[-1, 128]], ALU.is_ge, 7, 8)
    cp("pool", eblb[:], ebl[:])
    mm(banks[1][:, 0:128], eblb[0:16, :], eblb[0:16, :], True, True)
    blk = mixv(12288, 128, F32)
    cp("dve", blk, banks[1][:, 0:128])
    asel(blk, blk, [[1, 128]], ALU.is_gt, 0, -1)
    ts("dve", maskN[:], blk, -NEG, ALU.mult, NEG, ALU.add)

    wsems = [[P.dsem(), P.dsem()], [P.dsem(), P.dsem()]]
    WSPECS = []

    def wspec_layer0():
        for b in range(2):
            WSPECS.append([(w_in_e[:, b * 256:(b + 1) * 256], 8, 0, 256, 512),
                           (w_in_e[:, 512 + b * 256:512 + (b + 1) * 256], 8, 256, 256, 512)])
        for q0 in (1024, 1536, 2048):
            WSPECS.append([(w_in_e[:, q0:q0 + 512], 8, 0, 512, 512)])

    def wspec_out(w):
        for ch in range(2):
            WSPECS.append([(w[:, ch * 512:(ch + 1) * 512], 8, 0, 512, 512)])

    def wspec_ffn(layer):
        wi = w_ffn_in[layer * D:(layer + 1) * D, :]
        wo = w_ffn_out[layer * FFN:(layer + 1) * FFN, :]
        for (b0, b1) in ((0, 6), (6, 11)):
            for b in range(b0, b1):
                WSPECS.append([(wi[:, b * 256:(b + 1) * 256], 8, 0, 256, 512),
                               (wi[:, FFN + b * 256:FFN + (b + 1) * 256], 8, 256, 256, 512)])
            nch = 2 * (b1 - b0)
            for ch in range(2):
                WSPECS.append([(wo[b0 * 256:b1 * 256, ch * 512:(ch + 1) * 512], nch, 0, 512, 512)])

    def wspec_layer1():
        for b in range(2):
            WSPECS.append([(w_in_o[:, 512 + b * 256:512 + (b + 1) * 256], 8, 0, 256, 512),
                           (w_in_o[:, 1024 + b * 256:1024 + (b + 1) * 256], 8, 256, 256, 512)])
        for b in range(2):
            WSPECS.append([(w_in_o[:, b * 256:(b + 1) * 256], 8, 0, 256, 512),
                           (w_in_o[:, 1536 + b * 256:1536 + (b + 1) * 256], 8, 256, 256, 512)])

    for _g in range(2):
        wspec_layer0()
        wspec_out(w_out_e)
        wspec_ffn(0)
        wspec_layer1()
        wspec_out(w_out_o)
        wspec_ffn(1)
    wstate = {"cur": 0, "issued": 0}

    def wissue(i):
        s = i % 2
        for pi, (src, nk, c0, ncols, stride) in enumerate(WSPECS[i]):
            dst = wslot[:, s, 0:nk * stride].rearrange("p (k n) -> p k n", k=nk)[:, :, c0:c0 + ncols]
            srcv = src.rearrange("(k p) n -> p k n", p=128)
            P.add("pool", lambda e, dst=dst, srcv=srcv: e.dma_start(out=dst, in_=srcv), [], [V(dst)], dsem=wsems[s][pi])

    def wget(nk, stride=512):
        i = wstate["cur"]
        wstate["cur"] += 1
        while wstate["issued"] <= min(i + 1, len(WSPECS) - 1):
            wissue(wstate["issued"])
            wstate["issued"] += 1
        assert WSPECS[i][0][1] == nk, (i, nk, WSPECS[i][0][1])
        return wslot[:, i % 2, 0:nk * stride].rearrange("p (k n) -> p k n", k=nk)

    groups = [
        dict(ptiles=list(range(0, 9)), sample=False),
        dict(ptiles=list(range(9, 16)), sample=True),
    ]
    xsems = [P.dsem() for _ in range(9)]
    gsem = [P.dsem(), P.dsem()]
    osem = [P.dsem(out=True), P.dsem(out=True)]
    ostate = {"n": 0}

    def stage_slot():
        i = ostate["n"] % 2
        ostate["n"] += 1
        return i

    gstate = {"n": 0}

    def load_g(row):
        i = gstate["n"] % 2
        gstate["n"] += 1
        dma("sp", gb[:, i, :], gvec[row].partition_broadcast(128), gsem[i], wr=[gb[:, i, :]])
        return i

    statc = {"n": 0}

    def stat_col(n=3):
        c = statc["n"]
        if c + n > 60:
            c = 0
        statc["n"] = c + n
        return c

    def col_tiles(lo, hi, step=512):
        r = []
        c = lo
        while c < hi:
            r.append((c, min(step, hi - c)))
            c += step
        return r

    bank_rr = {"n": 0}

    def nbank(lo=2, hi=8):
        b = lo + bank_rr["n"] % (hi - lo)
        bank_rr["n"] += 1
        return banks[b]

    def run_groups():
        for gi, G in enumerate(groups):
            ptiles = G["ptiles"]
            npt = len(ptiles)
            ntl = npt + (1 if G["sample"] else 0)
            TP = npt * 128
            T = ntl * 128
            first_tok = ptiles[0] * 128
            hnT = actA
            cat = actA

            for lt, gt in enumerate(ptiles):
                dma("sp", h[:, lt, :], xp[gt * 128:(gt + 1) * 128, :], xsems[lt], wr=[h[:, lt, :]])
            if G["sample"]:
                dma("sp", h[:, npt, :], xs, xsems[npt], wr=[h[:, npt, :]])

            def norm_T(grow):
                gslot = load_g(grow)
                for lt in range(ntl):
                    hv = h[:, lt, :]
                    xs_ = lt % 2
                    c0 = stat_col(3)
                    act(xn[:, xs_, :], hv, AF.Square, accum=stats[:, c0:c0 + 1])
                    act(stats[:, c0 + 1:c0 + 2], stats[:, c0:c0 + 1], AF.Ln, bias=epst[:, 0:1], scale=1.0 / D)
                    act(stats[:, c0 + 2:c0 + 3], stats[:, c0 + 1:c0 + 2], AF.Exp, scale=-0.5)
                    stt("dve", xn[:, xs_, :], hv, stats[:, c0 + 2:c0 + 3], gb[:, gslot, :], ALU.mult, ALU.mult)
                    pb = banks[lt % 2]
                    pbv = pb[:].bitcast(BF16)
                    for kc in range(8):
                        tr(pbv[:, kc * 128:(kc + 1) * 128], xn[:, xs_, kc * 128:(kc + 1) * 128], identb[:])
                    cp("dve" if lt % 2 == 0 else "act", hnT[:, :, lt * 128:(lt + 1) * 128],
                       pbv.rearrange("p (k n) -> p k n", k=8))

            def fm_mm(psv, wv, wc0, c0, n):
                for kc in range(8):
                    mm(psv, wv[:, kc, wc0:wc0 + 128], hnT[:, kc, c0:c0 + n], kc == 0, kc == 7)

            def tm_mm(psv, wv, wc0, ncols, lt):
                for kc in range(8):
                    mm(psv, hnT[:, kc, lt * 128:(lt + 1) * 128], wv[:, kc, wc0:wc0 + ncols], kc == 0, kc == 7)

            def residual_proj(nk_total, src):
                for ch in range(2):
                    wv = wget(nk_total)
                    for lt in range(ntl):
                        pb = nbank()
                        for kc in range(nk_total):
                            mm(pb[:], src[:, kc, lt * 128:(lt + 1) * 128], wv[:, kc, :], kc == 0, kc == nk_total - 1)
                        tt("dve", h[:, lt, ch * 512:(ch + 1) * 512], h[:, lt, ch * 512:(ch + 1) * 512], pb[:], ALU.add)

            def ffn(layer):
                hid = mixv(0, 12 * T, BF16).rearrange("p (c t) -> p c t", c=12)
                sg = mixv(12 * T * 2, 2 * 512, BF16).rearrange("p (a n) -> p a n", a=2)
                norm_T(1 + 2 * layer)
                for half, (b0, b1) in enumerate(((0, 6), (6, 11))):
                    nch = 2 * (b1 - b0)
                    for b in range(b0, b1):
                        wv = wget(8)
                        for fc in range(2):
                            lc = 2 * (b - b0) + fc
                            for (c0, n) in col_tiles(0, T):
                                pg = nbank()
                                pu = nbank()
                                fm_mm(pg[:, 0:n], wv, fc * 128, c0, n)
                                fm_mm(pu[:, 0:n], wv, 256 + fc * 128, c0, n)
                                k = bank_rr["n"] % 2
                                act(sg[:, k, 0:n], pg[:, 0:n], AF.Silu)
                                tt("dve", hid[:, lc, c0:c0 + n], pu[:, 0:n], sg[:, k, 0:n], ALU.mult)
                    residual_proj(nch, hid)

            norm_T(0)
            o = 0
            QT = mixv(o, 4 * T, BF16).rearrange("p (c t) -> p c t", c=4); o += 4 * T * 2
            sgt = mixv(o, 2 * 512, F32).rearrange("p (a n) -> p a n", a=2)
            vs_tok = mixv(o, 512, BF16)
            o += 4096
            R1 = o
            extp = mixv(o, 4 * (30 + TP), BF16).rearrange("p (c t) -> p c t", c=4); o += 4 * (30 + TP) * 2
            if G["sample"]:
                exts = mixv(o, 4 * 16 * 38, BF16).rearrange("p (c s t) -> p c s t", c=4, s=16); o += 4 * 16 * 38 * 2
            diag = mixv(o, 31 * 128, BF16).rearrange("p (j n) -> p j n", j=31); o += 31 * 128 * 2
            a16o = o
            a16 = mixv(o, 4 * T, BF16).rearrange("p (c t) -> p c t", c=4); o += 4 * T * 2
            a2 = mixv(o, 4 * T, BF16).rearrange("p (c t) -> p c t", c=4); o += 4 * T * 2
            lnt = mixv(o, 3 * 512, F32).rearrange("p (a n) -> p a n", a=3); o += 3 * 2048
            assert o <= MIXB, o

            if gi == 0:
                mset("pool", extp[:, :, 0:30], 0.0)
            else:
                cp("pool", extp[:, :, 0:30], hist_a[:])
                stt_ = mixv(a16o, 4 * 512, F32).rearrange("p (a n) -> p a n", a=4)
                for i in range(4):
                    lload(stt_[0:120, i, :], st_conf[i * 120:(i + 1) * 120, :])
                for i in range(4):
                    pb = nbank()
                    for c in range(4):
                        tr(pb[:, c * 128:c * 128 + 120], stt_[0:120, i, c * 128:(c + 1) * 128], identf[0:120, 0:120])
                    cp("dve", exts[:, :, 4 * i:4 * i + 4, 0:30],
                       pb[:].rearrange("p (c x) -> p c x", c=4)[:, :, 0:120].rearrange("p c (s r) -> p c s r", s=4))
                P.add("sp", lambda e: e.dma_start(out=conv_s[:, 0:22, :], in_=st_conf.rearrange("(s r) f -> s r f", r=30)[:, 8:30, :]),
                      [], [], dsem=P.dsem(out=True))

            chk('L0 norm gi=%d' % gi)
            for b in range(2):
                wv = wget(8)
                for fc in range(2):
                    c = 2 * b + fc
                    for (c0, n) in col_tiles(0, T):
                        pa = nbank()
                        pg = nbank()
                        fm_mm(pa[:, 0:n], wv, fc * 128, c0, n)
                        fm_mm(pg[:, 0:n], wv, 256 + fc * 128, c0, n)
                        k = bank_rr["n"] % 2
                        act(sgt[:, k, 0:n], pg[:, 0:n], AF.Sigmoid)
                        npp = max(0, min(n, TP - c0))
                        if npp > 0:
                            tt("dve", extp[:, c, 30 + c0:30 + c0 + npp], pa[:, 0:npp], sgt[:, k, 0:npp], ALU.mult)
                        if npp < n:
                            assert n - npp == 128
                            tt("dve", exts[:, c, :, 30:38], pa[:, npp:n].rearrange("p (s t) -> p s t", t=8),
                               sgt[:, k, npp:n].rearrange("p (s t) -> p s t", t=8), ALU.mult)
                if gi == 1:
                    for lt, dst in ((npt - 1, "p"), (npt, "s")):
                        pa = nbank()
                        tm_mm(pa[:], wv, 0, 512, lt)
                        si = stage_slot()
                        act(stage[:, si, 0:256], pa[:, 256:512], AF.Sigmoid)
                        tt("dve", stage[:, si, 256:512], pa[:, 0:256], stage[:, si, 0:256], ALU.mult)
                        if dst == "p":
                            dma("sp", conv_p[:, b * 256:(b + 1) * 256], stage[98:128, si, 256:512], osem[si], rd=[stage[:, si, 256:512]])
                        else:
                            for s_ in range(NSEQ):
                                dma("sp", conv_s[s_, 22:30, b * 256:(b + 1) * 256], stage[8 * s_:8 * s_ + 8, si, 256:512], osem[si],
                                    rd=[stage[:, si, 256:512]])
            chk('L0 glu gi=%d' % gi)
            wv = wget(8)
            for c in range(4):
                for (c0, n) in col_tiles(0, T):
                    pb = nbank()
                    fm_mm(pb[:, 0:n], wv, c * 128, c0, n)
                    cp("act", QT[:, c, c0:c0 + n], pb[:, 0:n])
            wv = wget(8)
            for c in range(4):
                for (c0, n) in col_tiles(0, T):
                    pb = nbank()
                    fm_mm(pb[:, 0:n], wv, c * 128, c0, n)
                    npp = max(0, min(n, TP - c0))
                    if npp > 0:
                        cp("act", KT[:, c, first_tok + c0:first_tok + c0 + npp], pb[:, 0:npp])
                    if npp < n:
                        cp("act", KT[:, c, SEQ:SEQ + 128], pb[:, npp:n])
            for lt in range(ntl):
                pb = nbank()
                tm_mm(pb[:], wv, 0, 512, lt)
                si = stage_slot()
                cp("dve", stage[:, si, 0:512], pb[:])
                dst = kp[ptiles[lt] * 128:(ptiles[lt] + 1) * 128, :] if lt < npt else ks
                dma("sp", dst, stage[:, si, 0:512], osem[si], rd=[stage[:, si, 0:512]])
            wv = wget(8)
            for lt in range(ntl):
                pb = nbank()
                tm_mm(pb[:], wv, 0, 512, lt)
                si = stage_slot()
                cp("dve", stage[:, si, 0:512], pb[:])
                if lt < npt:
                    cp("act", Vall[:, ptiles[lt], :], pb[:])
                    dst = vp[ptiles[lt] * 128:(ptiles[lt] + 1) * 128, :]
                else:
                    cp("act", vs_tok, pb[:])
                    dst = vs
                dma("sp", dst, stage[:, si, 0:512], osem[si], rd=[stage[:, si, 0:512]])

            chk('L0 qkv gi=%d' % gi)
            for c in range(4):
                for j in range(31):
                    ts("pool", diag[:, j, :], identb[:], chv[:, c, j:j + 1], ALU.mult)
                for (c0, n) in col_tiles(0, TP):
                    pb = nbank()
                    for j in range(31):
                        mm(pb[:, 0:n], diag[:, j, :], extp[:, c, c0 + j:c0 + j + n], j == 0, j == 30)
                    act(a16[:, c, c0:c0 + n], pb[:, 0:n], AF.Identity, bias=chv[:, c, 31:32])
                    act(a2[:, c, c0:c0 + n], pb[:, 0:n], AF.Square, bias=chv[:, c, 31:32])
                if G["sample"]:
                    pb = nbank()
                    for j in range(31):
                        mm(pb[:, 0:128].rearrange("p (s t) -> p s t", t=8), diag[:, j, :], exts[:, c, :, j:j + 8], j == 0, j == 30)
                    act(a16[:, c, TP:T], pb[:, 0:128], AF.Identity, bias=chv[:, c, 31:32])
                    act(a2[:, c, TP:T], pb[:, 0:128], AF.Square, bias=chv[:, c, 31:32])
            if gi == 0:
                cp("pool", hist_a[:], extp[:, :, TP:TP + 30])
            for (c0, n) in col_tiles(0, T):
                pm = nbank()
                pv_ = nbank()
                for c in range(4):
                    mm(pm[:, 0:n], onesb[:], a16[:, c, c0:c0 + n], c == 0, c == 3)
                for c in range(4):
                    mm(pv_[:, 0:n], onesb[:], a2[:, c, c0:c0 + n], c == 0, c == 3)
                mean = lnt[:, 0, 0:n]
                rstd = lnt[:, 1, 0:n]
                tmp = lnt[:, 2, 0:n]
                ts("dve", mean, pm[:, 0:n], 1.0 / 512, ALU.mult)
                tt("pool", tmp, mean, mean, ALU.mult)
                stt("dve", rstd, pv_[:, 0:n], 1.0 / 512, tmp, ALU.mult, ALU.subtract)
                act(rstd, rstd, AF.Ln, bias=epst[:, 0:1])
                act(rstd, rstd, AF.Exp, scale=-0.5)
                for c in range(4):
                    tt("dve", tmp, a16[:, c, c0:c0 + n], mean, ALU.subtract)
                    tt("pool", tmp, tmp, rstd, ALU.mult)
                    act(cat[:, c, c0:c0 + n], tmp, AF.Silu, bias=chv[:, c, 33:34], scale=chv[:, c, 32:33])

            chk('L0 conformer gi=%d' % gi)
            o = R1
            ebuf = mixv(o, 2 * 512, F32).rearrange("p (a n) -> p a n", a=2); o += 4096
            xbuf = mixv(o, 2 * 512, F32).rearrange("p (a n) -> p a n", a=2); o += 4096
            spb = mixv(o, 2 * 512, BF16).rearrange("p (a n) -> p a n", a=2); o += 2048
            wb = mixv(o, 2 * 512, BF16).rearrange("p (a n) -> p a n", a=2); o += 2048
            lacc = mixv(o, 2 * 512, BF16).rearrange("p (a n) -> p a n", a=2); o += 2048
            it = 0
            qgroups = [ptiles[i:i + 4] for i in range(0, npt, 4)]
            for qg in qgroups:
                gq0, gq1 = qg[0], qg[-1]
                NQ = len(qg) * 128
                ql0 = (gq0 - ptiles[0]) * 128
                for hp in range(4):
                    pB = banks[6 + (hp % 2)]
                    for hh in range(2):
                        hsl = slice(hh * 64, hh * 64 + 64)
                        hd = 2 * hp + hh
                        first = True
                        lk = 0
                        for j in range(gq1, -1, -1):
                            off = (max(gq0, j) - gq0) * 128
                            pS = banks[2 + it % 2]
                            pA = banks[4 + it % 2]
                            k = it % 2
                            it += 1
                            diagblk = j >= gq0
                            mm(pS[:, off:NQ], KT[hsl, hp, j * 128:(j + 1) * 128], QT[hsl, hp, ql0 + off:ql0 + NQ], True, not diagblk)
                            if diagblk:
                                mm(pS[:, off:off + 128], identb[:], negm[:], False, True)
                            act(ebuf[:, k, off:NQ], pS[:, off:NQ], AF.Exp, bias=sbias[:, hd:hd + 1], scale=0.125)
                            act(spb[:, k, off:NQ], ebuf[:, k, off:NQ], AF.Ln, bias=1.0)
                            mm(pA[:, off:NQ], tinc[:], spb[:, k, off:NQ], True, first)
                            if not first:
                                mm(pA[:, off:NQ], onesb[:], lacc[:, lk, off:NQ], False, True)
                            if j > 0:
                                if first:
                                    if off > 0:
                                        mset("pool", lacc[:, 1 - lk, 0:off], 0.0)
                                    cp("pool", lacc[:, 1 - lk, off:NQ], spb[:, k, off:NQ])
                                else:
                                    if off > 0:
                                        mset("pool", lacc[:, 1 - lk, 0:off], 0.0)
                                    tt("pool", lacc[:, 1 - lk, off:NQ], lacc[:, lk, off:NQ], spb[:, k, off:NQ], ALU.add)
                                lk = 1 - lk
                            act(xbuf[:, k, off:NQ], pA[:, off:NQ], AF.Exp, scale=-1.0)
                            tt("dve", wb[:, k, off:NQ], ebuf[:, k, off:NQ], xbuf[:, k, off:NQ], ALU.mult)
                            mm(pB[hsl, off:NQ], Vall[:, j, hd * 64:(hd + 1) * 64], wb[:, k, off:NQ], first, j == 0,
                               tp=(0, 64) if hh == 1 else None, sgc=True)
                            first = False
                    cp("dve", cat[:, 4 + hp, ql0:ql0 + NQ], pB[:, 0:NQ])

            chk('L0 attn gi=%d' % gi)
            if G["sample"]:
                sc0 = TP
                o = R1
                qblk = mixv(o, 4 * 16 * 16, BF16).rearrange("p (c s x) -> p c s x", c=4, s=16); o += 2048
                kring = mixv(o, 4 * 512, BF16).rearrange("p (a f) -> p a f", a=4); o += 4096
                ktp = mixv(o, 2 * 512, BF16).rearrange("p (a f) -> p a f", a=2); o += 2048
                vring = mixv(o, 8 * 512, BF16).rearrange("p (a f) -> p a f", a=8); o += 8192
                zb = mixv(o, 1024, F32); o += 4096
                xd = mixv(o, 1024, F32); o += 4096
                spd = mixv(o, 1024, BF16); o += 2048
                wd = mixv(o, 1024, BF16); o += 2048
                eN = mixv(o, 1024, F32); o += 4096
                spN = mixv(o, 1024, BF16); o += 2048
                wN = mixv(o, 1024, BF16); o += 2048
                assert o <= MIXB, o
                ksem = [P.dsem() for _ in range(4)]
                vsem = [P.dsem() for _ in range(8)]
                mset("pool", qblk[:], 0.0)
                for hp in range(4):
                    cp("pool", qblk[0:64, hp, :, 0:8], QT[0:64, hp, sc0:sc0 + 128].rearrange("p (s t) -> p s t", t=8))
                    cp("pool", qblk[64:128, hp, :, 8:16], QT[64:128, hp, sc0:sc0 + 128].rearrange("p (s t) -> p s t", t=8))
                pT = banks[0]
                pTv = pT[:].bitcast(BF16)
                pS2 = [banks[1], banks[2]]
                pA2 = [banks[3], banks[4]]
                pBd = banks[5]
                for hb in range(2):
                    for hd in range(4 * hb, 4 * hb + 4):
                        hp, hh = hd // 2, hd % 2
                        hsl = slice(hh * 64, hh * 64 + 64)
                        dstp = pS2[hb][:, (hd % 4) * 128:(hd % 4 + 1) * 128]
                        mm(dstp, KT[hsl, hp, SEQ:SEQ + 128], QT[hsl, hp, sc0:sc0 + 128], hd % 4 == 0, False, sgc=True)
                        mm(dstp, identb[:], maskN[:], False, True, sgc=True)
                    for hd in range(4 * hb, 4 * hb + 4):
                        dstp = pS2[hb][:, (hd % 4) * 128:(hd % 4 + 1) * 128]
                        act(eN[:, hd * 128:(hd + 1) * 128], dstp, AF.Exp, bias=sbias[:, hd:hd + 1], scale=0.125)
                act(spN, eN, AF.Ln, bias=1.0)
                for bq in range(2):
                    mm(pA2[bq][:], tinc[:], spN[:, bq * 512:(bq + 1) * 512], True, True)
                    act(xd[:, bq * 512:(bq + 1) * 512], pA2[bq][:], AF.Exp, scale=-1.0)
                tt("dve", wN, eN, xd, ALU.mult)
                spN3 = spN.rearrange("p (h c) -> p h c", h=8)
                wN3 = wN.rearrange("p (h c) -> p h c", h=8)
                gcount = 0
                vcount = 0
                for s_ in range(NSEQ):
                    for j in range(NPAGE):
                        ks_ = gcount % 4
                        col = s_ * NPAGE + j
                        kdst = kring[:, ks_, :]
                        P.add("pool", lambda e, kdst=kdst, col=col: e.indirect_dma_start(
                            out=kdst, out_offset=None, in_=ck,
                            in_offset=bass.IndirectOffsetOnAxis(ap=pidx[:, col:col + 1], axis=0)),
                            [V(pidx[:, col:col + 1])], [V(kdst)], dsem=ksem[ks_])
                        half = gcount % 2
                        for c in range(4):
                            tr(pTv[:, half * 512 + c * 128: half * 512 + (c + 1) * 128], kring[:, ks_, c * 128:(c + 1) * 128], identb[:])
                        cp("dve" if gcount % 2 == 0 else "act", ktp[:, half, :], pTv[:, half * 512:(half + 1) * 512])
                        for hp in range(4):
                            mm(pS2[j // 8][:, (j % 8) * 64 + hp * 16:(j % 8) * 64 + hp * 16 + 16],
                               ktp[:, half, hp * 128:(hp + 1) * 128], qblk[:, hp, s_, :], True, True)
                        gcount += 1
                    for bq in range(2):
                        stt("dve", zb[:, bq * 512:(bq + 1) * 512].rearrange("p (j c) -> p j c", c=64),
                            pS2[bq][:].rearrange("p (j c) -> p j c", c=64),
                            0.125, biasd[:].unsqueeze(1).broadcast_to([128, 8, 64]), ALU.mult, ALU.add)
                    act(zb, zb, AF.Exp)
                    act(spd, zb, AF.Ln, bias=1.0)
                    for bq in range(2):
                        mm(pA2[bq][:], tinc[:], spd[:, bq * 512:(bq + 1) * 512], True, False)
                    for jp in range(1, NPAGE):
                        src = spd[:, jp * 64:(jp + 1) * 64]
                        n0 = min(jp, 8)
                        mm(pA2[0][:, 0:n0 * 64].rearrange("p (j c) -> p j c", c=64), onesb[:],
                           src.unsqueeze(1).broadcast_to([128, n0, 64]), False, False)
                        if jp > 8:
                            n1 = jp - 8
                            mm(pA2[1][:, 0:n1 * 64].rearrange("p (j c) -> p j c", c=64), onesb[:],
                               src.unsqueeze(1).broadcast_to([128, n1, 64]), False, False)
                    newc = spN3[:, :, 8 * s_:8 * s_ + 8].unsqueeze(1).broadcast_to([128, 8, 8, 8])
                    for bq in range(2):
                        mm(pA2[bq][:].rearrange("p (j h q) -> p j h q", j=8, h=8), onesb[:], newc, False, True)
                    for bq in range(2):
                        act(xd[:, bq * 512:(bq + 1) * 512], pA2[bq][:], AF.Exp, scale=-1.0)
                    tt("dve", wd, zb, xd, ALU.mult)
                    for j in range(NPAGE):
                        vs_ = vcount % 8
                        vcount += 1
                        col = s_ * NPAGE + j
                        vdst = vring[:, vs_, :]
                        P.add("pool", lambda e, vdst=vdst, col=col: e.indirect_dma_start(
                            out=vdst, out_offset=None, in_=cv,
                            in_offset=bass.IndirectOffsetOnAxis(ap=pidx[:, col:col + 1], axis=0)),
                            [V(pidx[:, col:col + 1])], [V(vdst)], dsem=vsem[vs_])
                        for hp in range(4):
                            mm(pBd[:, hp * 16:hp * 16 + 16], vring[:, vs_, hp * 128:(hp + 1) * 128],
                               wd[:, j * 64 + hp * 16:j * 64 + hp * 16 + 16], j == 0 and hp == 0, False, sgc=True)
                    for hp in range(4):
                        mm(pBd[:, hp * 16:hp * 16 + 16].rearrange("p (a q) -> p a q", a=2), vs_tok[:, hp * 128:(hp + 1) * 128],
                           wN3[:, 2 * hp:2 * hp + 2, 8 * s_:8 * s_ + 8], False, hp == 3, sgc=True)
                    pbv4 = pBd[:, 0:64].rearrange("p (c x) -> p c x", c=4)
                    cp("dve", cat[0:64, 4:8, sc0 + 8 * s_:sc0 + 8 * s_ + 8], pbv4[0:64, :, 0:8])
                    cp("act", cat[64:128, 4:8, sc0 + 8 * s_:sc0 + 8 * s_ + 8], pbv4[64:128, :, 8:16])

            chk('L0 decode gi=%d' % gi)
            residual_proj(8, cat)
            ffn(0)

            chk('L0 ffn gi=%d' % gi)
            norm_T(2)
            o = 0
            gbb = mixv(o, 4 * T, BF16).rearrange("p (c t) -> p c t", c=4); o += 4 * T * 2
            cxp = mixv(o, 4 * (2 + TP), BF16).rearrange("p (c t) -> p c t", c=4); o += 4 * (2 + TP) * 2
            upx = mixv(o, 4 * (16 + TP), BF16).rearrange("p (c t) -> p c t", c=4)[:, :, 1:16 + TP]; o += 4 * (16 + TP) * 2
            if G["sample"]:
                cxs = mixv(o, 4 * 16 * 10, BF16).rearrange("p (c s t) -> p c s t", c=4, s=16); o += 4 * 160 * 2
                usx = mixv(o, 4 * 16 * 24, BF16).rearrange("p (c s t) -> p c s t", c=4, s=16)[:, :, :, 1:24]; o += 4 * 16 * 24 * 2
            gct = mixv(o, 2 * 512, BF16).rearrange("p (a n) -> p a n", a=2); o += 2048
            acc = mixv(o, 16 + T, F32); o += (16 + T) * 4
            acc2 = mixv(o, 16 + T, F32); o += (16 + T) * 4
            pooled = mixv(o, T, BF16); o += T * 2
            sts = mixv(o, 2 * 512, F32).rearrange("p (a n) -> p a n", a=2); o += 4096
            assert o <= MIXB, o
            if gi == 0:
                mset("pool", cxp[:, :, 0:2], 0.0)
                mset("pool", upx[:, :, 0:15], 0.0)
            else:
                cp("pool", cxp[:, :, 0:2], hist_c[:])
                cp("pool", upx[:, :, 0:15], hist_u[:])
                lload(sts[0:32, 0, :], st_sc)
                pb = nbank()
                for c in range(4):
                    tr(pb[:, c * 128:c * 128 + 32], sts[0:32, 0, c * 128:(c + 1) * 128], identf[0:32, 0:32])
                cp("dve", cxs[:, :, :, 0:2], pb[:].rearrange("p (c x) -> p c x", c=4)[:, :, 0:32].rearrange("p c (s r) -> p c s r", s=16))
                for i in range(2):
                    lload(sts[0:120, 1, :], st_pool[i * 120:(i + 1) * 120, :])
                    pb = nbank()
                    for c in range(4):
                        tr(pb[:, c * 128:c * 128 + 120], sts[0:120, 1, c * 128:(c + 1) * 128], identf[0:120, 0:120])
                    cp("dve", usx[:, :, 8 * i:8 * i + 8, 0:15],
                       pb[:].rearrange("p (c x) -> p c x", c=4)[:, :, 0:120].rearrange("p c (s r) -> p c s r", s=8))
                P.add("sp", lambda e: e.dma_start(out=pool_s[:, 0:7, :], in_=st_pool.rearrange("(s r) f -> s r f", r=15)[:, 8:15, :]),
                      [], [], dsem=P.dsem(out=True))

            chk('L1 norm gi=%d' % gi)
            for b in range(2):
                wv = wget(8)
                for fc in range(2):
                    c = 2 * b + fc
                    for (c0, n) in col_tiles(0, T):
                        pa = nbank()
                        pg = nbank()
                        fm_mm(pa[:, 0:n], wv, fc * 128, c0, n)
                        fm_mm(pg[:, 0:n], wv, 256 + fc * 128, c0, n)
                        k = bank_rr["n"] % 2
                        cp("act", gct[:, k, 0:n], pa[:, 0:n])
                        npp = max(0, min(n, TP - c0))
                        if npp > 0:
                            tt("dve", cxp[:, c, 2 + c0:2 + c0 + npp], pg[:, 0:npp], gct[:, k, 0:npp], ALU.mult)
                        if npp < n:
                            tt("dve", cxs[:, c, :, 2:10], pg[:, npp:n].rearrange("p (s t) -> p s t", t=8),
                               gct[:, k, npp:n].rearrange("p (s t) -> p s t", t=8), ALU.mult)
                if gi == 1:
                    for lt, dst in ((npt - 1, "p"), (npt, "s")):
                        pa = nbank()
                        tm_mm(pa[:], wv, 0, 512, lt)
                        si = stage_slot()
                        cp("act", stage[:, si, 0:256], pa[:, 0:256])
                        tt("dve", stage[:, si, 256:512], pa[:, 256:512], stage[:, si, 0:256], ALU.mult)
                        if dst == "p":
                            dma("sp", sc_p[:, b * 256:(b + 1) * 256], stage[126:128, si, 256:512], osem[si], rd=[stage[:, si, 256:512]])
                        else:
                            for s_ in range(NSEQ):
                                dma("sp", sc_s[s_, :, b * 256:(b + 1) * 256], stage[8 * s_ + 6:8 * s_ + 8, si, 256:512], osem[si],
                                    rd=[stage[:, si, 256:512]])
            for b in range(2):
                wv = wget(8)
                for fc in range(2):
                    c = 2 * b + fc
                    for (c0, n) in col_tiles(0, T):
                        pa = nbank()
                        pu = nbank()
                        fm_mm(pa[:, 0:n], wv, fc * 128, c0, n)
                        fm_mm(pu[:, 0:n], wv, 256 + fc * 128, c0, n)
                        cp("act", gbb[:, c, c0:c0 + n], pa[:, 0:n])
                        npp = max(0, min(n, TP - c0))
                        if npp > 0:
                            cp("dve", upx[:, c, 15 + c0:15 + c0 + npp], pu[:, 0:npp])
                        if npp < n:
                            cp("dve", usx[:, c, :, 15:23], pu[:, npp:n].rearrange("p (s t) -> p s t", t=8))
                if gi == 1:
                    for lt, dst in ((npt - 1, "p"), (npt, "s")):
                        pa = nbank()
                        tm_mm(pa[:, 0:256], wv, 256, 256, lt)
                        si = stage_slot()
                        cp("dve", stage[:, si, 0:256], pa[:, 0:256])
                        if dst == "p":
                            dma("sp", pool_p[:, b * 256:(b + 1) * 256], stage[113:128, si, 0:256], osem[si], rd=[stage[:, si, 0:256]])
                        else:
                            for s_ in range(NSEQ):
                                dma("sp", pool_s[s_, 7:15, b * 256:(b + 1) * 256], stage[8 * s_:8 * s_ + 8, si, 0:256], osem[si],
                                    rd=[stage[:, si, 0:256]])
            if gi == 0:
                cp("pool", hist_c[:], cxp[:, :, TP:TP + 2])
                cp("pool", hist_u[:], upx[:, :, TP:TP + 15])

            chk('L1 proj gi=%d' % gi)
            for c in range(4):
                a_ = acc[:, 0:TP]
                ts("dve", a_, cxp[:, c, 0:TP], chv[:, c, 34:35], ALU.mult)
                stt("dve", a_, cxp[:, c, 1:1 + TP], chv[:, c, 35:36], a_, ALU.mult, ALU.add)
                stt("dve", a_, cxp[:, c, 2:2 + TP], chv[:, c, 36:37], a_, ALU.mult, ALU.add)
                tt("pool", cat[:, c, 0:TP], a_, gbb[:, c, 0:TP], ALU.mult)
                if G["sample"]:
                    a3 = acc2[:, 0:128].rearrange("p (s t) -> p s t", t=8)
                    ts("dve", a3, cxs[:, c, :, 0:8], chv[:, c, 34:35], ALU.mult)
                    stt("dve", a3, cxs[:, c, :, 1:9], chv[:, c, 35:36], a3, ALU.mult, ALU.add)
                    stt("dve", a3, cxs[:, c, :, 2:10], chv[:, c, 36:37], a3, ALU.mult, ALU.add)
                    tt("pool", cat[:, c, TP:T], acc2[:, 0:128], gbb[:, c, TP:T], ALU.mult)
            for g, w in enumerate((2, 4, 8, 16)):
                L = 15 + TP
                src = upx[:, g, 0:L]
                cur, nxt = acc[:, 0:L], acc2[:, 0:L]
                sh = 1
                first = True
                while sh < w:
                    a_in = src if first else cur
                    eng = "dve" if (sh in (1, 4)) else "pool"
                    if not first:
                        cp(eng, nxt[:, 0:sh], a_in[:, 0:sh])
                    tt(eng, nxt[:, sh:L], a_in[:, sh:L], a_in[:, 0:L - sh], ALU.add)
                    cur, nxt = nxt, cur
                    first = False
                    sh *= 2
                stt("dve", pooled[:, 0:TP], cur[:, 15:15 + TP], 1.0 / w, upx[:, g, 15:15 + TP], ALU.mult, ALU.subtract)
                if gi == 0:
                    tt("dve", nxt[:, 0:16], cur[:, 15:31], rc[:, g, :], ALU.mult)
                    tt("dve", pooled[:, 0:16], nxt[:, 0:16], upx[:, g, 15:31], ALU.subtract)
                if G["sample"]:
                    s3 = usx[:, g, :, :]
                    c3 = acc[:, 0:16 * 23].rearrange("p (s t) -> p s t", t=23)
                    n3 = acc2[:, 0:16 * 23].rearrange("p (s t) -> p s t", t=23)
                    sh = 1
                    first = True
                    while sh < w:
                        a_in = s3 if first else c3
                        if not first:
                            cp("pool", n3[:, :, 0:sh], a_in[:, :, 0:sh])
                        tt("pool", n3[:, :, sh:23], a_in[:, :, sh:23], a_in[:, :, 0:23 - sh], ALU.add)
                        c3, n3 = n3, c3
                        first = False
                        sh *= 2
                    stt("dve", pooled[:, TP:T].rearrange("p (s t) -> p s t", t=8), c3[:, :, 15:23], 1.0 / w, usx[:, g, :, 15:23],
                        ALU.mult, ALU.subtract)
                for (c0, n) in col_tiles(0, T):
                    pb = nbank()
                    mm(pb[:, 0:n], poolw[:, g, :], pooled[:, c0:c0 + n], True, True)
                    act(cat[:, 4 + g, c0:c0 + n], pb[:, 0:n], AF.Copy, scale=chv[:, g, 37:38])

            residual_proj(8, cat)
            ffn(1)

            chk('L1 mixer+ffn gi=%d' % gi)
            gslot = load_g(4)
            for lt in range(ntl):
                hv = h[:, lt, :]
                c0 = stat_col(3)
                si = stage_slot()
                act(stage[:, si, :], hv, AF.Square, accum=stats[:, c0:c0 + 1])
                act(stats[:, c0 + 1:c0 + 2], stats[:, c0:c0 + 1], AF.Ln, bias=epst[:, 0:1], scale=1.0 / D)
                act(stats[:, c0 + 2:c0 + 3], stats[:, c0 + 1:c0 + 2], AF.Exp, scale=-0.5)
                stt("dve", stage[:, si, :], hv, stats[:, c0 + 2:c0 + 3], gb[:, gslot, :], ALU.mult, ALU.mult)
                dst = yp[ptiles[lt] * 128:(ptiles[lt] + 1) * 128, :] if lt < npt else ys
                dma("sp", dst, stage[:, si, :], osem[si], rd=[stage[:, si, :]])


    try:
        chk('consts')
        run_groups()
    except _Stop as e_:
        print('STOPPED at', e_)
    if limit is None:
        assert wstate["cur"] == len(WSPECS), (wstate, len(WSPECS))
    P.emit()
    es.close()
    return nc


def make_in_maps(inp, compact_cache=False):
    f = lambda a: np.ascontiguousarray(np.asarray(a))
    gvec = f(np.stack([inp["norm_mix_g"][0], inp["norm_ffn_g"][0], inp["norm_mix_g"][1], inp["norm_ffn_g"][1],
                       inp["norm_final_g"]]))
    chv = f(np.concatenate([inp["conv_a_w"][0], inp["conv_a_b"], inp["ln_a_g"], inp["ln_a_b"], inp["conv_c_w"][0],
                            inp["pool_scale"]], axis=0))
    shared = {
        "gvec": gvec, "chv": chv, "sbb": f(inp["sb_bias"]).reshape(1, 8),
        "poolw": f(inp["pool_w"]).reshape(512, 128),
        "w_in_e": f(inp["w_in_even"][0]), "w_out_e": f(inp["w_out_even"][0]),
        "w_in_o": f(inp["w_in_odd"][0]), "w_out_o": f(inp["w_out_odd"][0]),
        "w_ffn_in": f(inp["w_ffn_in"]).reshape(2 * D, 2 * FFN), "w_ffn_out": f(inp["w_ffn_out"]).reshape(2 * FFN, D),
    }
    ck_full = np.asarray(inp["cache_k"])[0].reshape(-1, 512)
    cv_full = np.asarray(inp["cache_v"])[0].reshape(-1, 512)
    maps = []
    for c in range(NCORES):
        sl = slice(NSEQ * c, NSEQ * (c + 1))
        m = dict(shared)
        m["xp"] = f(inp["x_prompt"][c])
        m["xs"] = f(inp["x_sample"][sl]).reshape(128, D)
        m["st_conf"] = f(inp["state_conformer"][0, sl]).reshape(NSEQ * 30, 512)
        m["st_sc"] = f(inp["state_shortconv"][0, sl]).reshape(NSEQ * 2, 512)
        m["st_pool"] = f(inp["state_pool"][0, sl]).reshape(NSEQ * 15, 512)
        ptc = np.asarray(inp["page_table"])[sl].astype(np.int32)
        if compact_cache:
            pages = ptc.reshape(-1)
            m["ck"] = f(ck_full.reshape(-1, 128, 512)[pages]).reshape(-1, 512)
            m["cv"] = f(cv_full.reshape(-1, 128, 512)[pages]).reshape(-1, 512)
            m["pt"] = np.arange(256, dtype=np.int32).reshape(NSEQ, NPAGE)
        else:
            m["ck"] = ck_full
            m["cv"] = cv_full
            m["pt"] = f(ptc)
        maps.append(m)
    return maps


def gather_outputs(res):
    R = res.results
    cat = lambda k: np.stack([R[c][k] for c in range(NCORES)])
    y_p = cat("yp")
    y_s = np.concatenate([R[c]["ys"].reshape(NSEQ, DSEQ, D) for c in range(NCORES)])
    k_p = cat("kp").reshape(1, NCORES, SEQ, 8, 64)
    v_p = cat("vp").reshape(1, NCORES, SEQ, 8, 64)
    k_s = np.concatenate([R[c]["ks"].reshape(NSEQ, DSEQ, 8, 64) for c in range(NCORES)])[None]
    v_s = np.concatenate([R[c]["vs"].reshape(NSEQ, DSEQ, 8, 64) for c in range(NCORES)])[None]
    conv_p = cat("conv_p")[None]
    conv_s = np.concatenate([R[c]["conv_s"] for c in range(NCORES)])[None]
    sc_p = cat("sc_p")[None]
    sc_s = np.concatenate([R[c]["sc_s"] for c in range(NCORES)])[None]
    pool_p = cat("pool_p")[None]
    pool_s = np.concatenate([R[c]["pool_s"] for c in range(NCORES)])[None]
    outs = (y_p, y_s, k_p, v_p, k_s, v_s, conv_p, conv_s, sc_p, sc_s, pool_p, pool_s)
    return tuple(np.ascontiguousarray(o, dtype=np.float32) for o in outs)


def kernel(**inputs):
    npool = int(np.asarray(inputs["cache_k"]).shape[1])
    nc = build_nc(npool)
    in_maps = make_in_maps(inputs)
    res = run_bass_kernel_spmd(nc, in_maps, core_ids=list(range(NCORES)))
    return gather_outputs(res)
```

```python
import numpy as np
from contextlib import ExitStack
import concourse.bass as bass
import concourse.mybir as mybir
from concourse.bass_utils import run_bass_kernel_spmd

F32, BF16, I32 = mybir.dt.float32, mybir.dt.bfloat16, mybir.dt.int32
AF = mybir.ActivationFunctionType
ALU = mybir.AluOpType
ESZ = {F32: 4, BF16: 2, I32: 4}

NCORES = 8
D = 1024
SEQ = 2048
NSEQ = 16
DSEQ = 8
NPAGE = 16
FFN = 2816
NEG = -240000.0
EPS = 1e-6


class V:
    __slots__ = ("ap", "key", "lo", "hi")

    def __init__(self, ap):
        self.ap = ap
        dims = ap.ap
        pstep = dims[0][0]
        off = ap.offset % pstep if pstep > 0 else ap.offset
        ext = 1
        for st, cnt in dims[1:]:
            ext += (cnt - 1) * abs(st)
        es = ESZ[ap.dtype]
        self.key = ap.tensor.name
        self.lo = off * es
        self.hi = (off + ext) * es
        if str(ap.space) == "PSUM":
            self.lo, self.hi = 0, 2048


class DSem:
    def __init__(self, sem, group=False):
        self.sem = sem
        self.count = 0
        self.group = group


class Op:
    __slots__ = ("eng", "fn", "deps", "sig", "cnt", "isdma", "dsem", "dval", "idx")


class Prog:
    ENGS = ("pe", "act", "dve", "pool", "sp")

    def __init__(self, nc, es):
        self.nc = nc
        self.es = es
        self.q = {e: [] for e in self.ENGS}
        self.recs = {}
        self.esem = {e: es.enter_context(nc.semaphore("sem_" + e)) for e in ("pe", "act", "dve", "pool")}
        self.nsem = 0
        self.out_dsems = []

    def dsem(self, group=False, out=False):
        self.nsem += 1
        d = DSem(self.es.enter_context(self.nc.semaphore("dsem%d" % self.nsem)), group)
        self.out_dsems.append(d)
        return d

    def _access(self, v, op, is_write):
        L = self.recs.get(v.key)
        if L is None:
            L = self.recs[v.key] = []
        raw, other = [], []
        newL = []
        lo, hi = v.lo, v.hi
        cov = []
        for rec in L:
            rlo, rhi, w, rd = rec
            if rhi <= lo or rlo >= hi:
                newL.append(rec)
                continue
            if w is not None:
                (other if is_write else raw).append(w)
            if is_write:
                other.extend(rd.values())
            if rlo < lo:
                newL.append([rlo, lo, w, dict(rd)])
            if rhi > hi:
                newL.append([hi, rhi, w, dict(rd)])
            if not is_write:
                nrd = dict(rd)
                nrd[("d", id(op)) if op.isdma else op.eng] = op
                a, b = max(rlo, lo), min(rhi, hi)
                newL.append([a, b, w, nrd])
                cov.append((a, b))
        if is_write:
            newL.append([lo, hi, op, {}])
        else:
            cov.sort()
            cur = lo
            k = ("d", id(op)) if op.isdma else op.eng
            for a, b in cov:
                if a > cur:
                    newL.append([cur, a, None, {k: op}])
                cur = max(cur, b)
            if cur < hi:
                newL.append([cur, hi, None, {k: op}])
        self.recs[v.key] = newL
        return raw, other

    def add(self, eng, fn, reads=(), writes=(), dsem=None):
        op = Op()
        op.eng = eng
        op.fn = fn
        op.isdma = dsem is not None
        op.sig = False
        op.cnt = 0
        op.idx = len(self.q[eng])
        raw, other = [], []
        for v in reads:
            r, o = self._access(v, op, v.key.startswith("ps"))
            raw += r
            other += o
        for v in writes:
            r, o = self._access(v, op, True)
            raw += r
            other += o
        deps = {}
        for lst, is_raw in ((raw, True), (other, False)):
            for d in lst:
                if d is op:
                    continue
                if d.isdma:
                    deps[("dsem", id(d.dsem))] = (d, d.dsem.count)
                    continue
                if (not op.isdma) and d.eng == eng:
                    if eng == "pe":
                        continue
                k = d.eng
                if k not in deps or deps[k].idx < d.idx:
                    deps[k] = d
        op.deps = [x if isinstance(x, tuple) else (x, None) for x in deps.values()]
        for d, _ in op.deps:
            if not d.isdma:
                d.sig = True
        if op.isdma:
            dsem.count += 16
            op.dsem = dsem
            op.dval = dsem.count
        self.q[eng].append(op)
        return op

    def emit(self):
        nc = self.nc
        for eng, L in self.q.items():
            c = 0
            for op in L:
                if (not op.isdma) and op.sig:
                    c += 1
                    op.cnt = c
        engobj = {"pe": "tensor", "act": "scalar", "dve": "vector", "pool": "gpsimd", "sp": "sync"}
        with nc.Block() as block:
            for eng in self.ENGS:
                def body(e, eng=eng):
                    waited = {}
                    for op in self.q[eng]:
                        for d, dv in op.deps:
                            if d.isdma:
                                sem = d.dsem.sem
                                val = d.dsem.count if d.dsem.group else dv
                            else:
                                sem = self.esem[d.eng]
                                val = d.cnt
                            if waited.get(sem.num, 0) < val:
                                e.wait_ge(sem, val)
                                waited[sem.num] = val
                        ins = op.fn(e)
                        if op.isdma:
                            ins.then_inc(op.dsem.sem, 16)
                        elif op.sig:
                            ins.then_inc(self.esem[eng], 1)
                    if eng == "sp":
                        for d in self.out_dsems:
                            if d.count > 0:
                                e.wait_ge(d.sem, d.count)
                getattr(block, engobj[eng])(body)


class _Stop(Exception):
    pass


def build_nc(npool, limit=None):
    nc = bass.Bass("TRN2", target_bir_lowering=False)
    es = ExitStack()
    P = Prog(nc, es)

    def din(name, shape, dt=F32):
        return nc.dram_tensor(name, list(shape), dt, kind="ExternalInput").ap()

    def dout(name, shape, dt=F32):
        return nc.dram_tensor(name, list(shape), dt, kind="ExternalOutput").ap()

    xp = din("xp", [SEQ, D])
    xs = din("xs", [128, D])
    ck = din("ck", [npool * 128, 512])
    cv = din("cv", [npool * 128, 512])
    st_conf = din("st_conf", [NSEQ * 30, 512])
    st_sc = din("st_sc", [NSEQ * 2, 512])
    st_pool = din("st_pool", [NSEQ * 15, 512])
    pt = din("pt", [NSEQ, NPAGE], I32)
    gvec = din("gvec", [5, D])
    chvd = din("chv", [38, 512])
    sbb = din("sbb", [1, 8])
    poolwd = din("poolw", [512, 128])
    w_in_e = din("w_in_e", [D, 2560])
    w_out_e = din("w_out_e", [D, D])
    w_in_o = din("w_in_o", [D, 2048])
    w_out_o = din("w_out_o", [D, D])
    w_ffn_in = din("w_ffn_in", [2 * D, 2 * FFN])
    w_ffn_out = din("w_ffn_out", [2 * FFN, D])

    yp = dout("yp", [SEQ, D])
    ys = dout("ys", [128, D])
    kp = dout("kp", [SEQ, 512])
    vp = dout("vp", [SEQ, 512])
    ks = dout("ks", [128, 512])
    vs = dout("vs", [128, 512])
    conv_p = dout("conv_p", [30, 512])
    conv_s = dout("conv_s", [NSEQ, 30, 512])
    sc_p = dout("sc_p", [2, 512])
    sc_s = dout("sc_s", [NSEQ, 2, 512])
    pool_p = dout("pool_p", [15, 512])
    pool_s = dout("pool_s", [NSEQ, 15, 512])

    def sb(name, shape, dt):
        return es.enter_context(nc.sbuf_tensor(name, list(shape), dt))

    def ps(name):
        return es.enter_context(nc.psum_tensor(name, [128, 512], F32))

    TMAX = 1152
    h = sb("h", [128, 9, D], F32)
    actA = sb("actA", [128, 8, TMAX], BF16)
    KT = sb("KT", [128, 4, SEQ + 128], BF16)
    Vall = sb("Vall", [128, 16, 512], BF16)
    wslot = sb("wslot", [128, 2, 6144], BF16)
    gb = sb("gb", [128, 2, D], F32)
    stage = sb("stage", [128, 2, D], F32)
    xn = sb("xn", [128, 2, D], BF16)
    identb = sb("identb", [128, 128], BF16)
    identf = sb("identf", [128, 128], F32)
    tinc = sb("tinc", [128, 128], BF16)
    onesb = sb("onesb", [128, 128], BF16)
    negm = sb("negm", [128, 128], BF16)
    onesf = sb("onesf", [128, 128], F32)
    chv = sb("chvs", [128, 4, 38], F32)
    chraw = sb("chraw", [38, 512], F32)
    sbias = sb("sbias", [128, 8], F32)
    biasd = sb("biasd", [128, 64], F32)
    epst = sb("epst", [128, 1], F32)
    poolw = sb("poolws", [128, 4, 128], BF16)
    rc = sb("rc", [128, 4, 16], F32)
    stats = sb("stats", [128, 64], F32)
    ptb = sb("ptb", [128, NSEQ * NPAGE], I32)
    pidx = sb("pidx", [128, NSEQ * NPAGE], I32)
    iot = sb("iot", [128, 1], F32)
    hist_a = sb("hist_a", [128, 4, 30], BF16)
    hist_c = sb("hist_c", [128, 4, 2], BF16)
    hist_u = sb("hist_u", [128, 4, 15], BF16)
    MIXB = 66 * 1024
    mix = sb("mix", [128, MIXB // 2], BF16)
    banks = [ps("ps%d" % i) for i in range(8)]

    maskN = sb("maskN", [128, 128], BF16)
    ebl = sb("ebl", [16, 128], F32)
    eblb = sb("eblb", [16, 128], BF16)

    def mixv(off_bytes, nelem, dt):
        assert off_bytes % 4 == 0
        assert off_bytes + nelem * ESZ[dt] <= MIXB, (off_bytes, nelem, MIXB)
        a = mix[:, off_bytes // 2: off_bytes // 2 + nelem * ESZ[dt] // 2]
        return a if dt == BF16 else a.bitcast(dt)

    def rv(x):
        return x if isinstance(x, V) else V(x)

    def act(out, in_, func, bias=0.0, scale=1.0, accum=None):
        out, in_ = rv(out), rv(in_)
        rd = [in_]
        wr = [out]
        kw = {}
        if hasattr(bias, "ap"):
            bias = rv(bias)
            rd.append(bias)
            kw["bias"] = bias.ap
        else:
            kw["bias"] = float(bias)
        if hasattr(scale, "ap"):
            scale = rv(scale)
            rd.append(scale)
            kw["scale"] = scale.ap
        else:
            kw["scale"] = float(scale)
        if accum is not None:
            accum = rv(accum)
            wr.append(accum)
            kw["accum_out"] = accum.ap
        return P.add("act", lambda e: e.activation(out=out.ap, in_=in_.ap, func=func, **kw), rd, wr)

    def tt(eng, out, in0, in1, op):
        out, in0, in1 = rv(out), rv(in0), rv(in1)
        return P.add(eng, lambda e: e.tensor_tensor(out=out.ap, in0=in0.ap, in1=in1.ap, op=op), [in0, in1], [out])

    def ts(eng, out, in0, s1, op0, s2=None, op1=None):
        out, in0 = rv(out), rv(in0)
        rd = [in0]
        a1 = s1
        if hasattr(s1, "ap"):
            s1 = rv(s1)
            rd.append(s1)
            a1 = s1.ap
        a2 = s2
        if s2 is not None and hasattr(s2, "ap"):
            s2 = rv(s2)
            rd.append(s2)
            a2 = s2.ap
        if op1 is None:
            return P.add(eng, lambda e: e.tensor_scalar(out=out.ap, in0=in0.ap, scalar1=a1, scalar2=None, op0=op0), rd, [out])
        return P.add(eng, lambda e: e.tensor_scalar(out=out.ap, in0=in0.ap, scalar1=a1, scalar2=a2, op0=op0, op1=op1), rd, [out])

    def stt(eng, out, in0, s, in1, op0, op1):
        out, in0, in1 = rv(out), rv(in0), rv(in1)
        rd = [in0, in1]
        a = s
        if hasattr(s, "ap"):
            s = rv(s)
            rd.append(s)
            a = s.ap
        return P.add(eng, lambda e: e.scalar_tensor_tensor(out=out.ap, in0=in0.ap, scalar=a, in1=in1.ap, op0=op0, op1=op1), rd, [out])

    def cp(eng, out, in_):
        out, in_ = rv(out), rv(in_)
        if eng == "act":
            return P.add("act", lambda e: e.copy(out=out.ap, in_=in_.ap), [in_], [out])
        return P.add(eng, lambda e: e.tensor_copy(out=out.ap, in_=in_.ap), [in_], [out])

    def mset(eng, out, val):
        out = rv(out)
        return P.add(eng, lambda e: e.memset(out.ap, val), [], [out])

    def asel(out, in_, pattern, op, base, cm):
        out, in_ = rv(out), rv(in_)
        return P.add("pool", lambda e: e.affine_select(out=out.ap, in_=in_.ap, pattern=pattern, compare_op=op, fill=0.0,
                                                       base=base, channel_multiplier=cm), [in_], [out])

    def mm(out, lhsT, rhs, start, stop, tp=None, sgc=False):
        out, lhsT, rhs = rv(out), rv(lhsT), rv(rhs)
        kw = {}
        if tp is not None:
            kw["tile_position"] = tp
        if sgc:
            kw["skip_group_check"] = True
        rd = [lhsT, rhs] + ([] if start else [out])
        return P.add("pe", lambda e: e.matmul(out.ap, lhsT=lhsT.ap, rhs=rhs.ap, start=start, stop=stop, **kw), rd, [out])

    def tr(out, in_, ident):
        out, in_, ident = rv(out), rv(in_), rv(ident)
        return P.add("pe", lambda e: e.transpose(out.ap, in_.ap, ident.ap), [in_, ident], [out])

    def dma(q, out, in_, dsem, rd=(), wr=()):
        return P.add(q, lambda e: e.dma_start(out=out, in_=in_), [rv(x) for x in rd], [rv(x) for x in wr], dsem=dsem)

    ckstate = {"n": 0}

    def chk(name):
        ckstate["n"] += 1
        if limit is not None and ckstate["n"] > limit:
            raise _Stop(name)

    cdsem = P.dsem(group=True)

    def cload(out_ap, in_ap):
        dma("sp", out_ap, in_ap, cdsem, wr=[out_ap])

    def lload(out_ap, in_ap):
        dma("sp", out_ap, in_ap, P.dsem(), wr=[out_ap])

    cload(chraw[0:38, :], chvd)
    cload(sbias[:], sbb.rearrange("a b -> (a b)").partition_broadcast(128))
    cload(ptb[:], pt.rearrange("a b -> (a b)").partition_broadcast(128))
    pw32 = mixv(0, 512, F32).rearrange("p (g d) -> p g d", g=4)
    cload(pw32, poolwd.rearrange("(g c) d -> c g d", c=128))
    cp("dve", poolw[:], pw32)

    mset("pool", onesf[:], 1.0)
    mset("pool", epst[:], EPS)
    asel(identf[:], onesf[:], [[-1, 128]], ALU.is_equal, 0, 1)
    tincf = mixv(4096, 128, F32)
    asel(tincf, onesf[:], [[-1, 128]], ALU.is_ge, 0, 1)
    cp("pool", identb[:], identf[:])
    cp("pool", tinc[:], tincf)
    cp("pool", onesb[:], onesf[:])
    ts("pool", negm[:], tincf, NEG, ALU.mult)
    P.add("pool", lambda e: e.iota(iot[:], [[0, 1]], base=0, channel_multiplier=1, allow_small_or_imprecise_dtypes=True),
          [], [V(iot[:])])
    ts("dve", pidx[:], ptb[:], 128.0, ALU.mult, iot[:, 0:1], ALU.add)
    cp("dve", biasd[:].rearrange("p (h q) -> p h q", q=8), sbias[:].unsqueeze(2).broadcast_to([128, 8, 8]))
    for c in range(4):
        tr(banks[0][0:128, c * 64: c * 64 + 38], chraw[0:38, c * 128:(c + 1) * 128], identf[0:38, 0:38])
    cp("dve", chv[:], banks[0][:, 0:256].rearrange("p (c r) -> p c r", c=4)[:, :, 0:38])
    rcf = mixv(8192, 16, F32)
    P.add("pool", lambda e: e.iota(rcf, [[1, 16]], base=1, channel_multiplier=0, allow_small_or_imprecise_dtypes=True),
          [], [V(rcf)])
    for g, w in enumerate((2, 4, 8, 16)):
        ts("dve", rc[:, g, :], rcf, float(w), ALU.min)
    P.add("dve", lambda e: e.reciprocal(out=rc[:], in_=rc[:]), [V(rc[:])], [V(rc[:])])
    mset("pool", ebl[:], 1.0)
    asel(ebl[:], ebl[:], [[1, 128]], ALU.is_ge, 0, -8)
    asel(ebl[:], ebl[:], [[-1, 128]], ALU.is_ge, 7, 8)
    cp("pool", eblb[:], ebl[:])
    mm(banks[1][:, 0:128], eblb[0:16, :], eblb[0:16, :], True, True)
    blk = mixv(12288, 128, F32)
    cp("dve", blk, banks[1][:, 0:128])
    asel(blk, blk, [[1, 128]], ALU.is_gt, 0, -1)
    ts("dve", maskN[:], blk, -NEG, ALU.mult, NEG, ALU.add)

    wsems = [[P.dsem(), P.dsem()], [P.dsem(), P.dsem()]]
    WSPECS = []

    def wspec_layer0():
        for b in range(2):
            WSPECS.append([(w_in_e[:, b * 256:(b + 1) * 256], 8, 0, 256, 512),
                           (w_in_e[:, 512 + b * 256:512 + (b + 1) * 256], 8, 256, 256, 512)])
        for q0 in (1024, 1536, 2048):
            WSPECS.append([(w_in_e[:, q0:q0 + 512], 8, 0, 512, 512)])

    def wspec_out(w):
        for ch in range(2):
            WSPECS.append([(w[:, ch * 512:(ch + 1) * 512], 8, 0, 512, 512)])

    def wspec_ffn(layer):
        wi = w_ffn_in[layer * D:(layer + 1) * D, :]
        wo = w_ffn_out[layer * FFN:(layer + 1) * FFN, :]
        for (b0, b1) in ((0, 6), (6, 11)):
            for b in range(b0, b1):
                WSPECS.append([(wi[:, b * 256:(b + 1) * 256], 8, 0, 256, 512),
                               (wi[:, FFN + b * 256:FFN + (b + 1) * 256], 8, 256, 256, 512)])
            nch = 2 * (b1 - b0)
            for ch in range(2):
                WSPECS.append([(wo[b0 * 256:b1 * 256, ch * 512:(ch + 1) * 512], nch, 0, 512, 512)])

    def wspec_layer1():
        for b in range(2):
            WSPECS.append([(w_in_o[:, 512 + b * 256:512 + (b + 1) * 256], 8, 0, 256, 512),
                           (w_in_o[:, 1024 + b * 256:1024 + (b + 1) * 256], 8, 256, 256, 512)])
        for b in range(2):
            WSPECS.append([(w_in_o[:, b * 256:(b + 1) * 256], 8, 0, 256, 512),
                           (w_in_o[:, 1536 + b * 256:1536 + (b + 1) * 256], 8, 256, 256, 512)])

    for _g in range(2):
        wspec_layer0()
        wspec_out(w_out_e)
        wspec_ffn(0)
        wspec_layer1()
        wspec_out(w_out_o)
        wspec_ffn(1)
    wstate = {"cur": 0, "issued": 0}

    def wissue(i):
        s = i % 2
        for pi, (src, nk, c0, ncols, stride) in enumerate(WSPECS[i]):
            dst = wslot[:, s, 0:nk * stride].rearrange("p (k n) -> p k n", k=nk)[:, :, c0:c0 + ncols]
            srcv = src.rearrange("(k p) n -> p k n", p=128)
            P.add("pool", lambda e, dst=dst, srcv=srcv: e.dma_start(out=dst, in_=srcv), [], [V(dst)], dsem=wsems[s][pi])

    def wget(nk, stride=512):
        i = wstate["cur"]
        wstate["cur"] += 1
        while wstate["issued"] <= min(i + 1, len(WSPECS) - 1):
            wissue(wstate["issued"])
            wstate["issued"] += 1
        assert WSPECS[i][0][1] == nk, (i, nk, WSPECS[i][0][1])
        return wslot[:, i % 2, 0:nk * stride].rearrange("p (k n) -> p k n", k=nk)

    groups = [
        dict(ptiles=list(range(0, 9)), sample=False),
        dict(ptiles=list(range(9, 16)), sample=True),
    ]
    xsems = [P.dsem() for _ in range(9)]
    gsem = [P.dsem(), P.dsem()]
    osem = [P.dsem(out=True), P.dsem(out=True)]
    ostate = {"n": 0}

    def stage_slot():
        i = ostate["n"] % 2
        ostate["n"] += 1
        return i

    gstate = {"n": 0}

    def load_g(row):
        i = gstate["n"] % 2
        gstate["n"] += 1
        dma("sp", gb[:, i, :], gvec[row].partition_broadcast(128), gsem[i], wr=[gb[:, i, :]])
        return i

    statc = {"n": 0}

    def stat_col(n=3):
        c = statc["n"]
        if c + n > 64:
            c = 0
        statc["n"] = c + n
        return c

    def col_tiles(lo, hi, step=512):
        r = []
        c = lo
        while c < hi:
            r.append((c, min(step, hi - c)))
            c += step
        return r

    bank_rr = {"n": 0}

    def nbank(lo=2, hi=8):
        b = lo + bank_rr["n"] % (hi - lo)
        bank_rr["n"] += 1
        return banks[b]

    def run_groups():
        for gi, G in enumerate(groups):
            ptiles = G["ptiles"]
            npt = len(ptiles)
            ntl = npt + (1 if G["sample"] else 0)
            TP = npt * 128
            T = ntl * 128
            first_tok = ptiles[0] * 128
            hnT = actA
            cat = actA

            for lt, gt in enumerate(ptiles):
                dma("sp", h[:, lt, :], xp[gt * 128:(gt + 1) * 128, :], xsems[lt], wr=[h[:, lt, :]])
            if G["sample"]:
                dma("sp", h[:, npt, :], xs, xsems[npt], wr=[h[:, npt, :]])

            def norm_T(grow):
                gslot = load_g(grow)
                c0 = stat_col(3 * ntl)
                for lt in range(ntl):
                    act(xn[:, lt % 2, :], h[:, lt, :], AF.Square, accum=stats[:, c0 + lt:c0 + lt + 1])
                act(stats[:, c0 + ntl:c0 + 2 * ntl], stats[:, c0:c0 + ntl], AF.Ln, bias=epst[:, 0:1], scale=1.0 / D)
                act(stats[:, c0 + 2 * ntl:c0 + 3 * ntl], stats[:, c0 + ntl:c0 + 2 * ntl], AF.Exp, scale=-0.5)
                for lt in range(ntl):
                    hv = h[:, lt, :]
                    xs_ = lt % 2
                    stt("dve", xn[:, xs_, :], hv, stats[:, c0 + 2 * ntl + lt:c0 + 2 * ntl + lt + 1], gb[:, gslot, :], ALU.mult, ALU.mult)
                    pb = banks[lt % 2]
                    pbv = pb[:].bitcast(BF16)
                    for kc in range(8):
                        tr(pbv[:, kc * 128:(kc + 1) * 128], xn[:, xs_, kc * 128:(kc + 1) * 128], identb[:])
                    cp("dve" if lt % 2 == 0 else "act", hnT[:, :, lt * 128:(lt + 1) * 128],
                       pbv.rearrange("p (k n) -> p k n", k=8))

            def fm_mm(psv, wv, wc0, c0, n):
                for kc in range(8):
                    mm(psv, wv[:, kc, wc0:wc0 + 128], hnT[:, kc, c0:c0 + n], kc == 0, kc == 7)

            def tm_mm(psv, wv, wc0, ncols, lt):
                for kc in range(8):
                    mm(psv, hnT[:, kc, lt * 128:(lt + 1) * 128], wv[:, kc, wc0:wc0 + ncols], kc == 0, kc == 7)

            def residual_proj(nk_total, src):
                for ch in range(2):
                    wv = wget(nk_total)
                    for lt in range(ntl):
                        pb = nbank()
                        for kc in range(nk_total):
                            mm(pb[:], src[:, kc, lt * 128:(lt + 1) * 128], wv[:, kc, :], kc == 0, kc == nk_total - 1)
                        tt("dve", h[:, lt, ch * 512:(ch + 1) * 512], h[:, lt, ch * 512:(ch + 1) * 512], pb[:], ALU.add)

            def ffn(layer):
                hid = mixv(0, 12 * T, BF16).rearrange("p (c t) -> p c t", c=12)
                sg = mixv(12 * T * 2, 2 * 512, BF16).rearrange("p (a n) -> p a n", a=2)
                norm_T(1 + 2 * layer)
                for half, (b0, b1) in enumerate(((0, 6), (6, 11))):
                    nch = 2 * (b1 - b0)
                    for b in range(b0, b1):
                        wv = wget(8)
                        for fc in range(2):
                            lc = 2 * (b - b0) + fc
                            for (c0, n) in col_tiles(0, T):
                                pg = nbank()
                                pu = nbank()
                                fm_mm(pg[:, 0:n], wv, fc * 128, c0, n)
                                fm_mm(pu[:, 0:n], wv, 256 + fc * 128, c0, n)
                                k = bank_rr["n"] % 2
                                act(sg[:, k, 0:n], pg[:, 0:n], AF.Silu)
                                tt("dve", hid[:, lc, c0:c0 + n], pu[:, 0:n], sg[:, k, 0:n], ALU.mult)
                    residual_proj(nch, hid)

            norm_T(0)
            o = 0
            QT = mixv(o, 4 * T, BF16).rearrange("p (c t) -> p c t", c=4); o += 4 * T * 2
            sgt = mixv(o, 2 * 512, F32).rearrange("p (a n) -> p a n", a=2)
            vs_tok = mixv(o, 512, BF16)
            lntmp = mixv(o + 1024, 3 * 512, BF16).rearrange("p (a n) -> p a n", a=3)
            o += 4096
            R1 = o
            extp = mixv(o, 4 * (30 + TP), BF16).rearrange("p (c t) -> p c t", c=4); o += 4 * (30 + TP) * 2
            if G["sample"]:
                exts = mixv(o, 4 * 16 * 38, BF16).rearrange("p (c s t) -> p c s t", c=4, s=16); o += 4 * 16 * 38 * 2
            diag2 = mixv(o, 2 * 31 * 128, BF16).rearrange("p (d j n) -> p d j n", d=2, j=31); o += 2 * 31 * 128 * 2
            a16o = o
            a16 = mixv(o, 4 * T, BF16).rearrange("p (c t) -> p c t", c=4); o += 4 * T * 2
            a2 = mixv(o, 4 * T, BF16).rearrange("p (c t) -> p c t", c=4); o += 4 * T * 2
            lnt = mixv(o, 4 * 512, F32).rearrange("p (a n) -> p a n", a=4); o += 4 * 2048
            assert o <= MIXB, o

            if gi == 0:
                mset("pool", extp[:, :, 0:30], 0.0)
            else:
                cp("pool", extp[:, :, 0:30], hist_a[:])
                stt_ = mixv(a16o, 4 * 512, F32).rearrange("p (a n) -> p a n", a=4)
                for i in range(4):
                    lload(stt_[0:120, i, :], st_conf[i * 120:(i + 1) * 120, :])
                for i in range(4):
                    pb = nbank()
                    for c in range(4):
                        tr(pb[:, c * 128:c * 128 + 120], stt_[0:120, i, c * 128:(c + 1) * 128], identf[0:120, 0:120])
                    cp("dve", exts[:, :, 4 * i:4 * i + 4, 0:30],
                       pb[:].rearrange("p (c x) -> p c x", c=4)[:, :, 0:120].rearrange("p c (s r) -> p c s r", s=4))
                P.add("sp", lambda e: e.dma_start(out=conv_s[:, 0:22, :], in_=st_conf.rearrange("(s r) f -> s r f", r=30)[:, 8:30, :]),
                      [], [], dsem=P.dsem(out=True))

            chk('L0 norm gi=%d' % gi)
            for b in range(2):
                wv = wget(8)
                for fc in range(2):
                    c = 2 * b + fc
                    for (c0, n) in col_tiles(0, T):
                        pa = nbank()
                        pg = nbank()
                        fm_mm(pa[:, 0:n], wv, fc * 128, c0, n)
                        fm_mm(pg[:, 0:n], wv, 256 + fc * 128, c0, n)
                        k = bank_rr["n"] % 2
                        act(sgt[:, k, 0:n], pg[:, 0:n], AF.Sigmoid)
                        npp = max(0, min(n, TP - c0))
                        if npp > 0:
                            tt("dve", extp[:, c, 30 + c0:30 + c0 + npp], pa[:, 0:npp], sgt[:, k, 0:npp], ALU.mult)
                        if npp < n:
                            assert n - npp == 128
                            tt("dve", exts[:, c, :, 30:38], pa[:, npp:n].rearrange("p (s t) -> p s t", t=8),
                               sgt[:, k, npp:n].rearrange("p (s t) -> p s t", t=8), ALU.mult)
                if gi == 1:
                    for lt, dst in ((npt - 1, "p"), (npt, "s")):
                        pa = nbank()
                        tm_mm(pa[:], wv, 0, 512, lt)
                        si = stage_slot()
                        act(stage[:, si, 0:256], pa[:, 256:512], AF.Sigmoid)
                        tt("dve", stage[:, si, 256:512], pa[:, 0:256], stage[:, si, 0:256], ALU.mult)
                        if dst == "p":
                            dma("sp", conv_p[:, b * 256:(b + 1) * 256], stage[98:128, si, 256:512], osem[si], rd=[stage[:, si, 256:512]])
                        else:
                            for s_ in range(NSEQ):
                                dma("sp", conv_s[s_, 22:30, b * 256:(b + 1) * 256], stage[8 * s_:8 * s_ + 8, si, 256:512], osem[si],
                                    rd=[stage[:, si, 256:512]])
            chk('L0 glu gi=%d' % gi)
            wv = wget(8)
            for c in range(4):
                for (c0, n) in col_tiles(0, T):
                    pb = nbank()
                    fm_mm(pb[:, 0:n], wv, c * 128, c0, n)
                    cp("act", QT[:, c, c0:c0 + n], pb[:, 0:n])
            wv = wget(8)
            for c in range(4):
                for (c0, n) in col_tiles(0, T):
                    pb = nbank()
                    fm_mm(pb[:, 0:n], wv, c * 128, c0, n)
                    npp = max(0, min(n, TP - c0))
                    if npp > 0:
                        cp("act", KT[:, c, first_tok + c0:first_tok + c0 + npp], pb[:, 0:npp])
                    if npp < n:
                        cp("act", KT[:, c, SEQ:SEQ + 128], pb[:, npp:n])
            for lt in range(ntl):
                pb = nbank()
                tm_mm(pb[:], wv, 0, 512, lt)
                si = stage_slot()
                cp("dve", stage[:, si, 0:512], pb[:])
                dst = kp[ptiles[lt] * 128:(ptiles[lt] + 1) * 128, :] if lt < npt else ks
                dma("sp", dst, stage[:, si, 0:512], osem[si], rd=[stage[:, si, 0:512]])
            wv = wget(8)
            for lt in range(ntl):
                pb = nbank()
                tm_mm(pb[:], wv, 0, 512, lt)
                si = stage_slot()
                cp("dve", stage[:, si, 0:512], pb[:])
                if lt < npt:
                    cp("act", Vall[:, ptiles[lt], :], pb[:])
                    dst = vp[ptiles[lt] * 128:(ptiles[lt] + 1) * 128, :]
                else:
                    cp("act", vs_tok, pb[:])
                    dst = vs
                dma("sp", dst, stage[:, si, 0:512], osem[si], rd=[stage[:, si, 0:512]])

            chk('L0 qkv gi=%d' % gi)
            for c in range(4):
                diag = diag2[:, c % 2]
                for j in range(31):
                    ts("dve", diag[:, j, :], identb[:], chv[:, c, j:j + 1], ALU.mult)
                for (c0, n) in col_tiles(0, TP):
                    pb = nbank()
                    for j in range(31):
                        mm(pb[:, 0:n], diag[:, j, :], extp[:, c, c0 + j:c0 + j + n], j == 0, j == 30)
                    act(a16[:, c, c0:c0 + n], pb[:, 0:n], AF.Identity, bias=chv[:, c, 31:32])
                    act(a2[:, c, c0:c0 + n], pb[:, 0:n], AF.Square, bias=chv[:, c, 31:32])
                if G["sample"]:
                    pb = nbank()
                    for j in range(31):
                        mm(pb[:, 0:128].rearrange("p (s t) -> p s t", t=8), diag[:, j, :], exts[:, c, :, j:j + 8], j == 0, j == 30)
                    act(a16[:, c, TP:T], pb[:, 0:128], AF.Identity, bias=chv[:, c, 31:32])
                    act(a2[:, c, TP:T], pb[:, 0:128], AF.Square, bias=chv[:, c, 31:32])
            if gi == 0:
                cp("pool", hist_a[:], extp[:, :, TP:TP + 30])
            cts = col_tiles(0, T)
            stat_ps = []
            for ci, (c0, n) in enumerate(cts):
                pm = banks[2 + 2 * ci]
                pv_ = banks[3 + 2 * ci]
                for c in range(4):
                    mm(pm[:, 0:n], onesb[:], a16[:, c, c0:c0 + n], c == 0, c == 3)
                for c in range(4):
                    mm(pv_[:, 0:n], onesb[:], a2[:, c, c0:c0 + n], c == 0, c == 3)
                stat_ps.append((pm, pv_))
            ti = 0
            for ci, (c0, n) in enumerate(cts):
                pm, pv_ = stat_ps[ci]
                mean = lnt[:, 2 * (ci % 2), 0:n]
                rstd = lnt[:, 2 * (ci % 2) + 1, 0:n]
                ts("dve", mean, pm[:, 0:n], 1.0 / 512, ALU.mult)
                tt("pool", rstd, mean, mean, ALU.mult)
                stt("dve", rstd, pv_[:, 0:n], 1.0 / 512, rstd, ALU.mult, ALU.subtract)
                act(rstd, rstd, AF.Ln, bias=epst[:, 0:1])
                act(rstd, rstd, AF.Exp, scale=-0.5)
                for c in range(4):
                    tmp = lntmp[:, ti % 3, 0:n]
                    ti += 1
                    tt("dve", tmp, a16[:, c, c0:c0 + n], mean, ALU.subtract)
                    tt("dve", tmp, tmp, rstd, ALU.mult)
                    act(cat[:, c, c0:c0 + n], tmp, AF.Silu, bias=chv[:, c, 33:34], scale=chv[:, c, 32:33])
            chk('L0 conformer gi=%d' % gi)
            o = R1
            ebuf = mixv(o, 3 * 512, F32).rearrange("p (a n) -> p a n", a=3); o += 6144
            xbuf = mixv(o, 2 * 512, F32).rearrange("p (a n) -> p a n", a=2); o += 4096
            spb = mixv(o, 3 * 512, BF16).rearrange("p (a n) -> p a n", a=3); o += 3072
            wb = mixv(o, 2 * 512, BF16).rearrange("p (a n) -> p a n", a=2); o += 2048
            lacc = mixv(o, 3 * 512, BF16).rearrange("p (a n) -> p a n", a=3); o += 3072
            its = []
            qgroups = [ptiles[i:i + 4] for i in range(0, npt, 4)]
            for qg in qgroups:
                gq0, gq1 = qg[0], qg[-1]
                NQ = len(qg) * 128
                ql0 = (gq0 - ptiles[0]) * 128
                for hp in range(4):
                    for hh in range(2):
                        for j in range(gq1, -1, -1):
                            its.append(dict(gq0=gq0, gq1=gq1, NQ=NQ, ql0=ql0, hp=hp, hh=hh, j=j,
                                            first=(j == gq1), last=(j == 0), i=len(its),
                                            off=(max(gq0, j) - gq0) * 128))

            def stageA(I):
                i, off, NQ, hp, hh, j = I["i"], I["off"], I["NQ"], I["hp"], I["hh"], I["j"]
                hsl = slice(hh * 64, hh * 64 + 64)
                hd = 2 * hp + hh
                pS = banks[2 + i % 2]
                k3 = i % 3
                diagblk = j >= I["gq0"]
                mm(pS[:, off:NQ], KT[hsl, hp, j * 128:(j + 1) * 128], QT[hsl, hp, I["ql0"] + off:I["ql0"] + NQ], True, not diagblk)
                if diagblk:
                    mm(pS[:, off:off + 128], identb[:], negm[:], False, True)
                act(ebuf[:, k3, off:NQ], pS[:, off:NQ], AF.Exp, bias=sbias[:, hd:hd + 1], scale=0.125)
                act(spb[:, k3, off:NQ], ebuf[:, k3, off:NQ], AF.Ln, bias=1.0)
                if not I["last"]:
                    ln_ = (i + 1) % 3
                    if off > 0:
                        mset("pool", lacc[:, ln_, 0:off], 0.0)
                    if I["first"]:
                        cp("pool", lacc[:, ln_, off:NQ], spb[:, k3, off:NQ])
                    else:
                        tt("pool", lacc[:, ln_, off:NQ], lacc[:, i % 3, off:NQ], spb[:, k3, off:NQ], ALU.add)

            def stageB(I):
                i, off, NQ, hp, hh, j = I["i"], I["off"], I["NQ"], I["hp"], I["hh"], I["j"]
                pA = banks[4 + i % 2]
                k3 = i % 3
                k2 = i % 2
                mm(pA[:, off:NQ], tinc[:], spb[:, k3, off:NQ], True, I["first"])
                if not I["first"]:
                    mm(pA[:, off:NQ], onesb[:], lacc[:, i % 3, off:NQ], False, True)
                act(xbuf[:, k2, off:NQ], pA[:, off:NQ], AF.Exp, scale=-1.0)
                tt("dve", wb[:, k2, off:NQ], ebuf[:, k3, off:NQ], xbuf[:, k2, off:NQ], ALU.mult)

            def stageC(I):
                i, off, NQ, hp, hh, j = I["i"], I["off"], I["NQ"], I["hp"], I["hh"], I["j"]
                hsl = slice(hh * 64, hh * 64 + 64)
                hd = 2 * hp + hh
                pB = banks[6 + (hp % 2)]
                k2 = i % 2
                mm(pB[hsl, off:NQ], Vall[:, j, hd * 64:(hd + 1) * 64], wb[:, k2, off:NQ], I["first"], I["last"],
                   tp=(0, 64) if hh == 1 else None, sgc=True)
                if I["last"] and hh == 1:
                    cp("dve", cat[:, 4 + hp, I["ql0"]:I["ql0"] + NQ], pB[:, 0:NQ])

            n_it = len(its)
            for step in range(n_it + 2):
                if step < n_it:
                    stageA(its[step])
                if 0 <= step - 1 < n_it:
                    stageB(its[step - 1])
                if 0 <= step - 2 < n_it:
                    stageC(its[step - 2])

            chk('L0 attn gi=%d' % gi)
            if G["sample"]:
                sc0 = TP
                o = R1
                qblk = mixv(o, 4 * 16 * 16, BF16).rearrange("p (c s x) -> p c s x", c=4, s=16); o += 2048
                kring = mixv(o, 4 * 512, BF16).rearrange("p (a f) -> p a f", a=4); o += 4096
                ktp = mixv(o, 2 * 512, BF16).rearrange("p (a f) -> p a f", a=2); o += 2048
                vring = mixv(o, 16 * 512, BF16).rearrange("p (a f) -> p a f", a=16); o += 16384
                zb = mixv(o, 1024, F32); o += 4096
                xd = mixv(o, 1024, F32); o += 4096
                spd = mixv(o, 1024, BF16); o += 2048
                wd = mixv(o, 1024, BF16); o += 2048
                eN = mixv(o, 1024, F32); o += 4096
                spN = mixv(o, 1024, BF16); o += 2048
                wN = mixv(o, 1024, BF16); o += 2048
                assert o <= MIXB, o
                ksem = [P.dsem() for _ in range(4)]
                vsem = [P.dsem() for _ in range(16)]
                mset("pool", qblk[:], 0.0)
                for hp in range(4):
                    cp("pool", qblk[0:64, hp, :, 0:8], QT[0:64, hp, sc0:sc0 + 128].rearrange("p (s t) -> p s t", t=8))
                    cp("pool", qblk[64:128, hp, :, 8:16], QT[64:128, hp, sc0:sc0 + 128].rearrange("p (s t) -> p s t", t=8))
                pT = banks[0]
                pTv = pT[:].bitcast(BF16)
                pS2 = [banks[1], banks[2]]
                pA2 = [banks[3], banks[4]]
                pBd = banks[5]
                for hb in range(2):
                    for hd in range(4 * hb, 4 * hb + 4):
                        hp, hh = hd // 2, hd % 2
                        hsl = slice(hh * 64, hh * 64 + 64)
                        dstp = pS2[hb][:, (hd % 4) * 128:(hd % 4 + 1) * 128]
                        mm(dstp, KT[hsl, hp, SEQ:SEQ + 128], QT[hsl, hp, sc0:sc0 + 128], hd % 4 == 0, False, sgc=True)
                        mm(dstp, identb[:], maskN[:], False, True, sgc=True)
                    for hd in range(4 * hb, 4 * hb + 4):
                        dstp = pS2[hb][:, (hd % 4) * 128:(hd % 4 + 1) * 128]
                        act(eN[:, hd * 128:(hd + 1) * 128], dstp, AF.Exp, bias=sbias[:, hd:hd + 1], scale=0.125)
                act(spN, eN, AF.Ln, bias=1.0)
                for bq in range(2):
                    mm(pA2[bq][:], tinc[:], spN[:, bq * 512:(bq + 1) * 512], True, True)
                    act(xd[:, bq * 512:(bq + 1) * 512], pA2[bq][:], AF.Exp, scale=-1.0)
                tt("dve", wN, eN, xd, ALU.mult)
                spN3 = spN.rearrange("p (h c) -> p h c", h=8)
                wN3 = wN.rearrange("p (h c) -> p h c", h=8)
                gstate_ = {"g": 0}
                pSset = [[banks[1], banks[2]], [banks[6], banks[7]]]

                def Kphase(s_):
                    pS2_ = pSset[s_ % 2]
                    for j in range(NPAGE):
                        gcount = gstate_["g"]
                        ks_ = gcount % 4
                        col = s_ * NPAGE + j
                        kdst = kring[:, ks_, :]
                        P.add("pool", lambda e, kdst=kdst, col=col: e.indirect_dma_start(
                            out=kdst, out_offset=None, in_=ck,
                            in_offset=bass.IndirectOffsetOnAxis(ap=pidx[:, col:col + 1], axis=0)),
                            [V(pidx[:, col:col + 1])], [V(kdst)], dsem=ksem[ks_])
                        half = gcount % 2
                        for c in range(4):
                            tr(pTv[:, half * 512 + c * 128: half * 512 + (c + 1) * 128], kring[:, ks_, c * 128:(c + 1) * 128], identb[:])
                        cp("dve" if gcount % 2 == 0 else "act", ktp[:, half, :], pTv[:, half * 512:(half + 1) * 512])
                        for hp in range(4):
                            mm(pS2_[j // 8][:, (j % 8) * 64 + hp * 16:(j % 8) * 64 + hp * 16 + 16],
                               ktp[:, half, hp * 128:(hp + 1) * 128], qblk[:, hp, s_, :], True, True)
                        gstate_["g"] += 1

                def Vgather(s_):
                    for j in range(NPAGE):
                        col = s_ * NPAGE + j
                        vdst = vring[:, j, :]
                        P.add("pool", lambda e, vdst=vdst, col=col: e.indirect_dma_start(
                            out=vdst, out_offset=None, in_=cv,
                            in_offset=bass.IndirectOffsetOnAxis(ap=pidx[:, col:col + 1], axis=0)),
                            [V(pidx[:, col:col + 1])], [V(vdst)], dsem=vsem[j])

                def E1(s_):
                    pS2_ = pSset[s_ % 2]
                    for bq in range(2):
                        stt("dve", zb[:, bq * 512:(bq + 1) * 512].rearrange("p (j c) -> p j c", c=64),
                            pS2_[bq][:].rearrange("p (j c) -> p j c", c=64),
                            0.125, biasd[:].unsqueeze(1).broadcast_to([128, 8, 64]), ALU.mult, ALU.add)
                    act(zb, zb, AF.Exp)
                    act(spd, zb, AF.Ln, bias=1.0)
                    for bq in range(2):
                        mm(pA2[bq][:], tinc[:], spd[:, bq * 512:(bq + 1) * 512], True, False)
                    for jp in range(1, NPAGE):
                        src = spd[:, jp * 64:(jp + 1) * 64]
                        n0 = min(jp, 8)
                        mm(pA2[0][:, 0:n0 * 64].rearrange("p (j c) -> p j c", c=64), onesb[:],
                           src.unsqueeze(1).broadcast_to([128, n0, 64]), False, False)
                        if jp > 8:
                            n1 = jp - 8
                            mm(pA2[1][:, 0:n1 * 64].rearrange("p (j c) -> p j c", c=64), onesb[:],
                               src.unsqueeze(1).broadcast_to([128, n1, 64]), False, False)
                    newc = spN3[:, :, 8 * s_:8 * s_ + 8].unsqueeze(1).broadcast_to([128, 8, 8, 8])
                    for bq in range(2):
                        mm(pA2[bq][:].rearrange("p (j h q) -> p j h q", j=8, h=8), onesb[:], newc, False, True)

                def E2(s_):
                    for bq in range(2):
                        act(xd[:, bq * 512:(bq + 1) * 512], pA2[bq][:], AF.Exp, scale=-1.0)
                    tt("dve", wd, zb, xd, ALU.mult)

                def PVphase(s_):
                    for j in range(NPAGE):
                        for hp in range(4):
                            mm(pBd[:, hp * 16:hp * 16 + 16], vring[:, j, hp * 128:(hp + 1) * 128],
                               wd[:, j * 64 + hp * 16:j * 64 + hp * 16 + 16], j == 0 and hp == 0, False, sgc=True)
                    for hp in range(4):
                        mm(pBd[:, hp * 16:hp * 16 + 16].rearrange("p (a q) -> p a q", a=2), vs_tok[:, hp * 128:(hp + 1) * 128],
                           wN3[:, 2 * hp:2 * hp + 2, 8 * s_:8 * s_ + 8], False, hp == 3, sgc=True)
                    pbv4 = pBd[:, 0:64].rearrange("p (c x) -> p c x", c=4)
                    cp("dve", cat[0:64, 4:8, sc0 + 8 * s_:sc0 + 8 * s_ + 8], pbv4[0:64, :, 0:8])
                    cp("act", cat[64:128, 4:8, sc0 + 8 * s_:sc0 + 8 * s_ + 8], pbv4[64:128, :, 8:16])

                Kphase(0)
                for s_ in range(NSEQ):
                    Vgather(s_)
                    E1(s_)
                    E2(s_)
                    if s_ + 1 < NSEQ:
                        Kphase(s_ + 1)
                    PVphase(s_)

            chk('L0 decode gi=%d' % gi)
            residual_proj(8, cat)
            ffn(0)

            chk('L0 ffn gi=%d' % gi)
            norm_T(2)
            o = 0
            gbb = mixv(o, 4 * T, BF16).rearrange("p (c t) -> p c t", c=4); o += 4 * T * 2
            cxp = mixv(o, 4 * (2 + TP), BF16).rearrange("p (c t) -> p c t", c=4); o += 4 * (2 + TP) * 2
            upx = mixv(o, 4 * (16 + TP), BF16).rearrange("p (c t) -> p c t", c=4)[:, :, 1:16 + TP]; o += 4 * (16 + TP) * 2
            if G["sample"]:
                cxs = mixv(o, 4 * 16 * 10, BF16).rearrange("p (c s t) -> p c s t", c=4, s=16); o += 4 * 160 * 2
                usx = mixv(o, 4 * 16 * 24, BF16).rearrange("p (c s t) -> p c s t", c=4, s=16)[:, :, :, 1:24]; o += 4 * 16 * 24 * 2
            gct = mixv(o, 2 * 512, BF16).rearrange("p (a n) -> p a n", a=2); o += 2048
            acc = mixv(o, 16 + T, F32); o += (16 + T) * 4
            acc2 = mixv(o, 16 + T, F32); o += (16 + T) * 4
            pooled = mixv(o, T, BF16); o += T * 2
            sts = mixv(o, 2 * 512, F32).rearrange("p (a n) -> p a n", a=2); o += 4096
            assert o <= MIXB, o
            if gi == 0:
                mset("pool", cxp[:, :, 0:2], 0.0)
                mset("pool", upx[:, :, 0:15], 0.0)
            else:
                cp("pool", cxp[:, :, 0:2], hist_c[:])
                cp("pool", upx[:, :, 0:15], hist_u[:])
                lload(sts[0:32, 0, :], st_sc)
                pb = nbank()
                for c in range(4):
                    tr(pb[:, c * 128:c * 128 + 32], sts[0:32, 0, c * 128:(c + 1) * 128], identf[0:32, 0:32])
                cp("dve", cxs[:, :, :, 0:2], pb[:].rearrange("p (c x) -> p c x", c=4)[:, :, 0:32].rearrange("p c (s r) -> p c s r", s=16))
                for i in range(2):
                    lload(sts[0:120, 1, :], st_pool[i * 120:(i + 1) * 120, :])
                    pb = nbank()
                    for c in range(4):
                        tr(pb[:, c * 128:c * 128 + 120], sts[0:120, 1, c * 128:(c + 1) * 128], identf[0:120, 0:120])
                    cp("dve", usx[:, :, 8 * i:8 * i + 8, 0:15],
                       pb[:].rearrange("p (c x) -> p c x", c=4)[:, :, 0:120].rearrange("p c (s r) -> p c s r", s=8))
                P.add("sp", lambda e: e.dma_start(out=pool_s[:, 0:7, :], in_=st_pool.rearrange("(s r) f -> s r f", r=15)[:, 8:15, :]),
                      [], [], dsem=P.dsem(out=True))

            chk('L1 norm gi=%d' % gi)
            for b in range(2):
                wv = wget(8)
                for fc in range(2):
                    c = 2 * b + fc
                    for (c0, n) in col_tiles(0, T):
                        pa = nbank()
                        pg = nbank()
                        fm_mm(pa[:, 0:n], wv, fc * 128, c0, n)
                        fm_mm(pg[:, 0:n], wv, 256 + fc * 128, c0, n)
                        k = bank_rr["n"] % 2
                        cp("act", gct[:, k, 0:n], pa[:, 0:n])
                        npp = max(0, min(n, TP - c0))
                        if npp > 0:
                            tt("dve", cxp[:, c, 2 + c0:2 + c0 + npp], pg[:, 0:npp], gct[:, k, 0:npp], ALU.mult)
                        if npp < n:
                            tt("dve", cxs[:, c, :, 2:10], pg[:, npp:n].rearrange("p (s t) -> p s t", t=8),
                               gct[:, k, npp:n].rearrange("p (s t) -> p s t", t=8), ALU.mult)
                if gi == 1:
                    for lt, dst in ((npt - 1, "p"), (npt, "s")):
                        pa = nbank()
                        tm_mm(pa[:], wv, 0, 512, lt)
                        si = stage_slot()
                        cp("act", stage[:, si, 0:256], pa[:, 0:256])
                        tt("dve", stage[:, si, 256:512], pa[:, 256:512], stage[:, si, 0:256], ALU.mult)
                        if dst == "p":
                            dma("sp", sc_p[:, b * 256:(b + 1) * 256], stage[126:128, si, 256:512], osem[si], rd=[stage[:, si, 256:512]])
                        else:
                            for s_ in range(NSEQ):
                                dma("sp", sc_s[s_, :, b * 256:(b + 1) * 256], stage[8 * s_ + 6:8 * s_ + 8, si, 256:512], osem[si],
                                    rd=[stage[:, si, 256:512]])
            for b in range(2):
                wv = wget(8)
                for fc in range(2):
                    c = 2 * b + fc
                    for (c0, n) in col_tiles(0, T):
                        pa = nbank()
                        pu = nbank()
                        fm_mm(pa[:, 0:n], wv, fc * 128, c0, n)
                        fm_mm(pu[:, 0:n], wv, 256 + fc * 128, c0, n)
                        cp("act", gbb[:, c, c0:c0 + n], pa[:, 0:n])
                        npp = max(0, min(n, TP - c0))
                        if npp > 0:
                            cp("dve", upx[:, c, 15 + c0:15 + c0 + npp], pu[:, 0:npp])
                        if npp < n:
                            cp("dve", usx[:, c, :, 15:23], pu[:, npp:n].rearrange("p (s t) -> p s t", t=8))
                if gi == 1:
                    for lt, dst in ((npt - 1, "p"), (npt, "s")):
                        pa = nbank()
                        tm_mm(pa[:, 0:256], wv, 256, 256, lt)
                        si = stage_slot()
                        cp("dve", stage[:, si, 0:256], pa[:, 0:256])
                        if dst == "p":
                            dma("sp", pool_p[:, b * 256:(b + 1) * 256], stage[113:128, si, 0:256], osem[si], rd=[stage[:, si, 0:256]])
                        else:
                            for s_ in range(NSEQ):
                                dma("sp", pool_s[s_, 7:15, b * 256:(b + 1) * 256], stage[8 * s_:8 * s_ + 8, si, 0:256], osem[si],
                                    rd=[stage[:, si, 0:256]])
            if gi == 0:
                cp("pool", hist_c[:], cxp[:, :, TP:TP + 2])
                cp("pool", hist_u[:], upx[:, :, TP:TP + 15])

            chk('L1 proj gi=%d' % gi)
            for c in range(4):
                a_ = acc[:, 0:TP]
                ts("dve", a_, cxp[:, c, 0:TP], chv[:, c, 34:35], ALU.mult)
                stt("dve", a_, cxp[:, c, 1:1 + TP], chv[:, c, 35:36], a_, ALU.mult, ALU.add)
                stt("dve", a_, cxp[:, c, 2:2 + TP], chv[:, c, 36:37], a_, ALU.mult, ALU.add)
                tt("pool", cat[:, c, 0:TP], a_, gbb[:, c, 0:TP], ALU.mult)
                if G["sample"]:
                    a3 = acc2[:, 0:128].rearrange("p (s t) -> p s t", t=8)
                    ts("dve", a3, cxs[:, c, :, 0:8], chv[:, c, 34:35], ALU.mult)
                    stt("dve", a3, cxs[:, c, :, 1:9], chv[:, c, 35:36], a3, ALU.mult, ALU.add)
                    stt("dve", a3, cxs[:, c, :, 2:10], chv[:, c, 36:37], a3, ALU.mult, ALU.add)
                    tt("pool", cat[:, c, TP:T], acc2[:, 0:128], gbb[:, c, TP:T], ALU.mult)
            for g, w in enumerate((2, 4, 8, 16)):
                L = 15 + TP
                src = upx[:, g, 0:L]
                cur, nxt = acc[:, 0:L], acc2[:, 0:L]
                sh = 1
                first = True
                while sh < w:
                    a_in = src if first else cur
                    eng = "dve" if (sh in (1, 4)) else "pool"
                    if not first:
                        cp(eng, nxt[:, 0:sh], a_in[:, 0:sh])
                    tt(eng, nxt[:, sh:L], a_in[:, sh:L], a_in[:, 0:L - sh], ALU.add)
                    cur, nxt = nxt, cur
                    first = False
                    sh *= 2
                stt("dve", pooled[:, 0:TP], cur[:, 15:15 + TP], 1.0 / w, upx[:, g, 15:15 + TP], ALU.mult, ALU.subtract)
                if gi == 0:
                    tt("dve", nxt[:, 0:16], cur[:, 15:31], rc[:, g, :], ALU.mult)
                    tt("dve", pooled[:, 0:16], nxt[:, 0:16], upx[:, g, 15:31], ALU.subtract)
                if G["sample"]:
                    s3 = usx[:, g, :, :]
                    c3 = acc[:, 0:16 * 23].rearrange("p (s t) -> p s t", t=23)
                    n3 = acc2[:, 0:16 * 23].rearrange("p (s t) -> p s t", t=23)
                    sh = 1
                    first = True
                    while sh < w:
                        a_in = s3 if first else c3
                        if not first:
                            cp("pool", n3[:, :, 0:sh], a_in[:, :, 0:sh])
                        tt("pool", n3[:, :, sh:23], a_in[:, :, sh:23], a_in[:, :, 0:23 - sh], ALU.add)
                        c3, n3 = n3, c3
                        first = False
                        sh *= 2
                    stt("dve", pooled[:, TP:T].rearrange("p (s t) -> p s t", t=8), c3[:, :, 15:23], 1.0 / w, usx[:, g, :, 15:23],
                        ALU.mult, ALU.subtract)
                for (c0, n) in col_tiles(0, T):
                    pb = nbank()
                    mm(pb[:, 0:n], poolw[:, g, :], pooled[:, c0:c0 + n], True, True)
                    act(cat[:, 4 + g, c0:c0 + n], pb[:, 0:n], AF.Copy, scale=chv[:, g, 37:38])

            residual_proj(8, cat)
            ffn(1)

            chk('L1 mixer+ffn gi=%d' % gi)
            gslot = load_g(4)
            c0 = stat_col(3 * ntl)
            for lt in range(ntl):
                act(xn[:, lt % 2, :], h[:, lt, :], AF.Square, accum=stats[:, c0 + lt:c0 + lt + 1])
            act(stats[:, c0 + ntl:c0 + 2 * ntl], stats[:, c0:c0 + ntl], AF.Ln, bias=epst[:, 0:1], scale=1.0 / D)
            act(stats[:, c0 + 2 * ntl:c0 + 3 * ntl], stats[:, c0 + ntl:c0 + 2 * ntl], AF.Exp, scale=-0.5)
            for lt in range(ntl):
                hv = h[:, lt, :]
                si = stage_slot()
                stt("dve", stage[:, si, :], hv, stats[:, c0 + 2 * ntl + lt:c0 + 2 * ntl + lt + 1], gb[:, gslot, :], ALU.mult, ALU.mult)
                dst = yp[ptiles[lt] * 128:(ptiles[lt] + 1) * 128, :] if lt < npt else ys
                dma("sp", dst, stage[:, si, :], osem[si], rd=[stage[:, si, :]])

    try:
        chk('consts')
        run_groups()
    except _Stop as e_:
        print('STOPPED at', e_)
    if limit is None:
        assert wstate["cur"] == len(WSPECS), (wstate, len(WSPECS))
    P.emit()
    es.close()
    return nc


def make_in_maps(inp, compact_cache=False):
    f = lambda a: np.ascontiguousarray(np.asarray(a))
    gvec = f(np.stack([inp["norm_mix_g"][0], inp["norm_ffn_g"][0], inp["norm_mix_g"][1], inp["norm_ffn_g"][1],
                       inp["norm_final_g"]]))
    chv = f(np.concatenate([inp["conv_a_w"][0], inp["conv_a_b"], inp["ln_a_g"], inp["ln_a_b"], inp["conv_c_w"][0],
                            inp["pool_scale"]], axis=0))
    shared = {
        "gvec": gvec, "chv": chv, "sbb": f(inp["sb_bias"]).reshape(1, 8),
        "poolw": f(inp["pool_w"]).reshape(512, 128),
        "w_in_e": f(inp["w_in_even"][0]), "w_out_e": f(inp["w_out_even"][0]),
        "w_in_o": f(inp["w_in_odd"][0]), "w_out_o": f(inp["w_out_odd"][0]),
        "w_ffn_in": f(inp["w_ffn_in"]).reshape(2 * D, 2 * FFN), "w_ffn_out": f(inp["w_ffn_out"]).reshape(2 * FFN, D),
    }
    ck_full = np.asarray(inp["cache_k"])[0].reshape(-1, 512)
    cv_full = np.asarray(inp["cache_v"])[0].reshape(-1, 512)
    maps = []
    for c in range(NCORES):
        sl = slice(NSEQ * c, NSEQ * (c + 1))
        m = dict(shared)
        m["xp"] = f(inp["x_prompt"][c])
        m["xs"] = f(inp["x_sample"][sl]).reshape(128, D)
        m["st_conf"] = f(inp["state_conformer"][0, sl]).reshape(NSEQ * 30, 512)
        m["st_sc"] = f(inp["state_shortconv"][0, sl]).reshape(NSEQ * 2, 512)
        m["st_pool"] = f(inp["state_pool"][0, sl]).reshape(NSEQ * 15, 512)
        ptc = np.asarray(inp["page_table"])[sl].astype(np.int32)
        if compact_cache:
            pages = ptc.reshape(-1)
            m["ck"] = f(ck_full.reshape(-1, 128, 512)[pages]).reshape(-1, 512)
            m["cv"] = f(cv_full.reshape(-1, 128, 512)[pages]).reshape(-1, 512)
            m["pt"] = np.arange(256, dtype=np.int32).reshape(NSEQ, NPAGE)
        else:
            m["ck"] = ck_full
            m["cv"] = cv_full
            m["pt"] = f(ptc)
        maps.append(m)
    return maps


def gather_outputs(res):
    R = res.results
    cat = lambda k: np.stack([R[c][k] for c in range(NCORES)])
    y_p = cat("yp")
    y_s = np.concatenate([R[c]["ys"].reshape(NSEQ, DSEQ, D) for c in range(NCORES)])
    k_p = cat("kp").reshape(1, NCORES, SEQ, 8, 64)
    v_p = cat("vp").reshape(1, NCORES, SEQ, 8, 64)
    k_s = np.concatenate([R[c]["ks"].reshape(NSEQ, DSEQ, 8, 64) for c in range(NCORES)])[None]
    v_s = np.concatenate([R[c]["vs"].reshape(NSEQ, DSEQ, 8, 64) for c in range(NCORES)])[None]
    conv_p = cat("conv_p")[None]
    conv_s = np.concatenate([R[c]["conv_s"] for c in range(NCORES)])[None]
    sc_p = cat("sc_p")[None]
    sc_s = np.concatenate([R[c]["sc_s"] for c in range(NCORES)])[None]
    pool_p = cat("pool_p")[None]
    pool_s = np.concatenate([R[c]["pool_s"] for c in range(NCORES)])[None]
    outs = (y_p, y_s, k_p, v_p, k_s, v_s, conv_p, conv_s, sc_p, sc_s, pool_p, pool_s)
    return tuple(np.ascontiguousarray(o, dtype=np.float32) for o in outs)


def kernel(**inputs):
    npool = int(np.asarray(inputs["cache_k"]).shape[1])
    nc = build_nc(npool)
    in_maps = make_in_maps(inputs)
    res = run_bass_kernel_spmd(nc, in_maps, core_ids=list(range(NCORES)))
    return gather_outputs(res)
```

```python
import numpy as np
from contextlib import ExitStack
import concourse.bass as bass
import concourse.mybir as mybir
from concourse.bass_utils import run_bass_kernel_spmd

F32, BF16, I32 = mybir.dt.float32, mybir.dt.bfloat16, mybir.dt.int32
AF = mybir.ActivationFunctionType
ALU = mybir.AluOpType
ESZ = {F32: 4, BF16: 2, I32: 4}

NCORES = 8
D = 1024
SEQ = 2048
NSEQ = 16
DSEQ = 8
NPAGE = 16
FFN = 2816
NEG = -240000.0
EPS = 1e-6


class V:
    __slots__ = ("ap", "key", "lo", "hi")

    def __init__(self, ap):
        self.ap = ap
        dims = ap.ap
        pstep = dims[0][0]
        off = ap.offset % pstep if pstep > 0 else ap.offset
        ext = 1
        for st, cnt in dims[1:]:
            ext += (cnt - 1) * abs(st)
        es = ESZ[ap.dtype]
        self.key = ap.tensor.name
        self.lo = off * es
        self.hi = (off + ext) * es
        if str(ap.space) == "PSUM":
            self.lo, self.hi = 0, 2048


class DSem:
    def __init__(self, sem, group=False):
        self.sem = sem
        self.count = 0
        self.group = group


class Op:
    __slots__ = ("eng", "fn", "deps", "sig", "cnt", "isdma", "dsem", "dval", "idx")


class Prog:
    ENGS = ("pe", "act", "dve", "pool", "sp")

    def __init__(self, nc, es):
        self.nc = nc
        self.es = es
        self.q = {e: [] for e in self.ENGS}
        self.recs = {}
        self.esem = {e: es.enter_context(nc.semaphore("sem_" + e)) for e in ("pe", "act", "dve", "pool")}
        self.nsem = 0
        self.out_dsems = []

    def dsem(self, group=False, out=False):
        self.nsem += 1
        d = DSem(self.es.enter_context(self.nc.semaphore("dsem%d" % self.nsem)), group)
        self.out_dsems.append(d)
        return d

    def _access(self, v, op, is_write):
        L = self.recs.get(v.key)
        if L is None:
            L = self.recs[v.key] = []
        raw, other = [], []
        newL = []
        lo, hi = v.lo, v.hi
        cov = []
        for rec in L:
            rlo, rhi, w, rd = rec
            if rhi <= lo or rlo >= hi:
                newL.append(rec)
                continue
            if w is not None:
                (other if is_write else raw).append(w)
            if is_write:
                other.extend(rd.values())
            if rlo < lo:
                newL.append([rlo, lo, w, dict(rd)])
            if rhi > hi:
                newL.append([hi, rhi, w, dict(rd)])
            if not is_write:
                nrd = dict(rd)
                nrd[("d", id(op)) if op.isdma else op.eng] = op
                a, b = max(rlo, lo), min(rhi, hi)
                newL.append([a, b, w, nrd])
                cov.append((a, b))
        if is_write:
            newL.append([lo, hi, op, {}])
        else:
            cov.sort()
            cur = lo
            k = ("d", id(op)) if op.isdma else op.eng
            for a, b in cov:
                if a > cur:
                    newL.append([cur, a, None, {k: op}])
                cur = max(cur, b)
            if cur < hi:
                newL.append([cur, hi, None, {k: op}])
        self.recs[v.key] = newL
        return raw, other

    def add(self, eng, fn, reads=(), writes=(), dsem=None):
        op = Op()
        op.eng = eng
        op.fn = fn
        op.isdma = dsem is not None
        op.sig = False
        op.cnt = 0
        op.idx = len(self.q[eng])
        raw, other = [], []
        for v in reads:
            r, o = self._access(v, op, v.key.startswith("ps"))
            raw += r
            other += o
        for v in writes:
            r, o = self._access(v, op, True)
            raw += r
            other += o
        deps = {}
        for lst, is_raw in ((raw, True), (other, False)):
            for d in lst:
                if d is op:
                    continue
                if d.isdma:
                    deps[("dsem", id(d.dsem))] = (d, d.dsem.count)
                    continue
                if (not op.isdma) and d.eng == eng:
                    if eng == "pe":
                        continue
                k = d.eng
                if k not in deps or deps[k].idx < d.idx:
                    deps[k] = d
        op.deps = [x if isinstance(x, tuple) else (x, None) for x in deps.values()]
        for d, _ in op.deps:
            if not d.isdma:
                d.sig = True
        if op.isdma:
            dsem.count += 16
            op.dsem = dsem
            op.dval = dsem.count
        self.q[eng].append(op)
        return op

    def emit(self):
        nc = self.nc
        for eng, L in self.q.items():
            c = 0
            for op in L:
                if (not op.isdma) and op.sig:
                    c += 1
                    op.cnt = c
        engobj = {"pe": "tensor", "act": "scalar", "dve": "vector", "pool": "gpsimd", "sp": "sync"}
        with nc.Block() as block:
            for eng in self.ENGS:
                def body(e, eng=eng):
                    waited = {}
                    for op in self.q[eng]:
                        for d, dv in op.deps:
                            if d.isdma:
                                sem = d.dsem.sem
                                val = d.dsem.count if d.dsem.group else dv
                            else:
                                sem = self.esem[d.eng]
                                val = d.cnt
                            if waited.get(sem.num, 0) < val:
                                e.wait_ge(sem, val)
                                waited[sem.num] = val
                        ins = op.fn(e)
                        if op.isdma:
                            ins.then_inc(op.dsem.sem, 16)
                        elif op.sig:
                            ins.then_inc(self.esem[eng], 1)
                    if eng == "sp":
                        for d in self.out_dsems:
                            if d.count > 0:
                                e.wait_ge(d.sem, d.count)
                getattr(block, engobj[eng])(body)


class _Stop(Exception):
    pass


def build_nc(npool, limit=None):
    nc = bass.Bass("TRN2", target_bir_lowering=False)
    es = ExitStack()
    P = Prog(nc, es)

    def din(name, shape, dt=F32):
        return nc.dram_tensor(name, list(shape), dt, kind="ExternalInput").ap()

    def dout(name, shape, dt=F32):
        return nc.dram_tensor(name, list(shape), dt, kind="ExternalOutput").ap()

    xp = din("xp", [SEQ, D])
    xs = din("xs", [128, D])
    ck = din("ck", [npool * 128, 512])
    cv = din("cv", [npool * 128, 512])
    st_conf = din("st_conf", [NSEQ * 30, 512])
    st_sc = din("st_sc", [NSEQ * 2, 512])
    st_pool = din("st_pool", [NSEQ * 15, 512])
    pt = din("pt", [NSEQ, NPAGE], I32)
    gvec = din("gvec", [5, D])
    chvd = din("chv", [38, 512])
    sbb = din("sbb", [1, 8])
    poolwd = din("poolw", [512, 128])
    w_in_e = din("w_in_e", [D, 2560])
    w_out_e = din("w_out_e", [D, D])
    w_in_o = din("w_in_o", [D, 2048])
    w_out_o = din("w_out_o", [D, D])
    w_ffn_in = din("w_ffn_in", [2 * D, 2 * FFN])
    w_ffn_out = din("w_ffn_out", [2 * FFN, D])

    yp = dout("yp", [SEQ, D])
    ys = dout("ys", [128, D])
    kp = dout("kp", [SEQ, 512])
    vp = dout("vp", [SEQ, 512])
    ks = dout("ks", [128, 512])
    vs = dout("vs", [128, 512])
    conv_p = dout("conv_p", [30, 512])
    conv_s = dout("conv_s", [NSEQ, 30, 512])
    sc_p = dout("sc_p", [2, 512])
    sc_s = dout("sc_s", [NSEQ, 2, 512])
    pool_p = dout("pool_p", [15, 512])
    pool_s = dout("pool_s", [NSEQ, 15, 512])

    def sb(name, shape, dt):
        return es.enter_context(nc.sbuf_tensor(name, list(shape), dt))

    def ps(name):
        return es.enter_context(nc.psum_tensor(name, [128, 512], F32))

    TMAX = 1152
    h = sb("h", [128, 9, D], F32)
    actA = sb("actA", [128, 8, TMAX], BF16)
    KT = sb("KT", [128, 4, SEQ + 128], BF16)
    Vall = sb("Vall", [128, 16, 512], BF16)
    wslot = sb("wslot", [128, 2, 6144], BF16)
    gb = sb("gb", [128, 2, D], F32)
    stage = sb("stage", [128, 2, D], F32)
    xn = sb("xn", [128, 2, D], BF16)
    identb = sb("identb", [128, 128], BF16)
    identf = sb("identf", [128, 128], F32)
    tinc = sb("tinc", [128, 128], BF16)
    onesb = sb("onesb", [128, 128], BF16)
    negm = sb("negm", [128, 128], BF16)
    onesf = sb("onesf", [128, 128], F32)
    chv = sb("chvs", [128, 4, 38], F32)
    chraw = sb("chraw", [38, 512], F32)
    sbias = sb("sbias", [128, 8], F32)
    biasd = sb("biasd", [128, 64], F32)
    epst = sb("epst", [128, 1], F32)
    poolw = sb("poolws", [128, 4, 128], BF16)
    rc = sb("rc", [128, 4, 16], F32)
    stats = sb("stats", [128, 64], F32)
    ptb = sb("ptb", [128, NSEQ * NPAGE], I32)
    pidx = sb("pidx", [128, NSEQ * NPAGE], I32)
    iot = sb("iot", [128, 1], F32)
    hist_a = sb("hist_a", [128, 4, 30], BF16)
    hist_c = sb("hist_c", [128, 4, 2], BF16)
    hist_u = sb("hist_u", [128, 4, 15], BF16)
    MIXB = 66 * 1024
    mix = sb("mix", [128, MIXB // 2], BF16)
    banks = [ps("ps%d" % i) for i in range(8)]

    maskN = sb("maskN", [128, 128], BF16)
    ebl = sb("ebl", [16, 128], F32)
    eblb = sb("eblb", [16, 128], BF16)

    def mixv(off_bytes, nelem, dt):
        assert off_bytes % 4 == 0
        assert off_bytes + nelem * ESZ[dt] <= MIXB, (off_bytes, nelem, MIXB)
        a = mix[:, off_bytes // 2: off_bytes // 2 + nelem * ESZ[dt] // 2]
        return a if dt == BF16 else a.bitcast(dt)

    def rv(x):
        return x if isinstance(x, V) else V(x)

    def act(out, in_, func, bias=0.0, scale=1.0, accum=None):
        out, in_ = rv(out), rv(in_)
        rd = [in_]
        wr = [out]
        kw = {}
        if hasattr(bias, "ap"):
            bias = rv(bias)
            rd.append(bias)
            kw["bias"] = bias.ap
        else:
            kw["bias"] = float(bias)
        if hasattr(scale, "ap"):
            scale = rv(scale)
            rd.append(scale)
            kw["scale"] = scale.ap
        else:
            kw["scale"] = float(scale)
        if accum is not None:
            accum = rv(accum)
            wr.append(accum)
            kw["accum_out"] = accum.ap
        return P.add("act", lambda e: e.activation(out=out.ap, in_=in_.ap, func=func, **kw), rd, wr)

    def tt(eng, out, in0, in1, op):
        out, in0, in1 = rv(out), rv(in0), rv(in1)
        return P.add(eng, lambda e: e.tensor_tensor(out=out.ap, in0=in0.ap, in1=in1.ap, op=op), [in0, in1], [out])

    def ts(eng, out, in0, s1, op0, s2=None, op1=None):
        out, in0 = rv(out), rv(in0)
        rd = [in0]
        a1 = s1
        if hasattr(s1, "ap"):
            s1 = rv(s1)
            rd.append(s1)
            a1 = s1.ap
        a2 = s2
        if s2 is not None and hasattr(s2, "ap"):
            s2 = rv(s2)
            rd.append(s2)
            a2 = s2.ap
        if op1 is None:
            return P.add(eng, lambda e: e.tensor_scalar(out=out.ap, in0=in0.ap, scalar1=a1, scalar2=None, op0=op0), rd, [out])
        return P.add(eng, lambda e: e.tensor_scalar(out=out.ap, in0=in0.ap, scalar1=a1, scalar2=a2, op0=op0, op1=op1), rd, [out])

    def stt(eng, out, in0, s, in1, op0, op1):
        out, in0, in1 = rv(out), rv(in0), rv(in1)
        rd = [in0, in1]
        a = s
        if hasattr(s, "ap"):
            s = rv(s)
            rd.append(s)
            a = s.ap
        return P.add(eng, lambda e: e.scalar_tensor_tensor(out=out.ap, in0=in0.ap, scalar=a, in1=in1.ap, op0=op0, op1=op1), rd, [out])

    def cp(eng, out, in_):
        out, in_ = rv(out), rv(in_)
        if eng == "act":
            return P.add("act", lambda e: e.copy(out=out.ap, in_=in_.ap), [in_], [out])
        return P.add(eng, lambda e: e.tensor_copy(out=out.ap, in_=in_.ap), [in_], [out])

    def mset(eng, out, val):
        out = rv(out)
        return P.add(eng, lambda e: e.memset(out.ap, val), [], [out])

    def asel(out, in_, pattern, op, base, cm):
        out, in_ = rv(out), rv(in_)
        return P.add("pool", lambda e: e.affine_select(out=out.ap, in_=in_.ap, pattern=pattern, compare_op=op, fill=0.0,
                                                       base=base, channel_multiplier=cm), [in_], [out])

    def mm(out, lhsT, rhs, start, stop, tp=None, sgc=False):
        out, lhsT, rhs = rv(out), rv(lhsT), rv(rhs)
        kw = {}
        if tp is not None:
            kw["tile_position"] = tp
        if sgc:
            kw["skip_group_check"] = True
        rd = [lhsT, rhs] + ([] if start else [out])
        return P.add("pe", lambda e: e.matmul(out.ap, lhsT=lhsT.ap, rhs=rhs.ap, start=start, stop=stop, **kw), rd, [out])

    def tr(out, in_, ident):
        out, in_, ident = rv(out), rv(in_), rv(ident)
        return P.add("pe", lambda e: e.transpose(out.ap, in_.ap, ident.ap), [in_, ident], [out])

    def dma(q, out, in_, dsem, rd=(), wr=()):
        return P.add(q, lambda e: e.dma_start(out=out, in_=in_), [rv(x) for x in rd], [rv(x) for x in wr], dsem=dsem)

    ckstate = {"n": 0}

    def chk(name):
        ckstate["n"] += 1
        if limit is not None and ckstate["n"] > limit:
            raise _Stop(name)

    cdsem = P.dsem(group=True)

    def cload(out_ap, in_ap):
        dma("sp", out_ap, in_ap, cdsem, wr=[out_ap])

    def lload(out_ap, in_ap):
        dma("sp", out_ap, in_ap, P.dsem(), wr=[out_ap])

    cload(chraw[0:38, :], chvd)
    cload(sbias[:], sbb.rearrange("a b -> (a b)").partition_broadcast(128))
    cload(ptb[:], pt.rearrange("a b -> (a b)").partition_broadcast(128))
    pw32 = mixv(0, 512, F32).rearrange("p (g d) -> p g d", g=4)
    cload(pw32, poolwd.rearrange("(g c) d -> c g d", c=128))
    cp("dve", poolw[:], pw32)

    mset("pool", onesf[:], 1.0)
    mset("pool", epst[:], EPS)
    asel(identf[:], onesf[:], [[-1, 128]], ALU.is_equal, 0, 1)
    tincf = mixv(4096, 128, F32)
    asel(tincf, onesf[:], [[-1, 128]], ALU.is_ge, 0, 1)
    cp("pool", identb[:], identf[:])
    cp("pool", tinc[:], tincf)
    cp("pool", onesb[:], onesf[:])
    ts("pool", negm[:], tincf, NEG, ALU.mult)
    P.add("pool", lambda e: e.iota(iot[:], [[0, 1]], base=0, channel_multiplier=1, allow_small_or_imprecise_dtypes=True),
          [], [V(iot[:])])
    ts("dve", pidx[:], ptb[:], 128.0, ALU.mult, iot[:, 0:1], ALU.add)
    cp("dve", biasd[:].rearrange("p (h q) -> p h q", q=8), sbias[:].unsqueeze(2).broadcast_to([128, 8, 8]))
    for c in range(4):
        tr(banks[0][0:128, c * 64: c * 64 + 38], chraw[0:38, c * 128:(c + 1) * 128], identf[0:38, 0:38])
    cp("dve", chv[:], banks[0][:, 0:256].rearrange("p (c r) -> p c r", c=4)[:, :, 0:38])
    rcf = mixv(8192, 16, F32)
    P.add("pool", lambda e: e.iota(rcf, [[1, 16]], base=1, channel_multiplier=0, allow_small_or_imprecise_dtypes=True),
          [], [V(rcf)])
    for g, w in enumerate((2, 4, 8, 16)):
        ts("dve", rc[:, g, :], rcf, float(w), ALU.min)
    P.add("dve", lambda e: e.reciprocal(out=rc[:], in_=rc[:]), [V(rc[:])], [V(rc[:])])
    mset("pool", ebl[:], 1.0)
    asel(ebl[:], ebl[:], [[1, 128]], ALU.is_ge, 0, -8)
    asel(ebl[:], ebl[:], [[-1, 128]], ALU.is_ge, 7, 8)
    cp("pool", eblb[:], ebl[:])
    mm(banks[1][:, 0:128], eblb[0:16, :], eblb[0:16, :], True, True)
    blk = mixv(12288, 128, F32)
    cp("dve", blk, banks[1][:, 0:128])
    asel(blk, blk, [[1, 128]], ALU.is_gt, 0, -1)
    ts("dve", maskN[:], blk, -NEG, ALU.mult, NEG, ALU.add)

    wsems = [[P.dsem(), P.dsem()], [P.dsem(), P.dsem()]]
    WSPECS = []

    def wspec_layer0():
        for b in range(2):
            WSPECS.append([(w_in_e[:, b * 256:(b + 1) * 256], 8, 0, 256, 512),
                           (w_in_e[:, 512 + b * 256:512 + (b + 1) * 256], 8, 256, 256, 512)])
        for q0 in (1024, 1536, 2048):
            WSPECS.append([(w_in_e[:, q0:q0 + 512], 8, 0, 512, 512)])

    def wspec_out(w):
        for ch in range(2):
            WSPECS.append([(w[:, ch * 512:(ch + 1) * 512], 8, 0, 512, 512)])

    def wspec_ffn(layer):
        wi = w_ffn_in[layer * D:(layer + 1) * D, :]
        wo = w_ffn_out[layer * FFN:(layer + 1) * FFN, :]
        for (b0, b1) in ((0, 6), (6, 11)):
            for b in range(b0, b1):
                WSPECS.append([(wi[:, b * 256:(b + 1) * 256], 8, 0, 256, 512),
                               (wi[:, FFN + b * 256:FFN + (b + 1) * 256], 8, 256, 256, 512)])
            nch = 2 * (b1 - b0)
            for ch in range(2):
                WSPECS.append([(wo[b0 * 256:b1 * 256, ch * 512:(ch + 1) * 512], nch, 0, 512, 512)])

    def wspec_layer1():
        for b in range(2):
            WSPECS.append([(w_in_o[:, 512 + b * 256:512 + (b + 1) * 256], 8, 0, 256, 512),
                           (w_in_o[:, 1024 + b * 256:1024 + (b + 1) * 256], 8, 256, 256, 512)])
        for b in range(2):
            WSPECS.append([(w_in_o[:, b * 256:(b + 1) * 256], 8, 0, 256, 512),
                           (w_in_o[:, 1536 + b * 256:1536 + (b + 1) * 256], 8, 256, 256, 512)])

    for _g in range(2):
        wspec_layer0()
        wspec_out(w_out_e)
        wspec_ffn(0)
        wspec_layer1()
        wspec_out(w_out_o)
        wspec_ffn(1)
    wstate = {"cur": 0, "issued": 0}

    def wissue(i):
        s = i % 2
        for pi, (src, nk, c0, ncols, stride) in enumerate(WSPECS[i]):
            dst = wslot[:, s, 0:nk * stride].rearrange("p (k n) -> p k n", k=nk)[:, :, c0:c0 + ncols]
            srcv = src.rearrange("(k p) n -> p k n", p=128)
            P.add("pool", lambda e, dst=dst, srcv=srcv: e.dma_start(out=dst, in_=srcv), [], [V(dst)], dsem=wsems[s][pi])

    def wget(nk, stride=512):
        i = wstate["cur"]
        wstate["cur"] += 1
        while wstate["issued"] <= min(i + 1, len(WSPECS) - 1):
            wissue(wstate["issued"])
            wstate["issued"] += 1
        assert WSPECS[i][0][1] == nk, (i, nk, WSPECS[i][0][1])
        return wslot[:, i % 2, 0:nk * stride].rearrange("p (k n) -> p k n", k=nk)

    groups = [
        dict(ptiles=list(range(0, 9)), sample=False),
        dict(ptiles=list(range(9, 16)), sample=True),
    ]
    xsems = [P.dsem() for _ in range(9)]
    gsem = [P.dsem(), P.dsem()]
    osem = [P.dsem(out=True), P.dsem(out=True)]
    ostate = {"n": 0}

    def stage_slot():
        i = ostate["n"] % 2
        ostate["n"] += 1
        return i

    gstate = {"n": 0}

    def load_g(row):
        i = gstate["n"] % 2
        gstate["n"] += 1
        dma("sp", gb[:, i, :], gvec[row].partition_broadcast(128), gsem[i], wr=[gb[:, i, :]])
        return i

    statc = {"n": 0}

    def stat_col(n=3):
        c = statc["n"]
        if c + n > 64:
            c = 0
        statc["n"] = c + n
        return c

    def col_tiles(lo, hi, step=512):
        r = []
        c = lo
        while c < hi:
            r.append((c, min(step, hi - c)))
            c += step
        return r

    bank_rr = {"n": 0}

    def nbank(lo=2, hi=8):
        b = lo + bank_rr["n"] % (hi - lo)
        bank_rr["n"] += 1
        return banks[b]

    def run_groups():
        for gi, G in enumerate(groups):
            ptiles = G["ptiles"]
            npt = len(ptiles)
            ntl = npt + (1 if G["sample"] else 0)
            TP = npt * 128
            T = ntl * 128
            first_tok = ptiles[0] * 128
            hnT = actA
            cat = actA

            for lt, gt in enumerate(ptiles):
                dma("sp", h[:, lt, :], xp[gt * 128:(gt + 1) * 128, :], xsems[lt], wr=[h[:, lt, :]])
            if G["sample"]:
                dma("sp", h[:, npt, :], xs, xsems[npt], wr=[h[:, npt, :]])

            def norm_T(grow):
                gslot = load_g(grow)
                c0 = stat_col(3 * ntl)
                for lt in range(ntl):
                    act(xn[:, lt % 2, :], h[:, lt, :], AF.Square, accum=stats[:, c0 + lt:c0 + lt + 1])
                act(stats[:, c0 + ntl:c0 + 2 * ntl], stats[:, c0:c0 + ntl], AF.Ln, bias=epst[:, 0:1], scale=1.0 / D)
                act(stats[:, c0 + 2 * ntl:c0 + 3 * ntl], stats[:, c0 + ntl:c0 + 2 * ntl], AF.Exp, scale=-0.5)
                for lt in range(ntl):
                    hv = h[:, lt, :]
                    xs_ = lt % 2
                    stt("dve", xn[:, xs_, :], hv, stats[:, c0 + 2 * ntl + lt:c0 + 2 * ntl + lt + 1], gb[:, gslot, :], ALU.mult, ALU.mult)
                    pb = banks[lt % 2]
                    pbv = pb[:].bitcast(BF16)
                    for kc in range(8):
                        tr(pbv[:, kc * 128:(kc + 1) * 128], xn[:, xs_, kc * 128:(kc + 1) * 128], identb[:])
                    cp("dve" if lt % 2 == 0 else "act", hnT[:, :, lt * 128:(lt + 1) * 128],
                       pbv.rearrange("p (k n) -> p k n", k=8))

            def fm_mm(psv, wv, wc0, c0, n):
                for kc in range(8):
                    mm(psv, wv[:, kc, wc0:wc0 + 128], hnT[:, kc, c0:c0 + n], kc == 0, kc == 7)

            def tm_mm(psv, wv, wc0, ncols, lt):
                for kc in range(8):
                    mm(psv, hnT[:, kc, lt * 128:(lt + 1) * 128], wv[:, kc, wc0:wc0 + ncols], kc == 0, kc == 7)

            def residual_proj(nk_total, src):
                for ch in range(2):
                    wv = wget(nk_total)
                    for lt in range(ntl):
                        pb = nbank()
                        for kc in range(nk_total):
                            mm(pb[:], src[:, kc, lt * 128:(lt + 1) * 128], wv[:, kc, :], kc == 0, kc == nk_total - 1)
                        tt("dve", h[:, lt, ch * 512:(ch + 1) * 512], h[:, lt, ch * 512:(ch + 1) * 512], pb[:], ALU.add)

            def ffn(layer):
                hid = mixv(0, 12 * T, BF16).rearrange("p (c t) -> p c t", c=12)
                sg = mixv(12 * T * 2, 2 * 512, BF16).rearrange("p (a n) -> p a n", a=2)
                norm_T(1 + 2 * layer)
                for half, (b0, b1) in enumerate(((0, 6), (6, 11))):
                    nch = 2 * (b1 - b0)
                    for b in range(b0, b1):
                        wv = wget(8)
                        for fc in range(2):
                            lc = 2 * (b - b0) + fc
                            for (c0, n) in col_tiles(0, T):
                                pg = nbank()
                                pu = nbank()
                                fm_mm(pg[:, 0:n], wv, fc * 128, c0, n)
                                fm_mm(pu[:, 0:n], wv, 256 + fc * 128, c0, n)
                                k = bank_rr["n"] % 2
                                act(sg[:, k, 0:n], pg[:, 0:n], AF.Silu)
                                tt("dve", hid[:, lc, c0:c0 + n], pu[:, 0:n], sg[:, k, 0:n], ALU.mult)
                    residual_proj(nch, hid)

            norm_T(0)
            o = 0
            QT = mixv(o, 4 * T, BF16).rearrange("p (c t) -> p c t", c=4); o += 4 * T * 2
            sgt = mixv(o, 2 * 512, F32).rearrange("p (a n) -> p a n", a=2)
            vs_tok = mixv(o, 512, BF16)
            lntmp = mixv(o + 1024, 3 * 512, BF16).rearrange("p (a n) -> p a n", a=3)
            o += 4096
            R1 = o
            extp = mixv(o, 4 * (30 + TP), BF16).rearrange("p (c t) -> p c t", c=4); o += 4 * (30 + TP) * 2
            if G["sample"]:
                exts = mixv(o, 4 * 16 * 38, BF16).rearrange("p (c s t) -> p c s t", c=4, s=16); o += 4 * 16 * 38 * 2
            diag2 = mixv(o, 2 * 31 * 128, BF16).rearrange("p (d j n) -> p d j n", d=2, j=31); o += 2 * 31 * 128 * 2
            a16o = o
            a16 = mixv(o, 4 * T, BF16).rearrange("p (c t) -> p c t", c=4); o += 4 * T * 2
            a2 = mixv(o, 4 * T, BF16).rearrange("p (c t) -> p c t", c=4); o += 4 * T * 2
            lnt = mixv(o, 4 * 512, F32).rearrange("p (a n) -> p a n", a=4); o += 4 * 2048
            assert o <= MIXB, o

            if gi == 0:
                mset("pool", extp[:, :, 0:30], 0.0)
            else:
                cp("pool", extp[:, :, 0:30], hist_a[:])
                stt_ = mixv(a16o, 4 * 512, F32).rearrange("p (a n) -> p a n", a=4)
                for i in range(4):
                    lload(stt_[0:120, i, :], st_conf[i * 120:(i + 1) * 120, :])
                for i in range(4):
                    pb = nbank()
                    for c in range(4):
                        tr(pb[:, c * 128:c * 128 + 120], stt_[0:120, i, c * 128:(c + 1) * 128], identf[0:120, 0:120])
                    cp("dve", exts[:, :, 4 * i:4 * i + 4, 0:30],
                       pb[:].rearrange("p (c x) -> p c x", c=4)[:, :, 0:120].rearrange("p c (s r) -> p c s r", s=4))
                P.add("sp", lambda e: e.dma_start(out=conv_s[:, 0:22, :], in_=st_conf.rearrange("(s r) f -> s r f", r=30)[:, 8:30, :]),
                      [], [], dsem=P.dsem(out=True))

            chk('L0 norm gi=%d' % gi)
            for b in range(2):
                wv = wget(8)
                for fc in range(2):
                    c = 2 * b + fc
                    for (c0, n) in col_tiles(0, T):
                        pa = nbank()
                        pg = nbank()
                        fm_mm(pa[:, 0:n], wv, fc * 128, c0, n)
                        fm_mm(pg[:, 0:n], wv, 256 + fc * 128, c0, n)
                        k = bank_rr["n"] % 2
                        act(sgt[:, k, 0:n], pg[:, 0:n], AF.Sigmoid)
                        npp = max(0, min(n, TP - c0))
                        if npp > 0:
                            tt("dve", extp[:, c, 30 + c0:30 + c0 + npp], pa[:, 0:npp], sgt[:, k, 0:npp], ALU.mult)
                        if npp < n:
                            assert n - npp == 128
                            tt("dve", exts[:, c, :, 30:38], pa[:, npp:n].rearrange("p (s t) -> p s t", t=8),
                               sgt[:, k, npp:n].rearrange("p (s t) -> p s t", t=8), ALU.mult)
                if gi == 1:
                    for lt, dst in ((npt - 1, "p"), (npt, "s")):
                        pa = nbank()
                        tm_mm(pa[:], wv, 0, 512, lt)
                        si = stage_slot()
                        act(stage[:, si, 0:256], pa[:, 256:512], AF.Sigmoid)
                        tt("dve", stage[:, si, 256:512], pa[:, 0:256], stage[:, si, 0:256], ALU.mult)
                        if dst == "p":
                            dma("sp", conv_p[:, b * 256:(b + 1) * 256], stage[98:128, si, 256:512], osem[si], rd=[stage[:, si, 256:512]])
                        else:
                            for s_ in range(NSEQ):
                                dma("sp", conv_s[s_, 22:30, b * 256:(b + 1) * 256], stage[8 * s_:8 * s_ + 8, si, 256:512], osem[si],
                                    rd=[stage[:, si, 256:512]])
            chk('L0 glu gi=%d' % gi)
            wv = wget(8)
            for c in range(4):
                for (c0, n) in col_tiles(0, T):
                    pb = nbank()
                    fm_mm(pb[:, 0:n], wv, c * 128, c0, n)
                    cp("act", QT[:, c, c0:c0 + n], pb[:, 0:n])
            wv = wget(8)
            for c in range(4):
                for (c0, n) in col_tiles(0, T):
                    pb = nbank()
                    fm_mm(pb[:, 0:n], wv, c * 128, c0, n)
                    npp = max(0, min(n, TP - c0))
                    if npp > 0:
                        cp("act", KT[:, c, first_tok + c0:first_tok + c0 + npp], pb[:, 0:npp])
                    if npp < n:
                        cp("act", KT[:, c, SEQ:SEQ + 128], pb[:, npp:n])
            for lt in range(ntl):
                pb = nbank()
                tm_mm(pb[:], wv, 0, 512, lt)
                si = stage_slot()
                cp("dve", stage[:, si, 0:512], pb[:])
                dst = kp[ptiles[lt] * 128:(ptiles[lt] + 1) * 128, :] if lt < npt else ks
                dma("sp", dst, stage[:, si, 0:512], osem[si], rd=[stage[:, si, 0:512]])
            wv = wget(8)
            for lt in range(ntl):
                pb = nbank()
                tm_mm(pb[:], wv, 0, 512, lt)
                si = stage_slot()
                cp("dve", stage[:, si, 0:512], pb[:])
                if lt < npt:
                    cp("act", Vall[:, ptiles[lt], :], pb[:])
                    dst = vp[ptiles[lt] * 128:(ptiles[lt] + 1) * 128, :]
                else:
                    cp("act", vs_tok, pb[:])
                    dst = vs
                dma("sp", dst, stage[:, si, 0:512], osem[si], rd=[stage[:, si, 0:512]])

            chk('L0 qkv gi=%d' % gi)
            for c in range(4):
                diag = diag2[:, c % 2]
                for j in range(31):
                    ts("dve", diag[:, j, :], identb[:], chv[:, c, j:j + 1], ALU.mult)
                for (c0, n) in col_tiles(0, TP):
                    pb = nbank()
                    for j in range(31):
                        mm(pb[:, 0:n], diag[:, j, :], extp[:, c, c0 + j:c0 + j + n], j == 0, j == 30)
                    act(a16[:, c, c0:c0 + n], pb[:, 0:n], AF.Identity, bias=chv[:, c, 31:32])
                    act(a2[:, c, c0:c0 + n], pb[:, 0:n], AF.Square, bias=chv[:, c, 31:32])
                if G["sample"]:
                    pb = nbank()
                    for j in range(31):
                        mm(pb[:, 0:128].rearrange("p (s t) -> p s t", t=8), diag[:, j, :], exts[:, c, :, j:j + 8], j == 0, j == 30)
                    act(a16[:, c, TP:T], pb[:, 0:128], AF.Identity, bias=chv[:, c, 31:32])
                    act(a2[:, c, TP:T], pb[:, 0:128], AF.Square, bias=chv[:, c, 31:32])
            if gi == 0:
                cp("pool", hist_a[:], extp[:, :, TP:TP + 30])
            cts = col_tiles(0, T)
            stat_ps = []
            for ci, (c0, n) in enumerate(cts):
                pm = banks[2 + 2 * ci]
                pv_ = banks[3 + 2 * ci]
                for c in range(4):
                    mm(pm[:, 0:n], onesb[:], a16[:, c, c0:c0 + n], c == 0, c == 3)
                for c in range(4):
                    mm(pv_[:, 0:n], onesb[:], a2[:, c, c0:c0 + n], c == 0, c == 3)
                stat_ps.append((pm, pv_))
            ti = 0
            for ci, (c0, n) in enumerate(cts):
                pm, pv_ = stat_ps[ci]
                mean = lnt[:, 2 * (ci % 2), 0:n]
                rstd = lnt[:, 2 * (ci % 2) + 1, 0:n]
                ts("dve", mean, pm[:, 0:n], 1.0 / 512, ALU.mult)
                tt("pool", rstd, mean, mean, ALU.mult)
                stt("dve", rstd, pv_[:, 0:n], 1.0 / 512, rstd, ALU.mult, ALU.subtract)
                act(rstd, rstd, AF.Ln, bias=epst[:, 0:1])
                act(rstd, rstd, AF.Exp, scale=-0.5)
                for c in range(4):
                    tmp = lntmp[:, ti % 3, 0:n]
                    ti += 1
                    tt("dve", tmp, a16[:, c, c0:c0 + n], mean, ALU.subtract)
                    tt("dve", tmp, tmp, rstd, ALU.mult)
                    act(cat[:, c, c0:c0 + n], tmp, AF.Silu, bias=chv[:, c, 33:34], scale=chv[:, c, 32:33])
            chk('L0 conformer gi=%d' % gi)
            o = R1
            ebuf = mixv(o, 3 * 512, F32).rearrange("p (a n) -> p a n", a=3); o += 6144
            xbuf = mixv(o, 2 * 512, F32).rearrange("p (a n) -> p a n", a=2); o += 4096
            spb = mixv(o, 3 * 512, BF16).rearrange("p (a n) -> p a n", a=3); o += 3072
            wb = mixv(o, 2 * 512, BF16).rearrange("p (a n) -> p a n", a=2); o += 2048
            lacc = mixv(o, 3 * 512, BF16).rearrange("p (a n) -> p a n", a=3); o += 3072
            DEC0 = o
            SMP = G["sample"]
            its = []
            qgroups = [ptiles[i:i + 4] for i in range(0, npt, 4)]
            for qg in qgroups:
                gq0, gq1 = qg[0], qg[-1]
                NQ = len(qg) * 128
                ql0 = (gq0 - ptiles[0]) * 128
                for hp in range(4):
                    for hh in range(2):
                        for j in range(gq1, -1, -1):
                            its.append(dict(gq0=gq0, gq1=gq1, NQ=NQ, ql0=ql0, hp=hp, hh=hh, j=j,
                                            first=(j == gq1), last=(j == 0), i=len(its),
                                            off=(max(gq0, j) - gq0) * 128))

            def stageA(I):
                i, off, NQ, hp, hh, j = I["i"], I["off"], I["NQ"], I["hp"], I["hh"], I["j"]
                hsl = slice(hh * 64, hh * 64 + 64)
                hd = 2 * hp + hh
                pS = banks[2 + i % 2]
                k3 = i % 3
                diagblk = j >= I["gq0"]
                mm(pS[:, off:NQ], KT[hsl, hp, j * 128:(j + 1) * 128], QT[hsl, hp, I["ql0"] + off:I["ql0"] + NQ], True, not diagblk)
                if diagblk:
                    mm(pS[:, off:off + 128], identb[:], negm[:], False, True)
                act(ebuf[:, k3, off:NQ], pS[:, off:NQ], AF.Exp, bias=sbias[:, hd:hd + 1], scale=0.125)
                act(spb[:, k3, off:NQ], ebuf[:, k3, off:NQ], AF.Ln, bias=1.0)
                if not I["last"]:
                    ln_ = (i + 1) % 3
                    le = "dve" if SMP else "pool"
                    if off > 0:
                        mset(le, lacc[:, ln_, 0:off], 0.0)
                    if I["first"]:
                        cp(le, lacc[:, ln_, off:NQ], spb[:, k3, off:NQ])
                    else:
                        tt(le, lacc[:, ln_, off:NQ], lacc[:, i % 3, off:NQ], spb[:, k3, off:NQ], ALU.add)

            def stageB(I):
                i, off, NQ, hp, hh, j = I["i"], I["off"], I["NQ"], I["hp"], I["hh"], I["j"]
                pA = banks[4] if SMP else banks[4 + i % 2]
                k3 = i % 3
                k2 = i % 2
                mm(pA[:, off:NQ], tinc[:], spb[:, k3, off:NQ], True, I["first"])
                if not I["first"]:
                    mm(pA[:, off:NQ], onesb[:], lacc[:, i % 3, off:NQ], False, True)
                act(xbuf[:, k2, off:NQ], pA[:, off:NQ], AF.Exp, scale=-1.0)
                tt("dve", wb[:, k2, off:NQ], ebuf[:, k3, off:NQ], xbuf[:, k2, off:NQ], ALU.mult)

            def stageC(I):
                i, off, NQ, hp, hh, j = I["i"], I["off"], I["NQ"], I["hp"], I["hh"], I["j"]
                hsl = slice(hh * 64, hh * 64 + 64)
                hd = 2 * hp + hh
                pB = banks[5] if SMP else banks[6 + (hp % 2)]
                k2 = i % 2
                mm(pB[hsl, off:NQ], Vall[:, j, hd * 64:(hd + 1) * 64], wb[:, k2, off:NQ], I["first"], I["last"],
                   tp=(0, 64) if hh == 1 else None, sgc=True)
                if I["last"] and hh == 1:
                    cp("dve", cat[:, 4 + hp, I["ql0"]:I["ql0"] + NQ], pB[:, 0:NQ])

            dec_hook = None
            if SMP:
                sc0 = TP
                o = DEC0
                qblk = mixv(o, 4 * 16 * 16, BF16).rearrange("p (c s x) -> p c s x", c=4, s=16); o += 2048
                kring = mixv(o, 8 * 512, BF16).rearrange("p (a f) -> p a f", a=8); o += 8192
                ktp = mixv(o, 2 * 512, BF16).rearrange("p (a f) -> p a f", a=2); o += 2048
                vring = mixv(o, 8 * 512, BF16).rearrange("p (a f) -> p a f", a=8)
                eN = mixv(o, 1024, F32)
                o += 8192
                zbb = mixv(o, 2 * 512, F32).rearrange("p (a n) -> p a n", a=2)
                xdN = mixv(o, 1024, F32)
                o += 4096
                xd = mixv(o, 512, F32); o += 2048
                spdb = mixv(o, 2 * 512, BF16).rearrange("p (a n) -> p a n", a=2); o += 2048
                wd = mixv(o, 512, BF16); o += 1024
                spN = mixv(o, 1024, BF16); o += 2048
                wN = mixv(o, 1024, BF16); o += 2048
                assert o <= MIXB, o
                ksem = [P.dsem() for _ in range(8)]
                vsem = [P.dsem() for _ in range(8)]
                mset("pool", qblk[:], 0.0)
                for hp in range(4):
                    cp("pool", qblk[0:64, hp, :, 0:8], QT[0:64, hp, sc0:sc0 + 128].rearrange("p (s t) -> p s t", t=8))
                    cp("pool", qblk[64:128, hp, :, 8:16], QT[64:128, hp, sc0:sc0 + 128].rearrange("p (s t) -> p s t", t=8))
                pTv = banks[0][:].bitcast(BF16)
                pSd = banks[1]
                pAd = banks[6]
                pBd = banks[7]
                pS2 = [banks[1], banks[2]]
                pA2 = [banks[3], banks[4]]
                for hb in range(2):
                    for hd in range(4 * hb, 4 * hb + 4):
                        hp, hh = hd // 2, hd % 2
                        hsl = slice(hh * 64, hh * 64 + 64)
                        dstp = pS2[hb][:, (hd % 4) * 128:(hd % 4 + 1) * 128]
                        mm(dstp, KT[hsl, hp, SEQ:SEQ + 128], QT[hsl, hp, sc0:sc0 + 128], hd % 4 == 0, False, sgc=True)
                        mm(dstp, identb[:], maskN[:], False, True, sgc=True)
                    for hd in range(4 * hb, 4 * hb + 4):
                        dstp = pS2[hb][:, (hd % 4) * 128:(hd % 4 + 1) * 128]
                        act(eN[:, hd * 128:(hd + 1) * 128], dstp, AF.Exp, bias=sbias[:, hd:hd + 1], scale=0.125)
                act(spN, eN, AF.Ln, bias=1.0)
                for bq in range(2):
                    mm(pA2[bq][:], tinc[:], spN[:, bq * 512:(bq + 1) * 512], True, True)
                    act(xdN[:, bq * 512:(bq + 1) * 512], pA2[bq][:], AF.Exp, scale=-1.0)
                tt("dve", wN, eN, xdN, ALU.mult)
                spN3 = spN.rearrange("p (h c) -> p h c", h=8)
                wN3 = wN.rearrange("p (h c) -> p h c", h=8)
                NU = 2 * NSEQ

                def pages(u):
                    return (u // 2, 8 if u % 2 == 0 else 0)

                def M1(u):
                    s_, p0 = pages(u)
                    for jl in range(8):
                        col = s_ * NPAGE + p0 + jl
                        kdst = kring[:, jl, :]
                        P.add("pool", lambda e, kdst=kdst, col=col: e.indirect_dma_start(
                            out=kdst, out_offset=None, in_=ck,
                            in_offset=bass.IndirectOffsetOnAxis(ap=pidx[:, col:col + 1], axis=0)),
                            [V(pidx[:, col:col + 1])], [V(kdst)], dsem=ksem[jl])

                def MV(u):
                    s_, p0 = pages(u)
                    for jl in range(8):
                        col = s_ * NPAGE + p0 + jl
                        vdst = vring[:, jl, :]
                        P.add("pool", lambda e, vdst=vdst, col=col: e.indirect_dma_start(
                            out=vdst, out_offset=None, in_=cv,
                            in_offset=bass.IndirectOffsetOnAxis(ap=pidx[:, col:col + 1], axis=0)),
                            [V(pidx[:, col:col + 1])], [V(vdst)], dsem=vsem[jl])

                def M2(u):
                    s_, p0 = pages(u)
                    for jl in range(8):
                        half = jl % 2
                        for c in range(4):
                            tr(pTv[:, half * 512 + c * 128: half * 512 + (c + 1) * 128], kring[:, jl, c * 128:(c + 1) * 128], identb[:])
                        cp("dve", ktp[:, half, :], pTv[:, half * 512:(half + 1) * 512])
                        for hp in range(4):
                            mm(pSd[:, jl * 64 + hp * 16:jl * 64 + hp * 16 + 16],
                               ktp[:, half, hp * 128:(hp + 1) * 128], qblk[:, hp, s_, :], True, True)

                def M3(u):
                    s_, p0 = pages(u)
                    hf = u % 2
                    zb = zbb[:, hf, :]
                    spd = spdb[:, hf, :]
                    stt("dve", zb.rearrange("p (j c) -> p j c", c=64), pSd[:].rearrange("p (j c) -> p j c", c=64),
                        0.125, biasd[:].unsqueeze(1).broadcast_to([128, 8, 64]), ALU.mult, ALU.add)
                    act(zb, zb, AF.Exp)
                    act(spd, zb, AF.Ln, bias=1.0)
                    mm(pAd[:], tinc[:], spd, True, False)
                    for jp in range(1, 8):
                        mm(pAd[:, 0:jp * 64].rearrange("p (j c) -> p j c", c=64), onesb[:],
                           spd[:, jp * 64:(jp + 1) * 64].unsqueeze(1).broadcast_to([128, jp, 64]), False, False)
                    if hf == 1:
                        sph = spdb[:, 0, :]
                        for jp in range(8):
                            mm(pAd[:].rearrange("p (j c) -> p j c", c=64), onesb[:],
                               sph[:, jp * 64:(jp + 1) * 64].unsqueeze(1).broadcast_to([128, 8, 64]), False, False)
                    newc = spN3[:, :, 8 * s_:8 * s_ + 8].unsqueeze(1).broadcast_to([128, 8, 8, 8])
                    mm(pAd[:].rearrange("p (j h q) -> p j h q", j=8, h=8), onesb[:], newc, False, True)

                def M4(u):
                    s_, p0 = pages(u)
                    hf = u % 2
                    zb = zbb[:, hf, :]
                    act(xd, pAd[:], AF.Exp, scale=-1.0)
                    tt("dve", wd, zb, xd, ALU.mult)
                    for jl in range(8):
                        for hp in range(4):
                            mm(pBd[:, hp * 16:hp * 16 + 16], vring[:, jl, hp * 128:(hp + 1) * 128],
                               wd[:, jl * 64 + hp * 16:jl * 64 + hp * 16 + 16], hf == 0 and jl == 0 and hp == 0, False, sgc=True)
                    if hf == 1:
                        for hp in range(4):
                            mm(pBd[:, hp * 16:hp * 16 + 16].rearrange("p (a q) -> p a q", a=2), vs_tok[:, hp * 128:(hp + 1) * 128],
                               wN3[:, 2 * hp:2 * hp + 2, 8 * s_:8 * s_ + 8], False, hp == 3, sgc=True)
                        pbv4 = pBd[:, 0:64].rearrange("p (c x) -> p c x", c=4)
                        cp("dve", cat[0:64, 4:8, sc0 + 8 * s_:sc0 + 8 * s_ + 8], pbv4[0:64, :, 0:8])
                        cp("dve", cat[64:128, 4:8, sc0 + 8 * s_:sc0 + 8 * s_ + 8], pbv4[64:128, :, 8:16])

                NMS = NU + 3
                IPM = max(4, -(-(len(its) + 2) // NMS))

                def micro(m, part):
                    if part == 0:
                        if 0 <= m - 3 < NU:
                            M4(m - 3)
                        if 0 <= m - 2 < NU:
                            MV(m - 2)
                    elif part == 1:
                        if 0 <= m - 2 < NU:
                            M3(m - 2)
                    elif part == 2:
                        if 0 <= m - 1 < NU:
                            M2(m - 1)
                        if m < NU:
                            M1(m)

                dstate = {"m": 0, "part": 0}

                def dec_hook(step, flush=False):
                    while dstate["m"] < NMS:
                        m, part = dstate["m"], dstate["part"]
                        due = m * IPM + min(2 * part, IPM - 1)
                        if not flush and due > step:
                            break
                        micro(m, part)
                        if part == 2:
                            dstate["m"] += 1
                            dstate["part"] = 0
                        else:
                            dstate["part"] += 1

            n_it = len(its)
            for step in range(n_it + 2):
                if dec_hook is not None:
                    dec_hook(step)
                if step < n_it:
                    stageA(its[step])
                if 0 <= step - 1 < n_it:
                    stageB(its[step - 1])
                if 0 <= step - 2 < n_it:
                    stageC(its[step - 2])
            if dec_hook is not None:
                dec_hook(0, flush=True)

            chk('L0 decode gi=%d' % gi)
            residual_proj(8, cat)
            ffn(0)

            chk('L0 ffn gi=%d' % gi)
            norm_T(2)
            o = 0
            gbb = mixv(o, 4 * T, BF16).rearrange("p (c t) -> p c t", c=4); o += 4 * T * 2
            cxp = mixv(o, 4 * (2 + TP), BF16).rearrange("p (c t) -> p c t", c=4); o += 4 * (2 + TP) * 2
            upx = mixv(o, 4 * (16 + TP), BF16).rearrange("p (c t) -> p c t", c=4)[:, :, 1:16 + TP]; o += 4 * (16 + TP) * 2
            if G["sample"]:
                cxs = mixv(o, 4 * 16 * 10, BF16).rearrange("p (c s t) -> p c s t", c=4, s=16); o += 4 * 160 * 2
                usx = mixv(o, 4 * 16 * 24, BF16).rearrange("p (c s t) -> p c s t", c=4, s=16)[:, :, :, 1:24]; o += 4 * 16 * 24 * 2
            gct = mixv(o, 2 * 512, BF16).rearrange("p (a n) -> p a n", a=2); o += 2048
            acc = mixv(o, 16 + T, F32); o += (16 + T) * 4
            acc2 = mixv(o, 16 + T, F32); o += (16 + T) * 4
            pooled = mixv(o, T, BF16); o += T * 2
            sts = mixv(o, 2 * 512, F32).rearrange("p (a n) -> p a n", a=2); o += 4096
            assert o <= MIXB, o
            if gi == 0:
                mset("pool", cxp[:, :, 0:2], 0.0)
                mset("pool", upx[:, :, 0:15], 0.0)
            else:
                cp("pool", cxp[:, :, 0:2], hist_c[:])
                cp("pool", upx[:, :, 0:15], hist_u[:])
                lload(sts[0:32, 0, :], st_sc)
                pb = nbank()
                for c in range(4):
                    tr(pb[:, c * 128:c * 128 + 32], sts[0:32, 0, c * 128:(c + 1) * 128], identf[0:32, 0:32])
                cp("dve", cxs[:, :, :, 0:2], pb[:].rearrange("p (c x) -> p c x", c=4)[:, :, 0:32].rearrange("p c (s r) -> p c s r", s=16))
                for i in range(2):
                    lload(sts[0:120, 1, :], st_pool[i * 120:(i + 1) * 120, :])
                    pb = nbank()
                    for c in range(4):
                        tr(pb[:, c * 128:c * 128 + 120], sts[0:120, 1, c * 128:(c + 1) * 128], identf[0:120, 0:120])
                    cp("dve", usx[:, :, 8 * i:8 * i + 8, 0:15],
                       pb[:].rearrange("p (c x) -> p c x", c=4)[:, :, 0:120].rearrange("p c (s r) -> p c s r", s=8))
                P.add("sp", lambda e: e.dma_start(out=pool_s[:, 0:7, :], in_=st_pool.rearrange("(s r) f -> s r f", r=15)[:, 8:15, :]),
                      [], [], dsem=P.dsem(out=True))

            chk('L1 norm gi=%d' % gi)
            for b in range(2):
                wv = wget(8)
                for fc in range(2):
                    c = 2 * b + fc
                    for (c0, n) in col_tiles(0, T):
                        pa = nbank()
                        pg = nbank()
                        fm_mm(pa[:, 0:n], wv, fc * 128, c0, n)
                        fm_mm(pg[:, 0:n], wv, 256 + fc * 128, c0, n)
                        k = bank_rr["n"] % 2
                        cp("act", gct[:, k, 0:n], pa[:, 0:n])
                        npp = max(0, min(n, TP - c0))
                        if npp > 0:
                            tt("dve", cxp[:, c, 2 + c0:2 + c0 + npp], pg[:, 0:npp], gct[:, k, 0:npp], ALU.mult)
                        if npp < n:
                            tt("dve", cxs[:, c, :, 2:10], pg[:, npp:n].rearrange("p (s t) -> p s t", t=8),
                               gct[:, k, npp:n].rearrange("p (s t) -> p s t", t=8), ALU.mult)
                if gi == 1:
                    for lt, dst in ((npt - 1, "p"), (npt, "s")):
                        pa = nbank()
                        tm_mm(pa[:], wv, 0, 512, lt)
                        si = stage_slot()
                        cp("act", stage[:, si, 0:256], pa[:, 0:256])
                        tt("dve", stage[:, si, 256:512], pa[:, 256:512], stage[:, si, 0:256], ALU.mult)
                        if dst == "p":
                            dma("sp", sc_p[:, b * 256:(b + 1) * 256], stage[126:128, si, 256:512], osem[si], rd=[stage[:, si, 256:512]])
                        else:
                            for s_ in range(NSEQ):
                                dma("sp", sc_s[s_, :, b * 256:(b + 1) * 256], stage[8 * s_ + 6:8 * s_ + 8, si, 256:512], osem[si],
                                    rd=[stage[:, si, 256:512]])
            for b in range(2):
                wv = wget(8)
                for fc in range(2):
                    c = 2 * b + fc
                    for (c0, n) in col_tiles(0, T):
                        pa = nbank()
                        pu = nbank()
                        fm_mm(pa[:, 0:n], wv, fc * 128, c0, n)
                        fm_mm(pu[:, 0:n], wv, 256 + fc * 128, c0, n)
                        cp("act", gbb[:, c, c0:c0 + n], pa[:, 0:n])
                        npp = max(0, min(n, TP - c0))
                        if npp > 0:
                            cp("dve", upx[:, c, 15 + c0:15 + c0 + npp], pu[:, 0:npp])
                        if npp < n:
                            cp("dve", usx[:, c, :, 15:23], pu[:, npp:n].rearrange("p (s t) -> p s t", t=8))
                if gi == 1:
                    for lt, dst in ((npt - 1, "p"), (npt, "s")):
                        pa = nbank()
                        tm_mm(pa[:, 0:256], wv, 256, 256, lt)
                        si = stage_slot()
                        cp("dve", stage[:, si, 0:256], pa[:, 0:256])
                        if dst == "p":
                            dma("sp", pool_p[:, b * 256:(b + 1) * 256], stage[113:128, si, 0:256], osem[si], rd=[stage[:, si, 0:256]])
                        else:
                            for s_ in range(NSEQ):
                                dma("sp", pool_s[s_, 7:15, b * 256:(b + 1) * 256], stage[8 * s_:8 * s_ + 8, si, 0:256], osem[si],
                                    rd=[stage[:, si, 0:256]])
            if gi == 0:
                cp("pool", hist_c[:], cxp[:, :, TP:TP + 2])
                cp("pool", hist_u[:], upx[:, :, TP:TP + 15])

            chk('L1 proj gi=%d' % gi)
            for c in range(4):
                a_ = acc[:, 0:TP]
                ts("dve", a_, cxp[:, c, 0:TP], chv[:, c, 34:35], ALU.mult)
                stt("dve", a_, cxp[:, c, 1:1 + TP], chv[:, c, 35:36], a_, ALU.mult, ALU.add)
                stt("dve", a_, cxp[:, c, 2:2 + TP], chv[:, c, 36:37], a_, ALU.mult, ALU.add)
                tt("pool", cat[:, c, 0:TP], a_, gbb[:, c, 0:TP], ALU.mult)
                if G["sample"]:
                    a3 = acc2[:, 0:128].rearrange("p (s t) -> p s t", t=8)
                    ts("dve", a3, cxs[:, c, :, 0:8], chv[:, c, 34:35], ALU.mult)
                    stt("dve", a3, cxs[:, c, :, 1:9], chv[:, c, 35:36], a3, ALU.mult, ALU.add)
                    stt("dve", a3, cxs[:, c, :, 2:10], chv[:, c, 36:37], a3, ALU.mult, ALU.add)
                    tt("pool", cat[:, c, TP:T], acc2[:, 0:128], gbb[:, c, TP:T], ALU.mult)
            for g, w in enumerate((2, 4, 8, 16)):
                L = 15 + TP
                src = upx[:, g, 0:L]
                cur, nxt = acc[:, 0:L], acc2[:, 0:L]
                sh = 1
                first = True
                while sh < w:
                    a_in = src if first else cur
                    eng = "dve" if (sh in (1, 4)) else "pool"
                    if not first:
                        cp(eng, nxt[:, 0:sh], a_in[:, 0:sh])
                    tt(eng, nxt[:, sh:L], a_in[:, sh:L], a_in[:, 0:L - sh], ALU.add)
                    cur, nxt = nxt, cur
                    first = False
                    sh *= 2
                stt("dve", pooled[:, 0:TP], cur[:, 15:15 + TP], 1.0 / w, upx[:, g, 15:15 + TP], ALU.mult, ALU.subtract)
                if gi == 0:
                    tt("dve", nxt[:, 0:16], cur[:, 15:31], rc[:, g, :], ALU.mult)
                    tt("dve", pooled[:, 0:16], nxt[:, 0:16], upx[:, g, 15:31], ALU.subtract)
                if G["sample"]:
                    s3 = usx[:, g, :, :]
                    c3 = acc[:, 0:16 * 23].rearrange("p (s t) -> p s t", t=23)
                    n3 = acc2[:, 0:16 * 23].rearrange("p (s t) -> p s t", t=23)
                    sh = 1
                    first = True
                    while sh < w:
                        a_in = s3 if first else c3
                        if not first:
                            cp("pool", n3[:, :, 0:sh], a_in[:, :, 0:sh])
                        tt("pool", n3[:, :, sh:23], a_in[:, :, sh:23], a_in[:, :, 0:23 - sh], ALU.add)
                        c3, n3 = n3, c3
                        first = False
                        sh *= 2
                    stt("dve", pooled[:, TP:T].rearrange("p (s t) -> p s t", t=8), c3[:, :, 15:23], 1.0 / w, usx[:, g, :, 15:23],
                        ALU.mult, ALU.subtract)
                for (c0, n) in col_tiles(0, T):
                    pb = nbank()
                    mm(pb[:, 0:n], poolw[:, g, :], pooled[:, c0:c0 + n], True, True)
                    act(cat[:, 4 + g, c0:c0 + n], pb[:, 0:n], AF.Copy, scale=chv[:, g, 37:38])

            residual_proj(8, cat)
            ffn(1)

            chk('L1 mixer+ffn gi=%d' % gi)
            gslot = load_g(4)
            c0 = stat_col(3 * ntl)
            for lt in range(ntl):
                act(xn[:, lt % 2, :], h[:, lt, :], AF.Square, accum=stats[:, c0 + lt:c0 + lt + 1])
            act(stats[:, c0 + ntl:c0 + 2 * ntl], stats[:, c0:c0 + ntl], AF.Ln, bias=epst[:, 0:1], scale=1.0 / D)
            act(stats[:, c0 + 2 * ntl:c0 + 3 * ntl], stats[:, c0 + ntl:c0 + 2 * ntl], AF.Exp, scale=-0.5)
            for lt in range(ntl):
                hv = h[:, lt, :]
                si = stage_slot()
                stt("dve", stage[:, si, :], hv, stats[:, c0 + 2 * ntl + lt:c0 + 2 * ntl + lt + 1], gb[:, gslot, :], ALU.mult, ALU.mult)
                dst = yp[ptiles[lt] * 128:(ptiles[lt] + 1) * 128, :] if lt < npt else ys
                dma("sp", dst, stage[:, si, :], osem[si], rd=[stage[:, si, :]])

    try:
        chk('consts')
        run_groups()
    except _Stop as e_:
        print('STOPPED at', e_)
    if limit is None:
        assert wstate["cur"] == len(WSPECS), (wstate, len(WSPECS))
    P.emit()
    es.close()
    return nc


def make_in_maps(inp, compact_cache=False):
    f = lambda a: np.ascontiguousarray(np.asarray(a))
    gvec = f(np.stack([inp["norm_mix_g"][0], inp["norm_ffn_g"][0], inp["norm_mix_g"][1], inp["norm_ffn_g"][1],
                       inp["norm_final_g"]]))
    chv = f(np.concatenate([inp["conv_a_w"][0], inp["conv_a_b"], inp["ln_a_g"], inp["ln_a_b"], inp["conv_c_w"][0],
                            inp["pool_scale"]], axis=0))
    shared = {
        "gvec": gvec, "chv": chv, "sbb": f(inp["sb_bias"]).reshape(1, 8),
        "poolw": f(inp["pool_w"]).reshape(512, 128),
        "w_in_e": f(inp["w_in_even"][0]), "w_out_e": f(inp["w_out_even"][0]),
        "w_in_o": f(inp["w_in_odd"][0]), "w_out_o": f(inp["w_out_odd"][0]),
        "w_ffn_in": f(inp["w_ffn_in"]).reshape(2 * D, 2 * FFN), "w_ffn_out": f(inp["w_ffn_out"]).reshape(2 * FFN, D),
    }
    ck_full = np.asarray(inp["cache_k"])[0].reshape(-1, 512)
    cv_full = np.asarray(inp["cache_v"])[0].reshape(-1, 512)
    maps = []
    for c in range(NCORES):
        sl = slice(NSEQ * c, NSEQ * (c + 1))
        m = dict(shared)
        m["xp"] = f(inp["x_prompt"][c])
        m["xs"] = f(inp["x_sample"][sl]).reshape(128, D)
        m["st_conf"] = f(inp["state_conformer"][0, sl]).reshape(NSEQ * 30, 512)
        m["st_sc"] = f(inp["state_shortconv"][0, sl]).reshape(NSEQ * 2, 512)
        m["st_pool"] = f(inp["state_pool"][0, sl]).reshape(NSEQ * 15, 512)
        ptc = np.asarray(inp["page_table"])[sl].astype(np.int32)
        if compact_cache:
            pages = ptc.reshape(-1)
            m["ck"] = f(ck_full.reshape(-1, 128, 512)[pages]).reshape(-1, 512)
            m["cv"] = f(cv_full.reshape(-1, 128, 512)[pages]).reshape(-1, 512)
            m["pt"] = np.arange(256, dtype=np.int32).reshape(NSEQ, NPAGE)
        else:
            m["ck"] = ck_full
            m["cv"] = cv_full
            m["pt"] = f(ptc)
        maps.append(m)
    return maps


def gather_outputs(res):
    R = res.results
    cat = lambda k: np.stack([R[c][k] for c in range(NCORES)])
    y_p = cat("yp")
    y_s = np.concatenate([R[c]["ys"].reshape(NSEQ, DSEQ, D) for c in range(NCORES)])
    k_p = cat("kp").reshape(1, NCORES, SEQ, 8, 64)
    v_p = cat("vp").reshape(1, NCORES, SEQ, 8, 64)
    k_s = np.concatenate([R[c]["ks"].reshape(NSEQ, DSEQ, 8, 64) for c in range(NCORES)])[None]
    v_s = np.concatenate([R[c]["vs"].reshape(NSEQ, DSEQ, 8, 64) for c in range(NCORES)])[None]
    conv_p = cat("conv_p")[None]
    conv_s = np.concatenate([R[c]["conv_s"] for c in range(NCORES)])[None]
    sc_p = cat("sc_p")[None]
    sc_s = np.concatenate([R[c]["sc_s"] for c in range(NCORES)])[None]
    pool_p = cat("pool_p")[None]
    pool_s = np.concatenate([R[c]["pool_s"] for c in range(NCORES)])[None]
    outs = (y_p, y_s, k_p, v_p, k_s, v_s, conv_p, conv_s, sc_p, sc_s, pool_p, pool_s)
    return tuple(np.ascontiguousarray(o, dtype=np.float32) for o in outs)


def kernel(**inputs):
    npool = int(np.asarray(inputs["cache_k"]).shape[1])
    nc = build_nc(npool)
    in_maps = make_in_maps(inputs)
    res = run_bass_kernel_spmd(nc, in_maps, core_ids=list(range(NCORES)))
    return gather_outputs(res)
```

```python
import numpy as np
from contextlib import ExitStack
import concourse.bass as bass
import concourse.mybir as mybir
from concourse.bass_utils import run_bass_kernel_spmd

F32, BF16, I32 = mybir.dt.float32, mybir.dt.bfloat16, mybir.dt.int32
AF = mybir.ActivationFunctionType
ALU = mybir.AluOpType
ESZ = {F32: 4, BF16: 2, I32: 4}

NCORES = 8
D = 1024
SEQ = 2048
NSEQ = 16
DSEQ = 8
NPAGE = 16
FFN = 2816
NEG = -240000.0
EPS = 1e-6


class V:
    __slots__ = ("ap", "key", "lo", "hi")

    def __init__(self, ap):
        self.ap = ap
        dims = ap.ap
        pstep = dims[0][0]
        off = ap.offset % pstep if pstep > 0 else ap.offset
        ext = 1
        for st, cnt in dims[1:]:
            ext += (cnt - 1) * abs(st)
        es = ESZ[ap.dtype]
        self.key = ap.tensor.name
        self.lo = off * es
        self.hi = (off + ext) * es
        if str(ap.space) == "PSUM":
            self.lo, self.hi = 0, 2048


class DSem:
    def __init__(self, sem, group=False):
        self.sem = sem
        self.count = 0
        self.group = group


class Op:
    __slots__ = ("eng", "fn", "deps", "sig", "cnt", "isdma", "dsem", "dval", "idx")


class Prog:
    ENGS = ("pe", "act", "dve", "pool", "sp")

    def __init__(self, nc, es):
        self.nc = nc
        self.es = es
        self.q = {e: [] for e in self.ENGS}
        self.recs = {}
        self.esem = {e: es.enter_context(nc.semaphore("sem_" + e)) for e in ("pe", "act", "dve", "pool")}
        self.nsem = 0
        self.out_dsems = []

    def dsem(self, group=False, out=False):
        self.nsem += 1
        d = DSem(self.es.enter_context(self.nc.semaphore("dsem%d" % self.nsem)), group)
        self.out_dsems.append(d)
        return d

    def _access(self, v, op, is_write):
        L = self.recs.get(v.key)
        if L is None:
            L = self.recs[v.key] = []
        raw, other = [], []
        newL = []
        lo, hi = v.lo, v.hi
        cov = []
        for rec in L:
            rlo, rhi, w, rd = rec
            if rhi <= lo or rlo >= hi:
                newL.append(rec)
                continue
            if w is not None:
                (other if is_write else raw).append(w)
            if is_write:
                other.extend(rd.values())
            if rlo < lo:
                newL.append([rlo, lo, w, dict(rd)])
            if rhi > hi:
                newL.append([hi, rhi, w, dict(rd)])
            if not is_write:
                nrd = dict(rd)
                nrd[("d", id(op)) if op.isdma else op.eng] = op
                a, b = max(rlo, lo), min(rhi, hi)
                newL.append([a, b, w, nrd])
                cov.append((a, b))
        if is_write:
            newL.append([lo, hi, op, {}])
        else:
            cov.sort()
            cur = lo
            k = ("d", id(op)) if op.isdma else op.eng
            for a, b in cov:
                if a > cur:
                    newL.append([cur, a, None, {k: op}])
                cur = max(cur, b)
            if cur < hi:
                newL.append([cur, hi, None, {k: op}])
        self.recs[v.key] = newL
        return raw, other

    def add(self, eng, fn, reads=(), writes=(), dsem=None):
        op = Op()
        op.eng = eng
        op.fn = fn
        op.isdma = dsem is not None
        op.sig = False
        op.cnt = 0
        op.idx = len(self.q[eng])
        raw, other = [], []
        for v in reads:
            r, o = self._access(v, op, v.key.startswith("ps"))
            raw += r
            other += o
        for v in writes:
            r, o = self._access(v, op, True)
            raw += r
            other += o
        deps = {}
        for lst, is_raw in ((raw, True), (other, False)):
            for d in lst:
                if d is op:
                    continue
                if d.isdma:
                    deps[("dsem", id(d.dsem))] = (d, d.dsem.count)
                    continue
                if (not op.isdma) and d.eng == eng:
                    if eng == "pe":
                        continue
                k = d.eng
                if k not in deps or deps[k].idx < d.idx:
                    deps[k] = d
        op.deps = [x if isinstance(x, tuple) else (x, None) for x in deps.values()]
        for d, _ in op.deps:
            if not d.isdma:
                d.sig = True
        if op.isdma:
            dsem.count += 16
            op.dsem = dsem
            op.dval = dsem.count
        self.q[eng].append(op)
        return op

    def emit(self):
        nc = self.nc
        for eng, L in self.q.items():
            c = 0
            for op in L:
                if (not op.isdma) and op.sig:
                    c += 1
                    op.cnt = c
        engobj = {"pe": "tensor", "act": "scalar", "dve": "vector", "pool": "gpsimd", "sp": "sync"}
        with nc.Block() as block:
            for eng in self.ENGS:
                def body(e, eng=eng):
                    waited = {}
                    for op in self.q[eng]:
                        for d, dv in op.deps:
                            if d.isdma:
                                sem = d.dsem.sem
                                val = d.dsem.count if d.dsem.group else dv
                            else:
                                sem = self.esem[d.eng]
                                val = d.cnt
                            if waited.get(sem.num, 0) < val:
                                e.wait_ge(sem, val)
                                waited[sem.num] = val
                        ins = op.fn(e)
                        if op.isdma:
                            ins.then_inc(op.dsem.sem, 16)
                        elif op.sig:
                            ins.then_inc(self.esem[eng], 1)
                    if eng == "sp":
                        for d in self.out_dsems:
                            if d.count > 0:
                                e.wait_ge(d.sem, d.count)
                getattr(block, engobj[eng])(body)


class _Stop(Exception):
    pass


def build_nc(npool, limit=None):
    nc = bass.Bass("TRN2", target_bir_lowering=False)
    es = ExitStack()
    P = Prog(nc, es)

    def din(name, shape, dt=F32):
        return nc.dram_tensor(name, list(shape), dt, kind="ExternalInput").ap()

    def dout(name, shape, dt=F32):
        return nc.dram_tensor(name, list(shape), dt, kind="ExternalOutput").ap()

    xp = din("xp", [SEQ, D])
    xs = din("xs", [128, D])
    ck = din("ck", [npool * 128, 512])
    cv = din("cv", [npool * 128, 512])
    st_conf = din("st_conf", [NSEQ * 30, 512])
    st_sc = din("st_sc", [NSEQ * 2, 512])
    st_pool = din("st_pool", [NSEQ * 15, 512])
    pt = din("pt", [NSEQ, NPAGE], I32)
    gvec = din("gvec", [5, D])
    chvd = din("chv", [38, 512])
    sbb = din("sbb", [1, 8])
    poolwd = din("poolw", [512, 128])
    w_in_e = din("w_in_e", [D, 2560])
    w_out_e = din("w_out_e", [D, D])
    w_in_o = din("w_in_o", [D, 2048])
    w_out_o = din("w_out_o", [D, D])
    w_ffn_in = din("w_ffn_in", [2 * D, 2 * FFN])
    w_ffn_out = din("w_ffn_out", [2 * FFN, D])

    yp = dout("yp", [SEQ, D])
    ys = dout("ys", [128, D])
    kp = dout("kp", [SEQ, 512])
    vp = dout("vp", [SEQ, 512])
    ks = dout("ks", [128, 512])
    vs = dout("vs", [128, 512])
    conv_p = dout("conv_p", [30, 512])
    conv_s = dout("conv_s", [NSEQ, 30, 512])
    sc_p = dout("sc_p", [2, 512])
    sc_s = dout("sc_s", [NSEQ, 2, 512])
    pool_p = dout("pool_p", [15, 512])
    pool_s = dout("pool_s", [NSEQ, 15, 512])

    def sb(name, shape, dt):
        return es.enter_context(nc.sbuf_tensor(name, list(shape), dt))

    def ps(name):
        return es.enter_context(nc.psum_tensor(name, [128, 512], F32))

    TMAX = 1152
    h = sb("h", [128, 9, D], F32)
    actA = sb("actA", [128, 8, TMAX], BF16)
    KT = sb("KT", [128, 4, SEQ + 128], BF16)
    Vall = sb("Vall", [128, 16, 512], BF16)
    wslot = sb("wslot", [128, 2, 6144], BF16)
    gb = sb("gb", [128, 2, D], F32)
    stage = sb("stage", [128, 2, D], F32)
    xn = sb("xn", [128, 2, D], BF16)
    identb = sb("identb", [128, 128], BF16)
    identf = sb("identf", [128, 128], F32)
    tinc = sb("tinc", [128, 128], BF16)
    onesb = sb("onesb", [128, 128], BF16)
    negm = sb("negm", [128, 128], BF16)
    onesf = sb("onesf", [128, 128], F32)
    chv = sb("chvs", [128, 4, 38], F32)
    chraw = sb("chraw", [38, 512], F32)
    sbias = sb("sbias", [128, 8], F32)
    biasd = sb("biasd", [128, 64], F32)
    epst = sb("epst", [128, 1], F32)
    poolw = sb("poolws", [128, 4, 128], BF16)
    rc = sb("rc", [128, 4, 16], F32)
    stats = sb("stats", [128, 64], F32)
    ptb = sb("ptb", [128, NSEQ * NPAGE], I32)
    pidx = sb("pidx", [128, NSEQ * NPAGE], I32)
    iot = sb("iot", [128, 1], F32)
    hist_a = sb("hist_a", [128, 4, 30], BF16)
    hist_c = sb("hist_c", [128, 4, 2], BF16)
    hist_u = sb("hist_u", [128, 4, 15], BF16)
    MIXB = 66 * 1024
    mix = sb("mix", [128, MIXB // 2], BF16)
    banks = [ps("ps%d" % i) for i in range(8)]

    maskN = sb("maskN", [128, 128], BF16)
    ebl = sb("ebl", [16, 128], F32)
    eblb = sb("eblb", [16, 128], BF16)

    def mixv(off_bytes, nelem, dt):
        assert off_bytes % 4 == 0
        assert off_bytes + nelem * ESZ[dt] <= MIXB, (off_bytes, nelem, MIXB)
        a = mix[:, off_bytes // 2: off_bytes // 2 + nelem * ESZ[dt] // 2]
        return a if dt == BF16 else a.bitcast(dt)

    def rv(x):
        return x if isinstance(x, V) else V(x)

    def act(out, in_, func, bias=0.0, scale=1.0, accum=None):
        out, in_ = rv(out), rv(in_)
        rd = [in_]
        wr = [out]
        kw = {}
        if hasattr(bias, "ap"):
            bias = rv(bias)
            rd.append(bias)
            kw["bias"] = bias.ap
        else:
            kw["bias"] = float(bias)
        if hasattr(scale, "ap"):
            scale = rv(scale)
            rd.append(scale)
            kw["scale"] = scale.ap
        else:
            kw["scale"] = float(scale)
        if accum is not None:
            accum = rv(accum)
            wr.append(accum)
            kw["accum_out"] = accum.ap
        return P.add("act", lambda e: e.activation(out=out.ap, in_=in_.ap, func=func, **kw), rd, wr)

    def tt(eng, out, in0, in1, op):
        out, in0, in1 = rv(out), rv(in0), rv(in1)
        return P.add(eng, lambda e: e.tensor_tensor(out=out.ap, in0=in0.ap, in1=in1.ap, op=op), [in0, in1], [out])

    def ts(eng, out, in0, s1, op0, s2=None, op1=None):
        out, in0 = rv(out), rv(in0)
        rd = [in0]
        a1 = s1
        if hasattr(s1, "ap"):
            s1 = rv(s1)
            rd.append(s1)
            a1 = s1.ap
        a2 = s2
        if s2 is not None and hasattr(s2, "ap"):
            s2 = rv(s2)
            rd.append(s2)
            a2 = s2.ap
        if op1 is None:
            return P.add(eng, lambda e: e.tensor_scalar(out=out.ap, in0=in0.ap, scalar1=a1, scalar2=None, op0=op0), rd, [out])
        return P.add(eng, lambda e: e.tensor_scalar(out=out.ap, in0=in0.ap, scalar1=a1, scalar2=a2, op0=op0, op1=op1), rd, [out])

    def stt(eng, out, in0, s, in1, op0, op1):
        out, in0, in1 = rv(out), rv(in0), rv(in1)
        rd = [in0, in1]
        a = s
        if hasattr(s, "ap"):
            s = rv(s)
            rd.append(s)
            a = s.ap
        return P.add(eng, lambda e: e.scalar_tensor_tensor(out=out.ap, in0=in0.ap, scalar=a, in1=in1.ap, op0=op0, op1=op1), rd, [out])

    def cp(eng, out, in_):
        out, in_ = rv(out), rv(in_)
        if eng == "act":
            return P.add("act", lambda e: e.copy(out=out.ap, in_=in_.ap), [in_], [out])
        return P.add(eng, lambda e: e.tensor_copy(out=out.ap, in_=in_.ap), [in_], [out])

    def mset(eng, out, val):
        out = rv(out)
        return P.add(eng, lambda e: e.memset(out.ap, val), [], [out])

    def asel(out, in_, pattern, op, base, cm):
        out, in_ = rv(out), rv(in_)
        return P.add("pool", lambda e: e.affine_select(out=out.ap, in_=in_.ap, pattern=pattern, compare_op=op, fill=0.0,
                                                       base=base, channel_multiplier=cm), [in_], [out])

    def mm(out, lhsT, rhs, start, stop, tp=None, sgc=False):
        out, lhsT, rhs = rv(out), rv(lhsT), rv(rhs)
        kw = {}
        if tp is not None:
            kw["tile_position"] = tp
        if sgc:
            kw["skip_group_check"] = True
        rd = [lhsT, rhs] + ([] if start else [out])
        return P.add("pe", lambda e: e.matmul(out.ap, lhsT=lhsT.ap, rhs=rhs.ap, start=start, stop=stop, **kw), rd, [out])

    def tr(out, in_, ident):
        out, in_, ident = rv(out), rv(in_), rv(ident)
        return P.add("pe", lambda e: e.transpose(out.ap, in_.ap, ident.ap), [in_, ident], [out])

    def dma(q, out, in_, dsem, rd=(), wr=()):
        return P.add(q, lambda e: e.dma_start(out=out, in_=in_), [rv(x) for x in rd], [rv(x) for x in wr], dsem=dsem)

    ckstate = {"n": 0}

    def chk(name):
        ckstate["n"] += 1
        if limit is not None and ckstate["n"] > limit:
            raise _Stop(name)

    cdsem = P.dsem(group=True)

    def cload(out_ap, in_ap):
        dma("sp", out_ap, in_ap, cdsem, wr=[out_ap])

    def lload(out_ap, in_ap):
        dma("sp", out_ap, in_ap, P.dsem(), wr=[out_ap])

    cload(chraw[0:38, :], chvd)
    cload(sbias[:], sbb.rearrange("a b -> (a b)").partition_broadcast(128))
    cload(ptb[:], pt.rearrange("a b -> (a b)").partition_broadcast(128))
    pw32 = mixv(0, 512, F32).rearrange("p (g d) -> p g d", g=4)
    cload(pw32, poolwd.rearrange("(g c) d -> c g d", c=128))
    cp("dve", poolw[:], pw32)

    mset("pool", onesf[:], 1.0)
    mset("pool", epst[:], EPS)
    asel(identf[:], onesf[:], [[-1, 128]], ALU.is_equal, 0, 1)
    tincf = mixv(4096, 128, F32)
    asel(tincf, onesf[:], [[-1, 128]], ALU.is_ge, 0, 1)
    cp("pool", identb[:], identf[:])
    cp("pool", tinc[:], tincf)
    cp("pool", onesb[:], onesf[:])
    ts("pool", negm[:], tincf, NEG, ALU.mult)
    P.add("pool", lambda e: e.iota(iot[:], [[0, 1]], base=0, channel_multiplier=1, allow_small_or_imprecise_dtypes=True),
          [], [V(iot[:])])
    ts("dve", pidx[:], ptb[:], 128.0, ALU.mult, iot[:, 0:1], ALU.add)
    cp("dve", biasd[:].rearrange("p (h q) -> p h q", q=8), sbias[:].unsqueeze(2).broadcast_to([128, 8, 8]))
    for c in range(4):
        tr(banks[0][0:128, c * 64: c * 64 + 38], chraw[0:38, c * 128:(c + 1) * 128], identf[0:38, 0:38])
    cp("dve", chv[:], banks[0][:, 0:256].rearrange("p (c r) -> p c r", c=4)[:, :, 0:38])
    rcf = mixv(8192, 16, F32)
    P.add("pool", lambda e: e.iota(rcf, [[1, 16]], base=1, channel_multiplier=0, allow_small_or_imprecise_dtypes=True),
          [], [V(rcf)])
    for g, w in enumerate((2, 4, 8, 16)):
        ts("dve", rc[:, g, :], rcf, float(w), ALU.min)
    P.add("dve", lambda e: e.reciprocal(out=rc[:], in_=rc[:]), [V(rc[:])], [V(rc[:])])
    mset("pool", ebl[:], 1.0)
    asel(ebl[:], ebl[:], [[1, 128]], ALU.is_ge, 0, -8)
    asel(ebl[:], ebl[:], [[-1, 128]], ALU.is_ge, 7, 8)
    cp("pool", eblb[:], ebl[:])
    mm(banks[1][:, 0:128], eblb[0:16, :], eblb[0:16, :], True, True)
    blk = mixv(12288, 128, F32)
    cp("dve", blk, banks[1][:, 0:128])
    asel(blk, blk, [[1, 128]], ALU.is_gt, 0, -1)
    ts("dve", maskN[:], blk, -NEG, ALU.mult, NEG, ALU.add)

    wsems = [[P.dsem(), P.dsem()], [P.dsem(), P.dsem()]]
    WSPECS = []

    def wspec_layer0():
        for b in range(2):
            WSPECS.append([(w_in_e[:, b * 256:(b + 1) * 256], 8, 0, 256, 512),
                           (w_in_e[:, 512 + b * 256:512 + (b + 1) * 256], 8, 256, 256, 512)])
        for q0 in (1024, 1536, 2048):
            WSPECS.append([(w_in_e[:, q0:q0 + 512], 8, 0, 512, 512)])

    def wspec_out(w):
        for ch in range(2):
            WSPECS.append([(w[:, ch * 512:(ch + 1) * 512], 8, 0, 512, 512)])

    def wspec_ffn(layer):
        wi = w_ffn_in[layer * D:(layer + 1) * D, :]
        wo = w_ffn_out[layer * FFN:(layer + 1) * FFN, :]
        for (b0, b1) in ((0, 6), (6, 11)):
            for b in range(b0, b1):
                WSPECS.append([(wi[:, b * 256:(b + 1) * 256], 8, 0, 256, 512),
                               (wi[:, FFN + b * 256:FFN + (b + 1) * 256], 8, 256, 256, 512)])
            nch = 2 * (b1 - b0)
            for ch in range(2):
                WSPECS.append([(wo[b0 * 256:b1 * 256, ch * 512:(ch + 1) * 512], nch, 0, 512, 512)])

    def wspec_layer1():
        for b in range(2):
            WSPECS.append([(w_in_o[:, 512 + b * 256:512 + (b + 1) * 256], 8, 0, 256, 512),
                           (w_in_o[:, 1024 + b * 256:1024 + (b + 1) * 256], 8, 256, 256, 512)])
        for b in range(2):
            WSPECS.append([(w_in_o[:, b * 256:(b + 1) * 256], 8, 0, 256, 512),
                           (w_in_o[:, 1536 + b * 256:1536 + (b + 1) * 256], 8, 256, 256, 512)])

    for _g in range(2):
        wspec_layer0()
        wspec_out(w_out_e)
        wspec_ffn(0)
        wspec_layer1()
        wspec_out(w_out_o)
        wspec_ffn(1)
    wstate = {"cur": 0, "issued": 0}

    def wissue(i):
        s = i % 2
        for pi, (src, nk, c0, ncols, stride) in enumerate(WSPECS[i]):
            dst = wslot[:, s, 0:nk * stride].rearrange("p (k n) -> p k n", k=nk)[:, :, c0:c0 + ncols]
            srcv = src.rearrange("(k p) n -> p k n", p=128)
            P.add("pool", lambda e, dst=dst, srcv=srcv: e.dma_start(out=dst, in_=srcv), [], [V(dst)], dsem=wsems[s][pi])

    def wget(nk, stride=512):
        i = wstate["cur"]
        wstate["cur"] += 1
        while wstate["issued"] <= min(i + 1, len(WSPECS) - 1):
            wissue(wstate["issued"])
            wstate["issued"] += 1
        assert WSPECS[i][0][1] == nk, (i, nk, WSPECS[i][0][1])
        return wslot[:, i % 2, 0:nk * stride].rearrange("p (k n) -> p k n", k=nk)

    groups = [
        dict(ptiles=list(range(0, 9)), sample=False),
        dict(ptiles=list(range(9, 16)), sample=True),
    ]
    xsems = [P.dsem() for _ in range(9)]
    gsem = [P.dsem(), P.dsem()]
    osem = [P.dsem(out=True), P.dsem(out=True)]
    ostate = {"n": 0}

    def stage_slot():
        i = ostate["n"] % 2
        ostate["n"] += 1
        return i

    gstate = {"n": 0}

    def load_g(row):
        i = gstate["n"] % 2
        gstate["n"] += 1
        dma("sp", gb[:, i, :], gvec[row].partition_broadcast(128), gsem[i], wr=[gb[:, i, :]])
        return i

    statc = {"n": 0}

    def stat_col(n=3):
        c = statc["n"]
        if c + n > 64:
            c = 0
        statc["n"] = c + n
        return c

    def col_tiles(lo, hi, step=512):
        r = []
        c = lo
        while c < hi:
            r.append((c, min(step, hi - c)))
            c += step
        return r

    bank_rr = {"n": 0}

    def nbank(lo=2, hi=8):
        b = lo + bank_rr["n"] % (hi - lo)
        bank_rr["n"] += 1
        return banks[b]

    def run_groups():
        for gi, G in enumerate(groups):
            ptiles = G["ptiles"]
            npt = len(ptiles)
            ntl = npt + (1 if G["sample"] else 0)
            TP = npt * 128
            T = ntl * 128
            first_tok = ptiles[0] * 128
            hnT = actA
            cat = actA

            for lt, gt in enumerate(ptiles):
                dma("sp", h[:, lt, :], xp[gt * 128:(gt + 1) * 128, :], xsems[lt], wr=[h[:, lt, :]])
            if G["sample"]:
                dma("sp", h[:, npt, :], xs, xsems[npt], wr=[h[:, npt, :]])

            def norm_T(grow):
                gslot = load_g(grow)
                c0 = stat_col(3 * ntl)
                for lt in range(ntl):
                    act(xn[:, lt % 2, :], h[:, lt, :], AF.Square, accum=stats[:, c0 + lt:c0 + lt + 1])
                act(stats[:, c0 + ntl:c0 + 2 * ntl], stats[:, c0:c0 + ntl], AF.Ln, bias=epst[:, 0:1], scale=1.0 / D)
                act(stats[:, c0 + 2 * ntl:c0 + 3 * ntl], stats[:, c0 + ntl:c0 + 2 * ntl], AF.Exp, scale=-0.5)
                for lt in range(ntl):
                    hv = h[:, lt, :]
                    xs_ = lt % 2
                    stt("dve", xn[:, xs_, :], hv, stats[:, c0 + 2 * ntl + lt:c0 + 2 * ntl + lt + 1], gb[:, gslot, :], ALU.mult, ALU.mult)
                    pb = banks[lt % 2]
                    pbv = pb[:].bitcast(BF16)
                    for kc in range(8):
                        tr(pbv[:, kc * 128:(kc + 1) * 128], xn[:, xs_, kc * 128:(kc + 1) * 128], identb[:])
                    cp("dve" if lt % 2 == 0 else "act", hnT[:, :, lt * 128:(lt + 1) * 128],
                       pbv.rearrange("p (k n) -> p k n", k=8))

            def fm_mm(psv, wv, wc0, c0, n):
                for kc in range(8):
                    mm(psv, wv[:, kc, wc0:wc0 + 128], hnT[:, kc, c0:c0 + n], kc == 0, kc == 7)

            def tm_mm(psv, wv, wc0, ncols, lt):
                for kc in range(8):
                    mm(psv, hnT[:, kc, lt * 128:(lt + 1) * 128], wv[:, kc, wc0:wc0 + ncols], kc == 0, kc == 7)

            def residual_proj(nk_total, src):
                for ch in range(2):
                    wv = wget(nk_total)
                    for lt in range(ntl):
                        pb = nbank()
                        for kc in range(nk_total):
                            mm(pb[:], src[:, kc, lt * 128:(lt + 1) * 128], wv[:, kc, :], kc == 0, kc == nk_total - 1)
                        tt("dve", h[:, lt, ch * 512:(ch + 1) * 512], h[:, lt, ch * 512:(ch + 1) * 512], pb[:], ALU.add)

            def ffn(layer):
                hid = mixv(0, 12 * T, BF16).rearrange("p (c t) -> p c t", c=12)
                sg = mixv(12 * T * 2, 2 * 512, BF16).rearrange("p (a n) -> p a n", a=2)
                norm_T(1 + 2 * layer)
                for half, (b0, b1) in enumerate(((0, 6), (6, 11))):
                    nch = 2 * (b1 - b0)
                    for b in range(b0, b1):
                        wv = wget(8)
                        for fc in range(2):
                            lc = 2 * (b - b0) + fc
                            for (c0, n) in col_tiles(0, T):
                                pg = nbank()
                                pu = nbank()
                                fm_mm(pg[:, 0:n], wv, fc * 128, c0, n)
                                fm_mm(pu[:, 0:n], wv, 256 + fc * 128, c0, n)
                                k = bank_rr["n"] % 2
                                act(sg[:, k, 0:n], pg[:, 0:n], AF.Silu)
                                tt("dve", hid[:, lc, c0:c0 + n], pu[:, 0:n], sg[:, k, 0:n], ALU.mult)
                    residual_proj(nch, hid)

            norm_T(0)
            o = 0
            QT = mixv(o, 4 * T, BF16).rearrange("p (c t) -> p c t", c=4); o += 4 * T * 2
            sgt = mixv(o, 2 * 512, F32).rearrange("p (a n) -> p a n", a=2)
            vs_tok = mixv(o, 512, BF16)
            lntmp = mixv(o + 1024, 3 * 512, BF16).rearrange("p (a n) -> p a n", a=3)
            o += 4096
            R1 = o
            extp = mixv(o, 4 * (30 + TP), BF16).rearrange("p (c t) -> p c t", c=4); o += 4 * (30 + TP) * 2
            if G["sample"]:
                exts = mixv(o, 4 * 16 * 38, BF16).rearrange("p (c s t) -> p c s t", c=4, s=16); o += 4 * 16 * 38 * 2
            diag2 = mixv(o, 2 * 31 * 128, BF16).rearrange("p (d j n) -> p d j n", d=2, j=31); o += 2 * 31 * 128 * 2
            a16o = o
            a16 = mixv(o, 4 * T, BF16).rearrange("p (c t) -> p c t", c=4); o += 4 * T * 2
            a2 = mixv(o, 4 * T, BF16).rearrange("p (c t) -> p c t", c=4); o += 4 * T * 2
            lnt = mixv(o, 4 * 512, F32).rearrange("p (a n) -> p a n", a=4); o += 4 * 2048
            assert o <= MIXB, o

            if gi == 0:
                mset("pool", extp[:, :, 0:30], 0.0)
            else:
                cp("pool", extp[:, :, 0:30], hist_a[:])
                stt_ = mixv(a16o, 4 * 512, F32).rearrange("p (a n) -> p a n", a=4)
                for i in range(4):
                    lload(stt_[0:120, i, :], st_conf[i * 120:(i + 1) * 120, :])
                for i in range(4):
                    pb = nbank()
                    for c in range(4):
                        tr(pb[:, c * 128:c * 128 + 120], stt_[0:120, i, c * 128:(c + 1) * 128], identf[0:120, 0:120])
                    cp("dve", exts[:, :, 4 * i:4 * i + 4, 0:30],
                       pb[:].rearrange("p (c x) -> p c x", c=4)[:, :, 0:120].rearrange("p c (s r) -> p c s r", s=4))
                P.add("sp", lambda e: e.dma_start(out=conv_s[:, 0:22, :], in_=st_conf.rearrange("(s r) f -> s r f", r=30)[:, 8:30, :]),
                      [], [], dsem=P.dsem(out=True))

            chk('L0 norm gi=%d' % gi)
            for b in range(2):
                wv = wget(8)
                for fc in range(2):
                    c = 2 * b + fc
                    for (c0, n) in col_tiles(0, T):
                        pa = nbank()
                        pg = nbank()
                        fm_mm(pa[:, 0:n], wv, fc * 128, c0, n)
                        fm_mm(pg[:, 0:n], wv, 256 + fc * 128, c0, n)
                        k = bank_rr["n"] % 2
                        act(sgt[:, k, 0:n], pg[:, 0:n], AF.Sigmoid)
                        npp = max(0, min(n, TP - c0))
                        if npp > 0:
                            tt("dve", extp[:, c, 30 + c0:30 + c0 + npp], pa[:, 0:npp], sgt[:, k, 0:npp], ALU.mult)
                        if npp < n:
                            assert n - npp == 128
                            tt("dve", exts[:, c, :, 30:38], pa[:, npp:n].rearrange("p (s t) -> p s t", t=8),
                               sgt[:, k, npp:n].rearrange("p (s t) -> p s t", t=8), ALU.mult)
                if gi == 1:
                    for lt, dst in ((npt - 1, "p"), (npt, "s")):
                        pa = nbank()
                        tm_mm(pa[:], wv, 0, 512, lt)
                        si = stage_slot()
                        act(stage[:, si, 0:256], pa[:, 256:512], AF.Sigmoid)
                        tt("dve", stage[:, si, 256:512], pa[:, 0:256], stage[:, si, 0:256], ALU.mult)
                        if dst == "p":
                            dma("sp", conv_p[:, b * 256:(b + 1) * 256], stage[98:128, si, 256:512], osem[si], rd=[stage[:, si, 256:512]])
                        else:
                            for s_ in range(NSEQ):
                                dma("sp", conv_s[s_, 22:30, b * 256:(b + 1) * 256], stage[8 * s_:8 * s_ + 8, si, 256:512], osem[si],
                                    rd=[stage[:, si, 256:512]])
            chk('L0 glu gi=%d' % gi)
            wv = wget(8)
            for c in range(4):
                for (c0, n) in col_tiles(0, T):
                    pb = nbank()
                    fm_mm(pb[:, 0:n], wv, c * 128, c0, n)
                    cp("act", QT[:, c, c0:c0 + n], pb[:, 0:n])
            wv = wget(8)
            for c in range(4):
                for (c0, n) in col_tiles(0, T):
                    pb = nbank()
                    fm_mm(pb[:, 0:n], wv, c * 128, c0, n)
                    npp = max(0, min(n, TP - c0))
                    if npp > 0:
                        cp("act", KT[:, c, first_tok + c0:first_tok + c0 + npp], pb[:, 0:npp])
                    if npp < n:
                        cp("act", KT[:, c, SEQ:SEQ + 128], pb[:, npp:n])
            for lt in range(ntl):
                pb = nbank()
                tm_mm(pb[:], wv, 0, 512, lt)
                si = stage_slot()
                cp("dve", stage[:, si, 0:512], pb[:])
                dst = kp[ptiles[lt] * 128:(ptiles[lt] + 1) * 128, :] if lt < npt else ks
                dma("sp", dst, stage[:, si, 0:512], osem[si], rd=[stage[:, si, 0:512]])
            wv = wget(8)
            for lt in range(ntl):
                pb = nbank()
                tm_mm(pb[:], wv, 0, 512, lt)
                si = stage_slot()
                cp("dve", stage[:, si, 0:512], pb[:])
                if lt < npt:
                    cp("act", Vall[:, ptiles[lt], :], pb[:])
                    dst = vp[ptiles[lt] * 128:(ptiles[lt] + 1) * 128, :]
                else:
                    cp("act", vs_tok, pb[:])
                    dst = vs
                dma("sp", dst, stage[:, si, 0:512], osem[si], rd=[stage[:, si, 0:512]])

            chk('L0 qkv gi=%d' % gi)
            for c in range(4):
                diag = diag2[:, c % 2]
                for j in range(31):
                    ts("dve", diag[:, j, :], identb[:], chv[:, c, j:j + 1], ALU.mult)
                for (c0, n) in col_tiles(0, TP):
                    pb = nbank()
                    for j in range(31):
                        mm(pb[:, 0:n], diag[:, j, :], extp[:, c, c0 + j:c0 + j + n], j == 0, j == 30)
                    act(a16[:, c, c0:c0 + n], pb[:, 0:n], AF.Identity, bias=chv[:, c, 31:32])
                    act(a2[:, c, c0:c0 + n], pb[:, 0:n], AF.Square, bias=chv[:, c, 31:32])
                if G["sample"]:
                    pb = nbank()
                    for j in range(31):
                        mm(pb[:, 0:128].rearrange("p (s t) -> p s t", t=8), diag[:, j, :], exts[:, c, :, j:j + 8], j == 0, j == 30)
                    act(a16[:, c, TP:T], pb[:, 0:128], AF.Identity, bias=chv[:, c, 31:32])
                    act(a2[:, c, TP:T], pb[:, 0:128], AF.Square, bias=chv[:, c, 31:32])
            if gi == 0:
                cp("pool", hist_a[:], extp[:, :, TP:TP + 30])
            cts = col_tiles(0, T)
            stat_ps = []
            for ci, (c0, n) in enumerate(cts):
                pm = banks[2 + 2 * ci]
                pv_ = banks[3 + 2 * ci]
                for c in range(4):
                    mm(pm[:, 0:n], onesb[:], a16[:, c, c0:c0 + n], c == 0, c == 3)
                for c in range(4):
                    mm(pv_[:, 0:n], onesb[:], a2[:, c, c0:c0 + n], c == 0, c == 3)
                stat_ps.append((pm, pv_))
            ti = 0
            for ci, (c0, n) in enumerate(cts):
                pm, pv_ = stat_ps[ci]
                mean = lnt[:, 2 * (ci % 2), 0:n]
                rstd = lnt[:, 2 * (ci % 2) + 1, 0:n]
                ts("dve", mean, pm[:, 0:n], 1.0 / 512, ALU.mult)
                tt("pool", rstd, mean, mean, ALU.mult)
                stt("dve", rstd, pv_[:, 0:n], 1.0 / 512, rstd, ALU.mult, ALU.subtract)
                act(rstd, rstd, AF.Ln, bias=epst[:, 0:1])
                act(rstd, rstd, AF.Exp, scale=-0.5)
                for c in range(4):
                    tmp = lntmp[:, ti % 3, 0:n]
                    ti += 1
                    tt("dve", tmp, a16[:, c, c0:c0 + n], mean, ALU.subtract)
                    tt("dve", tmp, tmp, rstd, ALU.mult)
                    act(cat[:, c, c0:c0 + n], tmp, AF.Silu, bias=chv[:, c, 33:34], scale=chv[:, c, 32:33])
            chk('L0 conformer gi=%d' % gi)
            o = R1
            ebuf = mixv(o, 3 * 512, F32).rearrange("p (a n) -> p a n", a=3); o += 6144
            xbuf = mixv(o, 2 * 512, F32).rearrange("p (a n) -> p a n", a=2); o += 4096
            spb = mixv(o, 3 * 512, BF16).rearrange("p (a n) -> p a n", a=3); o += 3072
            wb = mixv(o, 2 * 512, BF16).rearrange("p (a n) -> p a n", a=2); o += 2048
            lacc = mixv(o, 3 * 512, BF16).rearrange("p (a n) -> p a n", a=3); o += 3072
            DEC0 = o
            SMP = G["sample"]
            its = []
            qgroups = [ptiles[i:i + 4] for i in range(0, npt, 4)]
            for qg in qgroups:
                gq0, gq1 = qg[0], qg[-1]
                NQ = len(qg) * 128
                ql0 = (gq0 - ptiles[0]) * 128
                for hp in range(4):
                    for hh in range(2):
                        for j in range(gq1, -1, -1):
                            its.append(dict(gq0=gq0, gq1=gq1, NQ=NQ, ql0=ql0, hp=hp, hh=hh, j=j,
                                            first=(j == gq1), last=(j == 0), i=len(its),
                                            off=(max(gq0, j) - gq0) * 128))

            def stageA(I):
                i, off, NQ, hp, hh, j = I["i"], I["off"], I["NQ"], I["hp"], I["hh"], I["j"]
                hsl = slice(hh * 64, hh * 64 + 64)
                hd = 2 * hp + hh
                pS = banks[2 + i % 2]
                k3 = i % 3
                diagblk = j >= I["gq0"]
                mm(pS[:, off:NQ], KT[hsl, hp, j * 128:(j + 1) * 128], QT[hsl, hp, I["ql0"] + off:I["ql0"] + NQ], True, not diagblk)
                if diagblk:
                    mm(pS[:, off:off + 128], identb[:], negm[:], False, True)
                act(ebuf[:, k3, off:NQ], pS[:, off:NQ], AF.Exp, bias=sbias[:, hd:hd + 1], scale=0.125)
                act(spb[:, k3, off:NQ], ebuf[:, k3, off:NQ], AF.Ln, bias=1.0)
                if not I["last"]:
                    ln_ = (i + 1) % 3
                    le = "dve" if SMP else "pool"
                    if off > 0:
                        mset(le, lacc[:, ln_, 0:off], 0.0)
                    if I["first"]:
                        cp(le, lacc[:, ln_, off:NQ], spb[:, k3, off:NQ])
                    else:
                        tt(le, lacc[:, ln_, off:NQ], lacc[:, i % 3, off:NQ], spb[:, k3, off:NQ], ALU.add)

            def stageB(I):
                i, off, NQ, hp, hh, j = I["i"], I["off"], I["NQ"], I["hp"], I["hh"], I["j"]
                pA = banks[4] if SMP else banks[4 + i % 2]
                k3 = i % 3
                k2 = i % 2
                mm(pA[:, off:NQ], tinc[:], spb[:, k3, off:NQ], True, I["first"])
                if not I["first"]:
                    mm(pA[:, off:NQ], onesb[:], lacc[:, i % 3, off:NQ], False, True)
                act(xbuf[:, k2, off:NQ], pA[:, off:NQ], AF.Exp, scale=-1.0)
                tt("dve", wb[:, k2, off:NQ], ebuf[:, k3, off:NQ], xbuf[:, k2, off:NQ], ALU.mult)

            def stageC(I):
                i, off, NQ, hp, hh, j = I["i"], I["off"], I["NQ"], I["hp"], I["hh"], I["j"]
                hsl = slice(hh * 64, hh * 64 + 64)
                hd = 2 * hp + hh
                pB = banks[5] if SMP else banks[6 + (hp % 2)]
                k2 = i % 2
                mm(pB[hsl, off:NQ], Vall[:, j, hd * 64:(hd + 1) * 64], wb[:, k2, off:NQ], I["first"], I["last"],
                   tp=(0, 64) if hh == 1 else None, sgc=True)
                if I["last"] and hh == 1:
                    cp("dve", cat[:, 4 + hp, I["ql0"]:I["ql0"] + NQ], pB[:, 0:NQ])

            dec_hook = None
            if SMP:
                sc0 = TP
                o = DEC0
                qblk = mixv(o, 4 * 16 * 16, BF16).rearrange("p (c s x) -> p c s x", c=4, s=16); o += 2048
                kring = mixv(o, 8 * 512, BF16).rearrange("p (a f) -> p a f", a=8); o += 8192
                ktp = mixv(o, 2 * 512, BF16).rearrange("p (a f) -> p a f", a=2); o += 2048
                vring = mixv(o, 8 * 512, BF16).rearrange("p (a f) -> p a f", a=8)
                eN = mixv(o, 1024, F32)
                o += 8192
                zbb = mixv(o, 2 * 512, F32).rearrange("p (a n) -> p a n", a=2)
                xdN = mixv(o, 1024, F32)
                o += 4096
                xd = mixv(o, 512, F32); o += 2048
                spdb = mixv(o, 2 * 512, BF16).rearrange("p (a n) -> p a n", a=2); o += 2048
                wd = mixv(o, 512, BF16); o += 1024
                spN = mixv(o, 1024, BF16); o += 2048
                wN = mixv(o, 1024, BF16); o += 2048
                assert o <= MIXB, o
                ksem = [P.dsem() for _ in range(8)]
                vsem = [P.dsem() for _ in range(8)]
                mset("pool", qblk[:], 0.0)
                for hp in range(4):
                    cp("pool", qblk[0:64, hp, :, 0:8], QT[0:64, hp, sc0:sc0 + 128].rearrange("p (s t) -> p s t", t=8))
                    cp("pool", qblk[64:128, hp, :, 8:16], QT[64:128, hp, sc0:sc0 + 128].rearrange("p (s t) -> p s t", t=8))
                pTv = banks[0][:].bitcast(BF16)
                pSd = banks[1]
                pAd = banks[6]
                pBd = banks[7]
                pS2 = [banks[1], banks[2]]
                pA2 = [banks[3], banks[4]]
                for hb in range(2):
                    for hd in range(4 * hb, 4 * hb + 4):
                        hp, hh = hd // 2, hd % 2
                        hsl = slice(hh * 64, hh * 64 + 64)
                        dstp = pS2[hb][:, (hd % 4) * 128:(hd % 4 + 1) * 128]
                        mm(dstp, KT[hsl, hp, SEQ:SEQ + 128], QT[hsl, hp, sc0:sc0 + 128], hd % 4 == 0, False, sgc=True)
                        mm(dstp, identb[:], maskN[:], False, True, sgc=True)
                    for hd in range(4 * hb, 4 * hb + 4):
                        dstp = pS2[hb][:, (hd % 4) * 128:(hd % 4 + 1) * 128]
                        act(eN[:, hd * 128:(hd + 1) * 128], dstp, AF.Exp, bias=sbias[:, hd:hd + 1], scale=0.125)
                act(spN, eN, AF.Ln, bias=1.0)
                for bq in range(2):
                    mm(pA2[bq][:], tinc[:], spN[:, bq * 512:(bq + 1) * 512], True, True)
                    act(xdN[:, bq * 512:(bq + 1) * 512], pA2[bq][:], AF.Exp, scale=-1.0)
                tt("dve", wN, eN, xdN, ALU.mult)
                spN3 = spN.rearrange("p (h c) -> p h c", h=8)
                wN3 = wN.rearrange("p (h c) -> p h c", h=8)
                NU = 2 * NSEQ

                def pages(u):
                    return (u // 2, 8 if u % 2 == 0 else 0)

                def M1(u):
                    s_, p0 = pages(u)
                    for jl in range(8):
                        col = s_ * NPAGE + p0 + jl
                        kdst = kring[:, jl, :]
                        P.add("pool", lambda e, kdst=kdst, col=col: e.indirect_dma_start(
                            out=kdst, out_offset=None, in_=ck,
                            in_offset=bass.IndirectOffsetOnAxis(ap=pidx[:, col:col + 1], axis=0)),
                            [V(pidx[:, col:col + 1])], [V(kdst)], dsem=ksem[jl])

                def MV(u):
                    s_, p0 = pages(u)
                    for jl in range(8):
                        col = s_ * NPAGE + p0 + jl
                        vdst = vring[:, jl, :]
                        P.add("pool", lambda e, vdst=vdst, col=col: e.indirect_dma_start(
                            out=vdst, out_offset=None, in_=cv,
                            in_offset=bass.IndirectOffsetOnAxis(ap=pidx[:, col:col + 1], axis=0)),
                            [V(pidx[:, col:col + 1])], [V(vdst)], dsem=vsem[jl])

                def M2(u):
                    s_, p0 = pages(u)

                    def smm(jl):
                        half = jl % 2
                        for hp in range(4):
                            mm(pSd[:, jl * 64 + hp * 16:jl * 64 + hp * 16 + 16],
                               ktp[:, half, hp * 128:(hp + 1) * 128], qblk[:, hp, s_, :], True, True)
                    for jl in range(8):
                        half = jl % 2
                        for c in range(4):
                            tr(pTv[:, half * 512 + c * 128: half * 512 + (c + 1) * 128], kring[:, jl, c * 128:(c + 1) * 128], identb[:])
                        cp("dve", ktp[:, half, :], pTv[:, half * 512:(half + 1) * 512])
                        if jl >= 1:
                            smm(jl - 1)
                    smm(7)

                def M3a(u):
                    hf = u % 2
                    zb = zbb[:, hf, :]
                    spd = spdb[:, hf, :]
                    stt("dve", zb.rearrange("p (j c) -> p j c", c=64), pSd[:].rearrange("p (j c) -> p j c", c=64),
                        0.125, biasd[:].unsqueeze(1).broadcast_to([128, 8, 64]), ALU.mult, ALU.add)
                    act(zb, zb, AF.Exp)
                    act(spd, zb, AF.Ln, bias=1.0)

                def M3b(u):
                    s_, p0 = pages(u)
                    hf = u % 2
                    spd = spdb[:, hf, :]
                    mm(pAd[:], tinc[:], spd, True, False)
                    for jp in range(1, 8):
                        mm(pAd[:, 0:jp * 64].rearrange("p (j c) -> p j c", c=64), onesb[:],
                           spd[:, jp * 64:(jp + 1) * 64].unsqueeze(1).broadcast_to([128, jp, 64]), False, False)
                    if hf == 1:
                        sph = spdb[:, 0, :]
                        for jp in range(8):
                            mm(pAd[:].rearrange("p (j c) -> p j c", c=64), onesb[:],
                               sph[:, jp * 64:(jp + 1) * 64].unsqueeze(1).broadcast_to([128, 8, 64]), False, False)
                    newc = spN3[:, :, 8 * s_:8 * s_ + 8].unsqueeze(1).broadcast_to([128, 8, 8, 8])
                    mm(pAd[:].rearrange("p (j h q) -> p j h q", j=8, h=8), onesb[:], newc, False, True)

                def M4a(u):
                    hf = u % 2
                    zb = zbb[:, hf, :]
                    act(xd, pAd[:], AF.Exp, scale=-1.0)
                    tt("dve", wd, zb, xd, ALU.mult)

                def M4b(u):
                    s_, p0 = pages(u)
                    hf = u % 2
                    for jl in range(8):
                        for hp in range(4):
                            mm(pBd[:, hp * 16:hp * 16 + 16], vring[:, jl, hp * 128:(hp + 1) * 128],
                               wd[:, jl * 64 + hp * 16:jl * 64 + hp * 16 + 16], hf == 0 and jl == 0 and hp == 0, False, sgc=True)
                    if hf == 1:
                        for hp in range(4):
                            mm(pBd[:, hp * 16:hp * 16 + 16].rearrange("p (a q) -> p a q", a=2), vs_tok[:, hp * 128:(hp + 1) * 128],
                               wN3[:, 2 * hp:2 * hp + 2, 8 * s_:8 * s_ + 8], False, hp == 3, sgc=True)
                        pbv4 = pBd[:, 0:64].rearrange("p (c x) -> p c x", c=4)
                        cp("dve", cat[0:64, 4:8, sc0 + 8 * s_:sc0 + 8 * s_ + 8], pbv4[0:64, :, 0:8])
                        cp("dve", cat[64:128, 4:8, sc0 + 8 * s_:sc0 + 8 * s_ + 8], pbv4[64:128, :, 8:16])

                NMS = NU + 4
                IPM = max(5, -(-(len(its) + 2) // NMS))

                def micro(m, part):
                    if part == 0:
                        if 0 <= m - 2 < NU:
                            M3b(m - 2)
                    elif part == 1:
                        if 0 <= m - 3 < NU:
                            M4b(m - 3)
                        if 0 <= m - 2 < NU:
                            MV(m - 2)
                    elif part == 2:
                        if 0 <= m - 1 < NU:
                            M2(m - 1)
                        if m < NU:
                            M1(m)
                    elif part == 3:
                        if 0 <= m - 2 < NU:
                            M4a(m - 2)
                    elif part == 4:
                        if 0 <= m - 1 < NU:
                            M3a(m - 1)

                dstate = {"m": 0, "part": 0}

                def dec_hook(step, flush=False):
                    while dstate["m"] < NMS:
                        m, part = dstate["m"], dstate["part"]
                        due = m * IPM + min(part, IPM - 1)
                        if not flush and due > step:
                            break
                        micro(m, part)
                        if part == 4:
                            dstate["m"] += 1
                            dstate["part"] = 0
                        else:
                            dstate["part"] += 1

            n_it = len(its)
            for step in range(n_it + 2):
                if dec_hook is not None:
                    dec_hook(step)
                if step < n_it:
                    stageA(its[step])
                if 0 <= step - 1 < n_it:
                    stageB(its[step - 1])
                if 0 <= step - 2 < n_it:
                    stageC(its[step - 2])
            if dec_hook is not None:
                dec_hook(0, flush=True)

            chk('L0 decode gi=%d' % gi)
            residual_proj(8, cat)
            ffn(0)

            chk('L0 ffn gi=%d' % gi)
            norm_T(2)
            o = 0
            gbb = mixv(o, 4 * T, BF16).rearrange("p (c t) -> p c t", c=4); o += 4 * T * 2
            cxp = mixv(o, 4 * (2 + TP), BF16).rearrange("p (c t) -> p c t", c=4); o += 4 * (2 + TP) * 2
            upx = mixv(o, 4 * (16 + TP), BF16).rearrange("p (c t) -> p c t", c=4)[:, :, 1:16 + TP]; o += 4 * (16 + TP) * 2
            if G["sample"]:
                cxs = mixv(o, 4 * 16 * 10, BF16).rearrange("p (c s t) -> p c s t", c=4, s=16); o += 4 * 160 * 2
                usx = mixv(o, 4 * 16 * 24, BF16).rearrange("p (c s t) -> p c s t", c=4, s=16)[:, :, :, 1:24]; o += 4 * 16 * 24 * 2
            gct = mixv(o, 2 * 512, BF16).rearrange("p (a n) -> p a n", a=2); o += 2048
            acc = mixv(o, 16 + T, F32); o += (16 + T) * 4
            acc2 = mixv(o, 16 + T, F32); o += (16 + T) * 4
            pooled = mixv(o, T, BF16); o += T * 2
            sts = mixv(o, 2 * 512, F32).rearrange("p (a n) -> p a n", a=2); o += 4096
            assert o <= MIXB, o
            if gi == 0:
                mset("pool", cxp[:, :, 0:2], 0.0)
                mset("pool", upx[:, :, 0:15], 0.0)
            else:
                cp("pool", cxp[:, :, 0:2], hist_c[:])
                cp("pool", upx[:, :, 0:15], hist_u[:])
                lload(sts[0:32, 0, :], st_sc)
                pb = nbank()
                for c in range(4):
                    tr(pb[:, c * 128:c * 128 + 32], sts[0:32, 0, c * 128:(c + 1) * 128], identf[0:32, 0:32])
                cp("dve", cxs[:, :, :, 0:2], pb[:].rearrange("p (c x) -> p c x", c=4)[:, :, 0:32].rearrange("p c (s r) -> p c s r", s=16))
                for i in range(2):
                    lload(sts[0:120, 1, :], st_pool[i * 120:(i + 1) * 120, :])
                    pb = nbank()
                    for c in range(4):
                        tr(pb[:, c * 128:c * 128 + 120], sts[0:120, 1, c * 128:(c + 1) * 128], identf[0:120, 0:120])
                    cp("dve", usx[:, :, 8 * i:8 * i + 8, 0:15],
                       pb[:].rearrange("p (c x) -> p c x", c=4)[:, :, 0:120].rearrange("p c (s r) -> p c s r", s=8))
                P.add("sp", lambda e: e.dma_start(out=pool_s[:, 0:7, :], in_=st_pool.rearrange("(s r) f -> s r f", r=15)[:, 8:15, :]),
                      [], [], dsem=P.dsem(out=True))

            chk('L1 norm gi=%d' % gi)
            for b in range(2):
                wv = wget(8)
                for fc in range(2):
                    c = 2 * b + fc
                    for (c0, n) in col_tiles(0, T):
                        pa = nbank()
                        pg = nbank()
                        fm_mm(pa[:, 0:n], wv, fc * 128, c0, n)
                        fm_mm(pg[:, 0:n], wv, 256 + fc * 128, c0, n)
                        k = bank_rr["n"] % 2
                        cp("act", gct[:, k, 0:n], pa[:, 0:n])
                        npp = max(0, min(n, TP - c0))
                        if npp > 0:
                            tt("dve", cxp[:, c, 2 + c0:2 + c0 + npp], pg[:, 0:npp], gct[:, k, 0:npp], ALU.mult)
                        if npp < n:
                            tt("dve", cxs[:, c, :, 2:10], pg[:, npp:n].rearrange("p (s t) -> p s t", t=8),
                               gct[:, k, npp:n].rearrange("p (s t) -> p s t", t=8), ALU.mult)
                if gi == 1:
                    for lt, dst in ((npt - 1, "p"), (npt, "s")):
                        pa = nbank()
                        tm_mm(pa[:], wv, 0, 512, lt)
                        si = stage_slot()
                        cp("act", stage[:, si, 0:256], pa[:, 0:256])
                        tt("dve", stage[:, si, 256:512], pa[:, 256:512], stage[:, si, 0:256], ALU.mult)
                        if dst == "p":
                            dma("sp", sc_p[:, b * 256:(b + 1) * 256], stage[126:128, si, 256:512], osem[si], rd=[stage[:, si, 256:512]])
                        else:
                            for s_ in range(NSEQ):
                                dma("sp", sc_s[s_, :, b * 256:(b + 1) * 256], stage[8 * s_ + 6:8 * s_ + 8, si, 256:512], osem[si],
                                    rd=[stage[:, si, 256:512]])
            for b in range(2):
                wv = wget(8)
                for fc in range(2):
                    c = 2 * b + fc
                    for (c0, n) in col_tiles(0, T):
                        pa = nbank()
                        pu = nbank()
                        fm_mm(pa[:, 0:n], wv, fc * 128, c0, n)
                        fm_mm(pu[:, 0:n], wv, 256 + fc * 128, c0, n)
                        cp("act", gbb[:, c, c0:c0 + n], pa[:, 0:n])
                        npp = max(0, min(n, TP - c0))
                        if npp > 0:
                            cp("dve", upx[:, c, 15 + c0:15 + c0 + npp], pu[:, 0:npp])
                        if npp < n:
                            cp("dve", usx[:, c, :, 15:23], pu[:, npp:n].rearrange("p (s t) -> p s t", t=8))
                if gi == 1:
                    for lt, dst in ((npt - 1, "p"), (npt, "s")):
                        pa = nbank()
                        tm_mm(pa[:, 0:256], wv, 256, 256, lt)
                        si = stage_slot()
                        cp("dve", stage[:, si, 0:256], pa[:, 0:256])
                        if dst == "p":
                            dma("sp", pool_p[:, b * 256:(b + 1) * 256], stage[113:128, si, 0:256], osem[si], rd=[stage[:, si, 0:256]])
                        else:
                            for s_ in range(NSEQ):
                                dma("sp", pool_s[s_, 7:15, b * 256:(b + 1) * 256], stage[8 * s_:8 * s_ + 8, si, 0:256], osem[si],
                                    rd=[stage[:, si, 0:256]])
            if gi == 0:
                cp("pool", hist_c[:], cxp[:, :, TP:TP + 2])
                cp("pool", hist_u[:], upx[:, :, TP:TP + 15])

            chk('L1 proj gi=%d' % gi)
            for c in range(4):
                a_ = acc[:, 0:TP]
                ts("dve", a_, cxp[:, c, 0:TP], chv[:, c, 34:35], ALU.mult)
                stt("dve", a_, cxp[:, c, 1:1 + TP], chv[:, c, 35:36], a_, ALU.mult, ALU.add)
                stt("dve", a_, cxp[:, c, 2:2 + TP], chv[:, c, 36:37], a_, ALU.mult, ALU.add)
                tt("dve", cat[:, c, 0:TP], a_, gbb[:, c, 0:TP], ALU.mult)
                if G["sample"]:
                    a3 = acc2[:, 0:128].rearrange("p (s t) -> p s t", t=8)
                    ts("dve", a3, cxs[:, c, :, 0:8], chv[:, c, 34:35], ALU.mult)
                    stt("dve", a3, cxs[:, c, :, 1:9], chv[:, c, 35:36], a3, ALU.mult, ALU.add)
                    stt("dve", a3, cxs[:, c, :, 2:10], chv[:, c, 36:37], a3, ALU.mult, ALU.add)
                    tt("dve", cat[:, c, TP:T], acc2[:, 0:128], gbb[:, c, TP:T], ALU.mult)
            for g, w in enumerate((2, 4, 8, 16)):
                L = 15 + TP
                src = upx[:, g, 0:L]
                cur, nxt = acc[:, 0:L], acc2[:, 0:L]
                sh = 1
                first = True
                while sh < w:
                    a_in = src if first else cur
                    eng = "dve"
                    if not first:
                        cp(eng, nxt[:, 0:sh], a_in[:, 0:sh])
                    tt(eng, nxt[:, sh:L], a_in[:, sh:L], a_in[:, 0:L - sh], ALU.add)
                    cur, nxt = nxt, cur
                    first = False
                    sh *= 2
                stt("dve", pooled[:, 0:TP], cur[:, 15:15 + TP], 1.0 / w, upx[:, g, 15:15 + TP], ALU.mult, ALU.subtract)
                if gi == 0:
                    tt("dve", nxt[:, 0:16], cur[:, 15:31], rc[:, g, :], ALU.mult)
                    tt("dve", pooled[:, 0:16], nxt[:, 0:16], upx[:, g, 15:31], ALU.subtract)
                if G["sample"]:
                    s3 = usx[:, g, :, :]
                    c3 = acc[:, 0:16 * 23].rearrange("p (s t) -> p s t", t=23)
                    n3 = acc2[:, 0:16 * 23].rearrange("p (s t) -> p s t", t=23)
                    sh = 1
                    first = True
                    while sh < w:
                        a_in = s3 if first else c3
                        if not first:
                            cp("pool", n3[:, :, 0:sh], a_in[:, :, 0:sh])
                        tt("pool", n3[:, :, sh:23], a_in[:, :, sh:23], a_in[:, :, 0:23 - sh], ALU.add)
                        c3, n3 = n3, c3
                        first = False
                        sh *= 2
                    stt("dve", pooled[:, TP:T].rearrange("p (s t) -> p s t", t=8), c3[:, :, 15:23], 1.0 / w, usx[:, g, :, 15:23],
                        ALU.mult, ALU.subtract)
                for (c0, n) in col_tiles(0, T):
                    pb = nbank()
                    mm(pb[:, 0:n], poolw[:, g, :], pooled[:, c0:c0 + n], True, True)
                    act(cat[:, 4 + g, c0:c0 + n], pb[:, 0:n], AF.Copy, scale=chv[:, g, 37:38])

            residual_proj(8, cat)
            ffn(1)

            chk('L1 mixer+ffn gi=%d' % gi)
            gslot = load_g(4)
            c0 = stat_col(3 * ntl)
            for lt in range(ntl):
                act(xn[:, lt % 2, :], h[:, lt, :], AF.Square, accum=stats[:, c0 + lt:c0 + lt + 1])
            act(stats[:, c0 + ntl:c0 + 2 * ntl], stats[:, c0:c0 + ntl], AF.Ln, bias=epst[:, 0:1], scale=1.0 / D)
            act(stats[:, c0 + 2 * ntl:c0 + 3 * ntl], stats[:, c0 + ntl:c0 + 2 * ntl], AF.Exp, scale=-0.5)
            for lt in range(ntl):
                hv = h[:, lt, :]
                si = stage_slot()
                stt("dve", stage[:, si, :], hv, stats[:, c0 + 2 * ntl + lt:c0 + 2 * ntl + lt + 1], gb[:, gslot, :], ALU.mult, ALU.mult)
                dst = yp[ptiles[lt] * 128:(ptiles[lt] + 1) * 128, :] if lt < npt else ys
                dma("sp", dst, stage[:, si, :], osem[si], rd=[stage[:, si, :]])

    try:
        chk('consts')
        run_groups()
    except _Stop as e_:
        print('STOPPED at', e_)
    if limit is None:
        assert wstate["cur"] == len(WSPECS), (wstate, len(WSPECS))
    P.emit()
    es.close()
    return nc


def make_in_maps(inp, compact_cache=False):
    f = lambda a: np.ascontiguousarray(np.asarray(a))
    gvec = f(np.stack([inp["norm_mix_g"][0], inp["norm_ffn_g"][0], inp["norm_mix_g"][1], inp["norm_ffn_g"][1],
                       inp["norm_final_g"]]))
    chv = f(np.concatenate([inp["conv_a_w"][0], inp["conv_a_b"], inp["ln_a_g"], inp["ln_a_b"], inp["conv_c_w"][0],
                            inp["pool_scale"]], axis=0))
    shared = {
        "gvec": gvec, "chv": chv, "sbb": f(inp["sb_bias"]).reshape(1, 8),
        "poolw": f(inp["pool_w"]).reshape(512, 128),
        "w_in_e": f(inp["w_in_even"][0]), "w_out_e": f(inp["w_out_even"][0]),
        "w_in_o": f(inp["w_in_odd"][0]), "w_out_o": f(inp["w_out_odd"][0]),
        "w_ffn_in": f(inp["w_ffn_in"]).reshape(2 * D, 2 * FFN), "w_ffn_out": f(inp["w_ffn_out"]).reshape(2 * FFN, D),
    }
    ck_full = np.asarray(inp["cache_k"])[0].reshape(-1, 512)
    cv_full = np.asarray(inp["cache_v"])[0].reshape(-1, 512)
    maps = []
    for c in range(NCORES):
        sl = slice(NSEQ * c, NSEQ * (c + 1))
        m = dict(shared)
        m["xp"] = f(inp["x_prompt"][c])
        m["xs"] = f(inp["x_sample"][sl]).reshape(128, D)
        m["st_conf"] = f(inp["state_conformer"][0, sl]).reshape(NSEQ * 30, 512)
        m["st_sc"] = f(inp["state_shortconv"][0, sl]).reshape(NSEQ * 2, 512)
        m["st_pool"] = f(inp["state_pool"][0, sl]).reshape(NSEQ * 15, 512)
        ptc = np.asarray(inp["page_table"])[sl].astype(np.int32)
        if compact_cache:
            pages = ptc.reshape(-1)
            m["ck"] = f(ck_full.reshape(-1, 128, 512)[pages]).reshape(-1, 512)
            m["cv"] = f(cv_full.reshape(-1, 128, 512)[pages]).reshape(-1, 512)
            m["pt"] = np.arange(256, dtype=np.int32).reshape(NSEQ, NPAGE)
        else:
            m["ck"] = ck_full
            m["cv"] = cv_full
            m["pt"] = f(ptc)
        maps.append(m)
    return maps


def gather_outputs(res):
    R = res.results
    cat = lambda k: np.stack([R[c][k] for c in range(NCORES)])
    y_p = cat("yp")
    y_s = np.concatenate([R[c]["ys"].reshape(NSEQ, DSEQ, D) for c in range(NCORES)])
    k_p = cat("kp").reshape(1, NCORES, SEQ, 8, 64)
    v_p = cat("vp").reshape(1, NCORES, SEQ, 8, 64)
    k_s = np.concatenate([R[c]["ks"].reshape(NSEQ, DSEQ, 8, 64) for c in range(NCORES)])[None]
    v_s = np.concatenate([R[c]["vs"].reshape(NSEQ, DSEQ, 8, 64) for c in range(NCORES)])[None]
    conv_p = cat("conv_p")[None]
    conv_s = np.concatenate([R[c]["conv_s"] for c in range(NCORES)])[None]
    sc_p = cat("sc_p")[None]
    sc_s = np.concatenate([R[c]["sc_s"] for c in range(NCORES)])[None]
    pool_p = cat("pool_p")[None]
    pool_s = np.concatenate([R[c]["pool_s"] for c in range(NCORES)])[None]
    outs = (y_p, y_s, k_p, v_p, k_s, v_s, conv_p, conv_s, sc_p, sc_s, pool_p, pool_s)
    return tuple(np.ascontiguousarray(o, dtype=np.float32) for o in outs)


def kernel(**inputs):
    npool = int(np.asarray(inputs["cache_k"]).shape[1])
    nc = build_nc(npool)
    in_maps = make_in_maps(inputs)
    res = run_bass_kernel_spmd(nc, in_maps, core_ids=list(range(NCORES)))
    return gather_outputs(res)
```

```python
import numpy as np
from contextlib import ExitStack
import concourse.bass as bass
import concourse.mybir as mybir
from concourse.bass_utils import run_bass_kernel_spmd

F32, BF16, I32 = mybir.dt.float32, mybir.dt.bfloat16, mybir.dt.int32
AF = mybir.ActivationFunctionType
ALU = mybir.AluOpType
ESZ = {F32: 4, BF16: 2, I32: 4}

NCORES = 8
D = 1024
SEQ = 2048
NSEQ = 16
DSEQ = 8
NPAGE = 16
FFN = 2816
NEG = -240000.0
EPS = 1e-6


class V:
    __slots__ = ("ap", "key", "lo", "hi")

    def __init__(self, ap):
        self.ap = ap
        dims = ap.ap
        pstep = dims[0][0]
        off = ap.offset % pstep if pstep > 0 else ap.offset
        ext = 1
        for st, cnt in dims[1:]:
            ext += (cnt - 1) * abs(st)
        es = ESZ[ap.dtype]
        self.key = ap.tensor.name
        self.lo = off * es
        self.hi = (off + ext) * es
        if str(ap.space) == "PSUM":
            self.lo, self.hi = 0, 2048


class DSem:
    def __init__(self, sem, group=False):
        self.sem = sem
        self.count = 0
        self.group = group


class Op:
    __slots__ = ("eng", "fn", "deps", "sig", "cnt", "isdma", "dsem", "dval", "idx")


class Prog:
    ENGS = ("pe", "act", "dve", "pool", "sp")

    def __init__(self, nc, es):
        self.nc = nc
        self.es = es
        self.q = {e: [] for e in self.ENGS}
        self.recs = {}
        self.esem = {e: es.enter_context(nc.semaphore("sem_" + e)) for e in ("pe", "act", "dve", "pool")}
        self.nsem = 0
        self.out_dsems = []

    def dsem(self, group=False, out=False):
        self.nsem += 1
        d = DSem(self.es.enter_context(self.nc.semaphore("dsem%d" % self.nsem)), group)
        self.out_dsems.append(d)
        return d

    def _access(self, v, op, is_write):
        L = self.recs.get(v.key)
        if L is None:
            L = self.recs[v.key] = []
        raw, other = [], []
        newL = []
        lo, hi = v.lo, v.hi
        cov = []
        for rec in L:
            rlo, rhi, w, rd = rec
            if rhi <= lo or rlo >= hi:
                newL.append(rec)
                continue
            if w is not None:
                (other if is_write else raw).append(w)
            if is_write:
                other.extend(rd.values())
            if rlo < lo:
                newL.append([rlo, lo, w, dict(rd)])
            if rhi > hi:
                newL.append([hi, rhi, w, dict(rd)])
            if not is_write:
                nrd = dict(rd)
                nrd[("d", id(op)) if op.isdma else op.eng] = op
                a, b = max(rlo, lo), min(rhi, hi)
                newL.append([a, b, w, nrd])
                cov.append((a, b))
        if is_write:
            newL.append([lo, hi, op, {}])
        else:
            cov.sort()
            cur = lo
            k = ("d", id(op)) if op.isdma else op.eng
            for a, b in cov:
                if a > cur:
                    newL.append([cur, a, None, {k: op}])
                cur = max(cur, b)
            if cur < hi:
                newL.append([cur, hi, None, {k: op}])
        self.recs[v.key] = newL
        return raw, other

    def add(self, eng, fn, reads=(), writes=(), dsem=None):
        op = Op()
        op.eng = eng
        op.fn = fn
        op.isdma = dsem is not None
        op.sig = False
        op.cnt = 0
        op.idx = len(self.q[eng])
        raw, other = [], []
        for v in reads:
            r, o = self._access(v, op, v.key.startswith("ps"))
            raw += r
            other += o
        for v in writes:
            r, o = self._access(v, op, True)
            raw += r
            other += o
        deps = {}
        for lst, is_raw in ((raw, True), (other, False)):
            for d in lst:
                if d is op:
                    continue
                if d.isdma:
                    deps[("dsem", id(d.dsem))] = (d, d.dsem.count)
                    continue
                if (not op.isdma) and d.eng == eng:
                    if eng == "pe":
                        continue
                k = d.eng
                if k not in deps or deps[k].idx < d.idx:
                    deps[k] = d
        op.deps = [x if isinstance(x, tuple) else (x, None) for x in deps.values()]
        for d, _ in op.deps:
            if not d.isdma:
                d.sig = True
        if op.isdma:
            dsem.count += 16
            op.dsem = dsem
            op.dval = dsem.count
        self.q[eng].append(op)
        return op

    def emit(self):
        nc = self.nc
        for eng, L in self.q.items():
            c = 0
            for op in L:
                if (not op.isdma) and op.sig:
                    c += 1
                    op.cnt = c
        engobj = {"pe": "tensor", "act": "scalar", "dve": "vector", "pool": "gpsimd", "sp": "sync"}
        with nc.Block() as block:
            for eng in self.ENGS:
                def body(e, eng=eng):
                    waited = {}
                    for op in self.q[eng]:
                        for d, dv in op.deps:
                            if d.isdma:
                                sem = d.dsem.sem
                                val = d.dsem.count if d.dsem.group else dv
                            else:
                                sem = self.esem[d.eng]
                                val = d.cnt
                            if waited.get(sem.num, 0) < val:
                                e.wait_ge(sem, val)
                                waited[sem.num] = val
                        ins = op.fn(e)
                        if op.isdma:
                            ins.then_inc(op.dsem.sem, 16)
                        elif op.sig:
                            ins.then_inc(self.esem[eng], 1)
                    if eng == "sp":
                        for d in self.out_dsems:
                            if d.count > 0:
                                e.wait_ge(d.sem, d.count)
                getattr(block, engobj[eng])(body)


class _Stop(Exception):
    pass


def build_nc(npool, limit=None):
    nc = bass.Bass("TRN2", target_bir_lowering=False)
    es = ExitStack()
    P = Prog(nc, es)

    def din(name, shape, dt=F32):
        return nc.dram_tensor(name, list(shape), dt, kind="ExternalInput").ap()

    def dout(name, shape, dt=F32):
        return nc.dram_tensor(name, list(shape), dt, kind="ExternalOutput").ap()

    xp = din("xp", [SEQ, D])
    xs = din("xs", [128, D])
    ck = din("ck", [npool * 128, 512])
    cv = din("cv", [npool * 128, 512])
    st_conf = din("st_conf", [NSEQ * 30, 512])
    st_sc = din("st_sc", [NSEQ * 2, 512])
    st_pool = din("st_pool", [NSEQ * 15, 512])
    pt = din("pt", [NSEQ, NPAGE], I32)
    gvec = din("gvec", [5, D])
    chvd = din("chv", [38, 512])
    sbb = din("sbb", [1, 8])
    poolwd = din("poolw", [512, 128])
    w_in_e = din("w_in_e", [D, 2560])
    w_out_e = din("w_out_e", [D, D])
    w_in_o = din("w_in_o", [D, 2048])
    w_out_o = din("w_out_o", [D, D])
    w_ffn_in = din("w_ffn_in", [2 * D, 2 * FFN])
    w_ffn_out = din("w_ffn_out", [2 * FFN, D])

    yp = dout("yp", [SEQ, D])
    ys = dout("ys", [128, D])
    kp = dout("kp", [SEQ, 512])
    vp = dout("vp", [SEQ, 512])
    ks = dout("ks", [128, 512])
    vs = dout("vs", [128, 512])
    conv_p = dout("conv_p", [30, 512])
    conv_s = dout("conv_s", [NSEQ, 30, 512])
    sc_p = dout("sc_p", [2, 512])
    sc_s = dout("sc_s", [NSEQ, 2, 512])
    pool_p = dout("pool_p", [15, 512])
    pool_s = dout("pool_s", [NSEQ, 15, 512])

    def sb(name, shape, dt):
        return es.enter_context(nc.sbuf_tensor(name, list(shape), dt))

    def ps(name):
        return es.enter_context(nc.psum_tensor(name, [128, 512], F32))

    TMAX = 1152
    h = sb("h", [128, 9, D], F32)
    actA = sb("actA", [128, 8, TMAX], BF16)
    KT = sb("KT", [128, 4, SEQ + 128], BF16)
    Vall = sb("Vall", [128, 16, 512], BF16)
    wslot = sb("wslot", [128, 2, 6144], BF16)
    gb = sb("gb", [128, 2, D], F32)
    stage = sb("stage", [128, 2, D], F32)
    xn = sb("xn", [128, 2, D], BF16)
    identb = sb("identb", [128, 128], BF16)
    identf = sb("identf", [128, 128], F32)
    tinc = sb("tinc", [128, 128], BF16)
    onesb = sb("onesb", [128, 128], BF16)
    negm = sb("negm", [128, 128], BF16)
    onesf = sb("onesf", [128, 128], F32)
    chv = sb("chvs", [128, 4, 38], F32)
    chraw = sb("chraw", [38, 512], F32)
    sbias = sb("sbias", [128, 8], F32)
    biasd = sb("biasd", [128, 64], F32)
    epst = sb("epst", [128, 1], F32)
    poolw = sb("poolws", [128, 4, 128], BF16)
    rc = sb("rc", [128, 4, 16], F32)
    stats = sb("stats", [128, 64], F32)
    ptb = sb("ptb", [128, NSEQ * NPAGE], I32)
    pidx = sb("pidx", [128, NSEQ * NPAGE], I32)
    iot = sb("iot", [128, 1], F32)
    hist_a = sb("hist_a", [128, 4, 30], BF16)
    hist_c = sb("hist_c", [128, 4, 2], BF16)
    hist_u = sb("hist_u", [128, 4, 15], BF16)
    MIXB = 66 * 1024
    mix = sb("mix", [128, MIXB // 2], BF16)
    banks = [ps("ps%d" % i) for i in range(8)]

    maskN = sb("maskN", [128, 128], BF16)
    ebl = sb("ebl", [16, 128], F32)
    eblb = sb("eblb", [16, 128], BF16)

    def mixv(off_bytes, nelem, dt):
        assert off_bytes % 4 == 0
        assert off_bytes + nelem * ESZ[dt] <= MIXB, (off_bytes, nelem, MIXB)
        a = mix[:, off_bytes // 2: off_bytes // 2 + nelem * ESZ[dt] // 2]
        return a if dt == BF16 else a.bitcast(dt)

    def rv(x):
        return x if isinstance(x, V) else V(x)

    def act(out, in_, func, bias=0.0, scale=1.0, accum=None):
        out, in_ = rv(out), rv(in_)
        rd = [in_]
        wr = [out]
        kw = {}
        if hasattr(bias, "ap"):
            bias = rv(bias)
            rd.append(bias)
            kw["bias"] = bias.ap
        else:
            kw["bias"] = float(bias)
        if hasattr(scale, "ap"):
            scale = rv(scale)
            rd.append(scale)
            kw["scale"] = scale.ap
        else:
            kw["scale"] = float(scale)
        if accum is not None:
            accum = rv(accum)
            wr.append(accum)
            kw["accum_out"] = accum.ap
        return P.add("act", lambda e: e.activation(out=out.ap, in_=in_.ap, func=func, **kw), rd, wr)

    def tt(eng, out, in0, in1, op):
        out, in0, in1 = rv(out), rv(in0), rv(in1)
        return P.add(eng, lambda e: e.tensor_tensor(out=out.ap, in0=in0.ap, in1=in1.ap, op=op), [in0, in1], [out])

    def ts(eng, out, in0, s1, op0, s2=None, op1=None):
        out, in0 = rv(out), rv(in0)
        rd = [in0]
        a1 = s1
        if hasattr(s1, "ap"):
            s1 = rv(s1)
            rd.append(s1)
            a1 = s1.ap
        a2 = s2
        if s2 is not None and hasattr(s2, "ap"):
            s2 = rv(s2)
            rd.append(s2)
            a2 = s2.ap
        if op1 is None:
            return P.add(eng, lambda e: e.tensor_scalar(out=out.ap, in0=in0.ap, scalar1=a1, scalar2=None, op0=op0), rd, [out])
        return P.add(eng, lambda e: e.tensor_scalar(out=out.ap, in0=in0.ap, scalar1=a1, scalar2=a2, op0=op0, op1=op1), rd, [out])

    def stt(eng, out, in0, s, in1, op0, op1):
        out, in0, in1 = rv(out), rv(in0), rv(in1)
        rd = [in0, in1]
        a = s
        if hasattr(s, "ap"):
            s = rv(s)
            rd.append(s)
            a = s.ap
        return P.add(eng, lambda e: e.scalar_tensor_tensor(out=out.ap, in0=in0.ap, scalar=a, in1=in1.ap, op0=op0, op1=op1), rd, [out])

    def cp(eng, out, in_):
        out, in_ = rv(out), rv(in_)
        if eng == "act":
            return P.add("act", lambda e: e.copy(out=out.ap, in_=in_.ap), [in_], [out])
        return P.add(eng, lambda e: e.tensor_copy(out=out.ap, in_=in_.ap), [in_], [out])

    def mset(eng, out, val):
        out = rv(out)
        return P.add(eng, lambda e: e.memset(out.ap, val), [], [out])

    def asel(out, in_, pattern, op, base, cm):
        out, in_ = rv(out), rv(in_)
        return P.add("pool", lambda e: e.affine_select(out=out.ap, in_=in_.ap, pattern=pattern, compare_op=op, fill=0.0,
                                                       base=base, channel_multiplier=cm), [in_], [out])

    def mm(out, lhsT, rhs, start, stop, tp=None, sgc=False):
        out, lhsT, rhs = rv(out), rv(lhsT), rv(rhs)
        kw = {}
        if tp is not None:
            kw["tile_position"] = tp
        if sgc:
            kw["skip_group_check"] = True
        rd = [lhsT, rhs] + ([] if start else [out])
        return P.add("pe", lambda e: e.matmul(out.ap, lhsT=lhsT.ap, rhs=rhs.ap, start=start, stop=stop, **kw), rd, [out])

    def tr(out, in_, ident):
        out, in_, ident = rv(out), rv(in_), rv(ident)
        return P.add("pe", lambda e: e.transpose(out.ap, in_.ap, ident.ap), [in_, ident], [out])

    def dma(q, out, in_, dsem, rd=(), wr=()):
        return P.add(q, lambda e: e.dma_start(out=out, in_=in_), [rv(x) for x in rd], [rv(x) for x in wr], dsem=dsem)

    ckstate = {"n": 0}

    def chk(name):
        ckstate["n"] += 1
        if limit is not None and ckstate["n"] > limit:
            raise _Stop(name)

    cdsem = P.dsem(group=True)

    def cload(out_ap, in_ap):
        dma("sp", out_ap, in_ap, cdsem, wr=[out_ap])

    def lload(out_ap, in_ap):
        dma("sp", out_ap, in_ap, P.dsem(), wr=[out_ap])

    wsems = [[P.dsem(), P.dsem()], [P.dsem(), P.dsem()]]
    WSPECS = []

    def wspec_layer0():
        for b in range(2):
            WSPECS.append([(w_in_e[:, b * 256:(b + 1) * 256], 8, 0, 256, 512),
                           (w_in_e[:, 512 + b * 256:512 + (b + 1) * 256], 8, 256, 256, 512)])
        for q0 in (1024, 1536, 2048):
            WSPECS.append([(w_in_e[:, q0:q0 + 512], 8, 0, 512, 512)])

    def wspec_out(w):
        for ch in range(2):
            WSPECS.append([(w[:, ch * 512:(ch + 1) * 512], 8, 0, 512, 512)])

    def wspec_ffn(layer):
        wi = w_ffn_in[layer * D:(layer + 1) * D, :]
        wo = w_ffn_out[layer * FFN:(layer + 1) * FFN, :]
        for (b0, b1) in ((0, 6), (6, 11)):
            for b in range(b0, b1):
                WSPECS.append([(wi[:, b * 256:(b + 1) * 256], 8, 0, 256, 512),
                               (wi[:, FFN + b * 256:FFN + (b + 1) * 256], 8, 256, 256, 512)])
            nch = 2 * (b1 - b0)
            for ch in range(2):
                WSPECS.append([(wo[b0 * 256:b1 * 256, ch * 512:(ch + 1) * 512], nch, 0, 512, 512)])

    def wspec_layer1():
        for b in range(2):
            WSPECS.append([(w_in_o[:, 512 + b * 256:512 + (b + 1) * 256], 8, 0, 256, 512),
                           (w_in_o[:, 1024 + b * 256:1024 + (b + 1) * 256], 8, 256, 256, 512)])
        for b in range(2):
            WSPECS.append([(w_in_o[:, b * 256:(b + 1) * 256], 8, 0, 256, 512),
                           (w_in_o[:, 1536 + b * 256:1536 + (b + 1) * 256], 8, 256, 256, 512)])

    for _g in range(2):
        wspec_layer0()
        wspec_out(w_out_e)
        wspec_ffn(0)
        wspec_layer1()
        wspec_out(w_out_o)
        wspec_ffn(1)
    wstate = {"cur": 0, "issued": 0}

    def wissue(i):
        s = i % 2
        for pi, (src, nk, c0, ncols, stride) in enumerate(WSPECS[i]):
            dst = wslot[:, s, 0:nk * stride].rearrange("p (k n) -> p k n", k=nk)[:, :, c0:c0 + ncols]
            srcv = src.rearrange("(k p) n -> p k n", p=128)
            P.add("pool", lambda e, dst=dst, srcv=srcv: e.dma_start(out=dst, in_=srcv), [], [V(dst)], dsem=wsems[s][pi])

    def wget(nk, stride=512):
        i = wstate["cur"]
        wstate["cur"] += 1
        while wstate["issued"] <= min(i + 1, len(WSPECS) - 1):
            wissue(wstate["issued"])
            wstate["issued"] += 1
        assert WSPECS[i][0][1] == nk, (i, nk, WSPECS[i][0][1])
        return wslot[:, i % 2, 0:nk * stride].rearrange("p (k n) -> p k n", k=nk)

    wissue(0)
    wissue(1)
    wstate["issued"] = 2

    cload(chraw[0:38, :], chvd)
    cload(sbias[:], sbb.rearrange("a b -> (a b)").partition_broadcast(128))
    cload(ptb[:], pt.rearrange("a b -> (a b)").partition_broadcast(128))
    pw32 = mixv(0, 512, F32).rearrange("p (g d) -> p g d", g=4)
    cload(pw32, poolwd.rearrange("(g c) d -> c g d", c=128))
    cp("dve", poolw[:], pw32)

    mset("pool", onesf[:], 1.0)
    mset("pool", epst[:], EPS)
    asel(identf[:], onesf[:], [[-1, 128]], ALU.is_equal, 0, 1)
    tincf = mixv(4096, 128, F32)
    asel(tincf, onesf[:], [[-1, 128]], ALU.is_ge, 0, 1)
    cp("pool", identb[:], identf[:])
    cp("pool", tinc[:], tincf)
    cp("pool", onesb[:], onesf[:])
    ts("pool", negm[:], tincf, NEG, ALU.mult)
    P.add("pool", lambda e: e.iota(iot[:], [[0, 1]], base=0, channel_multiplier=1, allow_small_or_imprecise_dtypes=True),
          [], [V(iot[:])])
    ts("dve", pidx[:], ptb[:], 128.0, ALU.mult, iot[:, 0:1], ALU.add)
    cp("dve", biasd[:].rearrange("p (h q) -> p h q", q=8), sbias[:].unsqueeze(2).broadcast_to([128, 8, 8]))
    for c in range(4):
        tr(banks[0][0:128, c * 64: c * 64 + 38], chraw[0:38, c * 128:(c + 1) * 128], identf[0:38, 0:38])
    cp("dve", chv[:], banks[0][:, 0:256].rearrange("p (c r) -> p c r", c=4)[:, :, 0:38])
    rcf = mixv(8192, 16, F32)
    P.add("pool", lambda e: e.iota(rcf, [[1, 16]], base=1, channel_multiplier=0, allow_small_or_imprecise_dtypes=True),
          [], [V(rcf)])
    for g, w in enumerate((2, 4, 8, 16)):
        ts("dve", rc[:, g, :], rcf, float(w), ALU.min)
    P.add("dve", lambda e: e.reciprocal(out=rc[:], in_=rc[:]), [V(rc[:])], [V(rc[:])])
    mset("pool", ebl[:], 1.0)
    asel(ebl[:], ebl[:], [[1, 128]], ALU.is_ge, 0, -8)
    asel(ebl[:], ebl[:], [[-1, 128]], ALU.is_ge, 7, 8)
    cp("pool", eblb[:], ebl[:])
    mm(banks[1][:, 0:128], eblb[0:16, :], eblb[0:16, :], True, True)
    blk = mixv(12288, 128, F32)
    cp("dve", blk, banks[1][:, 0:128])
    asel(blk, blk, [[1, 128]], ALU.is_gt, 0, -1)
    ts("dve", maskN[:], blk, -NEG, ALU.mult, NEG, ALU.add)

    groups = [
        dict(ptiles=list(range(0, 9)), sample=False),
        dict(ptiles=list(range(9, 16)), sample=True),
    ]
    xsems = [P.dsem() for _ in range(9)]
    gsem = [P.dsem(), P.dsem()]
    osem = [P.dsem(out=True), P.dsem(out=True)]
    ostate = {"n": 0}

    def stage_slot():
        i = ostate["n"] % 2
        ostate["n"] += 1
        return i

    gstate = {"n": 0}

    def load_g(row):
        i = gstate["n"] % 2
        gstate["n"] += 1
        dma("sp", gb[:, i, :], gvec[row].partition_broadcast(128), gsem[i], wr=[gb[:, i, :]])
        return i

    statc = {"n": 0}

    def stat_col(n=3):
        c = statc["n"]
        if c + n > 64:
            c = 0
        statc["n"] = c + n
        return c

    def col_tiles(lo, hi, step=512):
        r = []
        c = lo
        while c < hi:
            r.append((c, min(step, hi - c)))
            c += step
        return r

    bank_rr = {"n": 0}

    def nbank(lo=2, hi=8):
        b = lo + bank_rr["n"] % (hi - lo)
        bank_rr["n"] += 1
        return banks[b]

    def run_groups():
        for gi, G in enumerate(groups):
            ptiles = G["ptiles"]
            npt = len(ptiles)
            ntl = npt + (1 if G["sample"] else 0)
            TP = npt * 128
            T = ntl * 128
            first_tok = ptiles[0] * 128
            hnT = actA
            cat = actA

            for lt, gt in enumerate(ptiles):
                dma("sp", h[:, lt, :], xp[gt * 128:(gt + 1) * 128, :], xsems[lt], wr=[h[:, lt, :]])
            if G["sample"]:
                dma("sp", h[:, npt, :], xs, xsems[npt], wr=[h[:, npt, :]])

            def norm_T(grow):
                gslot = load_g(grow)
                c0 = stat_col(3 * ntl)
                halves = [(0, (ntl + 1) // 2), ((ntl + 1) // 2, ntl)]
                for (t0_, t1_) in halves:
                    for lt in range(t0_, t1_):
                        act(xn[:, lt % 2, :], h[:, lt, :], AF.Square, accum=stats[:, c0 + lt:c0 + lt + 1])
                    act(stats[:, c0 + ntl + t0_:c0 + ntl + t1_], stats[:, c0 + t0_:c0 + t1_], AF.Ln, bias=epst[:, 0:1], scale=1.0 / D)
                    act(stats[:, c0 + 2 * ntl + t0_:c0 + 2 * ntl + t1_], stats[:, c0 + ntl + t0_:c0 + ntl + t1_], AF.Exp, scale=-0.5)
                for lt in range(ntl):
                    hv = h[:, lt, :]
                    xs_ = lt % 2
                    stt("dve", xn[:, xs_, :], hv, stats[:, c0 + 2 * ntl + lt:c0 + 2 * ntl + lt + 1], gb[:, gslot, :], ALU.mult, ALU.mult)
                    pb = banks[lt % 2]
                    pbv = pb[:].bitcast(BF16)
                    for kc in range(8):
                        tr(pbv[:, kc * 128:(kc + 1) * 128], xn[:, xs_, kc * 128:(kc + 1) * 128], identb[:])
                    cp("dve" if lt % 2 == 0 else "act", hnT[:, :, lt * 128:(lt + 1) * 128],
                       pbv.rearrange("p (k n) -> p k n", k=8))

            def fm_mm(psv, wv, wc0, c0, n):
                for kc in range(8):
                    mm(psv, wv[:, kc, wc0:wc0 + 128], hnT[:, kc, c0:c0 + n], kc == 0, kc == 7)

            def tm_mm(psv, wv, wc0, ncols, lt):
                for kc in range(8):
                    mm(psv, hnT[:, kc, lt * 128:(lt + 1) * 128], wv[:, kc, wc0:wc0 + ncols], kc == 0, kc == 7)

            def residual_proj(nk_total, src):
                for ch in range(2):
                    wv = wget(nk_total)
                    for lt in range(ntl):
                        pb = nbank()
                        for kc in range(nk_total):
                            mm(pb[:], src[:, kc, lt * 128:(lt + 1) * 128], wv[:, kc, :], kc == 0, kc == nk_total - 1)
                        tt("dve", h[:, lt, ch * 512:(ch + 1) * 512], h[:, lt, ch * 512:(ch + 1) * 512], pb[:], ALU.add)

            def ffn(layer):
                hid = mixv(0, 12 * T, BF16).rearrange("p (c t) -> p c t", c=12)
                sg = mixv(12 * T * 2, 2 * 512, BF16).rearrange("p (a n) -> p a n", a=2)
                norm_T(1 + 2 * layer)
                for half, (b0, b1) in enumerate(((0, 6), (6, 11))):
                    nch = 2 * (b1 - b0)
                    for b in range(b0, b1):
                        wv = wget(8)
                        for fc in range(2):
                            lc = 2 * (b - b0) + fc
                            for (c0, n) in col_tiles(0, T):
                                pg = nbank()
                                pu = nbank()
                                fm_mm(pg[:, 0:n], wv, fc * 128, c0, n)
                                fm_mm(pu[:, 0:n], wv, 256 + fc * 128, c0, n)
                                k = bank_rr["n"] % 2
                                act(sg[:, k, 0:n], pg[:, 0:n], AF.Silu)
                                tt("dve", hid[:, lc, c0:c0 + n], pu[:, 0:n], sg[:, k, 0:n], ALU.mult)
                    residual_proj(nch, hid)

            norm_T(0)
            o = 0
            QT = mixv(o, 4 * T, BF16).rearrange("p (c t) -> p c t", c=4); o += 4 * T * 2
            sgt = mixv(o, 2 * 512, F32).rearrange("p (a n) -> p a n", a=2)
            vs_tok = mixv(o, 512, BF16)
            lntmp = mixv(o + 1024, 3 * 512, BF16).rearrange("p (a n) -> p a n", a=3)
            o += 4096
            R1 = o
            extp = mixv(o, 4 * (30 + TP), BF16).rearrange("p (c t) -> p c t", c=4); o += 4 * (30 + TP) * 2
            if G["sample"]:
                exts = mixv(o, 4 * 16 * 38, BF16).rearrange("p (c s t) -> p c s t", c=4, s=16); o += 4 * 16 * 38 * 2
            diag2 = mixv(o, 2 * 31 * 128, BF16).rearrange("p (d j n) -> p d j n", d=2, j=31); o += 2 * 31 * 128 * 2
            a16o = o
            a16 = mixv(o, 4 * T, BF16).rearrange("p (c t) -> p c t", c=4); o += 4 * T * 2
            a2 = mixv(o, 4 * T, BF16).rearrange("p (c t) -> p c t", c=4); o += 4 * T * 2
            lnt = mixv(o, 4 * 512, F32).rearrange("p (a n) -> p a n", a=4); o += 4 * 2048
            assert o <= MIXB, o

            if gi == 0:
                mset("pool", extp[:, :, 0:30], 0.0)
            else:
                cp("pool", extp[:, :, 0:30], hist_a[:])
                stt_ = mixv(a16o, 4 * 512, F32).rearrange("p (a n) -> p a n", a=4)
                for i in range(4):
                    lload(stt_[0:120, i, :], st_conf[i * 120:(i + 1) * 120, :])
                for i in range(4):
                    pb = nbank()
                    for c in range(4):
                        tr(pb[:, c * 128:c * 128 + 120], stt_[0:120, i, c * 128:(c + 1) * 128], identf[0:120, 0:120])
                    cp("dve", exts[:, :, 4 * i:4 * i + 4, 0:30],
                       pb[:].rearrange("p (c x) -> p c x", c=4)[:, :, 0:120].rearrange("p c (s r) -> p c s r", s=4))
                P.add("sp", lambda e: e.dma_start(out=conv_s[:, 0:22, :], in_=st_conf.rearrange("(s r) f -> s r f", r=30)[:, 8:30, :]),
                      [], [], dsem=P.dsem(out=True))

            chk('L0 norm gi=%d' % gi)
            for b in range(2):
                wv = wget(8)
                for fc in range(2):
                    c = 2 * b + fc
                    for (c0, n) in col_tiles(0, T):
                        pa = nbank()
                        pg = nbank()
                        fm_mm(pa[:, 0:n], wv, fc * 128, c0, n)
                        fm_mm(pg[:, 0:n], wv, 256 + fc * 128, c0, n)
                        k = bank_rr["n"] % 2
                        act(sgt[:, k, 0:n], pg[:, 0:n], AF.Sigmoid)
                        npp = max(0, min(n, TP - c0))
                        if npp > 0:
                            tt("dve", extp[:, c, 30 + c0:30 + c0 + npp], pa[:, 0:npp], sgt[:, k, 0:npp], ALU.mult)
                        if npp < n:
                            assert n - npp == 128
                            tt("dve", exts[:, c, :, 30:38], pa[:, npp:n].rearrange("p (s t) -> p s t", t=8),
                               sgt[:, k, npp:n].rearrange("p (s t) -> p s t", t=8), ALU.mult)
                if gi == 1:
                    for lt, dst in ((npt - 1, "p"), (npt, "s")):
                        pa = nbank()
                        tm_mm(pa[:], wv, 0, 512, lt)
                        si = stage_slot()
                        act(stage[:, si, 0:256], pa[:, 256:512], AF.Sigmoid)
                        tt("dve", stage[:, si, 256:512], pa[:, 0:256], stage[:, si, 0:256], ALU.mult)
                        if dst == "p":
                            dma("sp", conv_p[:, b * 256:(b + 1) * 256], stage[98:128, si, 256:512], osem[si], rd=[stage[:, si, 256:512]])
                        else:
                            for s_ in range(NSEQ):
                                dma("sp", conv_s[s_, 22:30, b * 256:(b + 1) * 256], stage[8 * s_:8 * s_ + 8, si, 256:512], osem[si],
                                    rd=[stage[:, si, 256:512]])
            chk('L0 glu gi=%d' % gi)
            wv = wget(8)
            for c in range(4):
                for (c0, n) in col_tiles(0, T):
                    pb = nbank()
                    fm_mm(pb[:, 0:n], wv, c * 128, c0, n)
                    cp("act", QT[:, c, c0:c0 + n], pb[:, 0:n])
            wv = wget(8)
            for c in range(4):
                for (c0, n) in col_tiles(0, T):
                    pb = nbank()
                    fm_mm(pb[:, 0:n], wv, c * 128, c0, n)
                    npp = max(0, min(n, TP - c0))
                    if npp > 0:
                        cp("act", KT[:, c, first_tok + c0:first_tok + c0 + npp], pb[:, 0:npp])
                    if npp < n:
                        cp("act", KT[:, c, SEQ:SEQ + 128], pb[:, npp:n])
            for lt in range(ntl):
                pb = nbank()
                tm_mm(pb[:], wv, 0, 512, lt)
                si = stage_slot()
                cp("dve", stage[:, si, 0:512], pb[:])
                dst = kp[ptiles[lt] * 128:(ptiles[lt] + 1) * 128, :] if lt < npt else ks
                dma("sp", dst, stage[:, si, 0:512], osem[si], rd=[stage[:, si, 0:512]])
            wv = wget(8)
            for lt in range(ntl):
                pb = nbank()
                tm_mm(pb[:], wv, 0, 512, lt)
                si = stage_slot()
                cp("dve", stage[:, si, 0:512], pb[:])
                if lt < npt:
                    cp("act", Vall[:, ptiles[lt], :], pb[:])
                    dst = vp[ptiles[lt] * 128:(ptiles[lt] + 1) * 128, :]
                else:
                    cp("act", vs_tok, pb[:])
                    dst = vs
                dma("sp", dst, stage[:, si, 0:512], osem[si], rd=[stage[:, si, 0:512]])

            chk('L0 qkv gi=%d' % gi)
            for c in range(4):
                diag = diag2[:, c % 2]
                for j in range(31):
                    ts("dve", diag[:, j, :], identb[:], chv[:, c, j:j + 1], ALU.mult)
                for (c0, n) in col_tiles(0, TP):
                    pb = nbank()
                    for j in range(31):
                        mm(pb[:, 0:n], diag[:, j, :], extp[:, c, c0 + j:c0 + j + n], j == 0, j == 30)
                    act(a16[:, c, c0:c0 + n], pb[:, 0:n], AF.Identity, bias=chv[:, c, 31:32])
                    act(a2[:, c, c0:c0 + n], pb[:, 0:n], AF.Square, bias=chv[:, c, 31:32])
                if G["sample"]:
                    pb = nbank()
                    for j in range(31):
                        mm(pb[:, 0:128].rearrange("p (s t) -> p s t", t=8), diag[:, j, :], exts[:, c, :, j:j + 8], j == 0, j == 30)
                    act(a16[:, c, TP:T], pb[:, 0:128], AF.Identity, bias=chv[:, c, 31:32])
                    act(a2[:, c, TP:T], pb[:, 0:128], AF.Square, bias=chv[:, c, 31:32])
            if gi == 0:
                cp("pool", hist_a[:], extp[:, :, TP:TP + 30])
            cts = col_tiles(0, T)
            stat_ps = []
            for ci, (c0, n) in enumerate(cts):
                pm = banks[2 + 2 * ci]
                pv_ = banks[3 + 2 * ci]
                for c in range(4):
                    mm(pm[:, 0:n], onesb[:], a16[:, c, c0:c0 + n], c == 0, c == 3)
                for c in range(4):
                    mm(pv_[:, 0:n], onesb[:], a2[:, c, c0:c0 + n], c == 0, c == 3)
                stat_ps.append((pm, pv_))
            ti = 0
            for ci, (c0, n) in enumerate(cts):
                pm, pv_ = stat_ps[ci]
                mean = lnt[:, 2 * (ci % 2), 0:n]
                rstd = lnt[:, 2 * (ci % 2) + 1, 0:n]
                ts("dve", mean, pm[:, 0:n], 1.0 / 512, ALU.mult)
                tt("pool", rstd, mean, mean, ALU.mult)
                stt("dve", rstd, pv_[:, 0:n], 1.0 / 512, rstd, ALU.mult, ALU.subtract)
                act(rstd, rstd, AF.Ln, bias=epst[:, 0:1])
                act(rstd, rstd, AF.Exp, scale=-0.5)
                for c in range(4):
                    tmp = lntmp[:, ti % 3, 0:n]
                    ti += 1
                    tt("dve", tmp, a16[:, c, c0:c0 + n], mean, ALU.subtract)
                    tt("dve", tmp, tmp, rstd, ALU.mult)
                    act(cat[:, c, c0:c0 + n], tmp, AF.Silu, bias=chv[:, c, 33:34], scale=chv[:, c, 32:33])
            chk('L0 conformer gi=%d' % gi)
            o = R1
            ebuf = mixv(o, 3 * 512, F32).rearrange("p (a n) -> p a n", a=3); o += 6144
            xbuf = mixv(o, 2 * 512, F32).rearrange("p (a n) -> p a n", a=2); o += 4096
            spb = mixv(o, 3 * 512, BF16).rearrange("p (a n) -> p a n", a=3); o += 3072
            wb = mixv(o, 2 * 512, BF16).rearrange("p (a n) -> p a n", a=2); o += 2048
            lacc = mixv(o, 3 * 512, BF16).rearrange("p (a n) -> p a n", a=3); o += 3072
            DEC0 = o
            SMP = G["sample"]
            its = []
            qgroups = [ptiles[i:i + 4] for i in range(0, npt, 4)]
            for qg in qgroups:
                gq0, gq1 = qg[0], qg[-1]
                NQ = len(qg) * 128
                ql0 = (gq0 - ptiles[0]) * 128
                for hp in range(4):
                    for hh in range(2):
                        for j in range(gq1, -1, -1):
                            its.append(dict(gq0=gq0, gq1=gq1, NQ=NQ, ql0=ql0, hp=hp, hh=hh, j=j,
                                            first=(j == gq1), last=(j == 0), i=len(its),
                                            off=(max(gq0, j) - gq0) * 128))

            def stageA(I):
                i, off, NQ, hp, hh, j = I["i"], I["off"], I["NQ"], I["hp"], I["hh"], I["j"]
                hsl = slice(hh * 64, hh * 64 + 64)
                hd = 2 * hp + hh
                pS = banks[2] if SMP else banks[2 + i % 2]
                k3 = i % 3
                diagblk = j >= I["gq0"]
                mm(pS[:, off:NQ], KT[hsl, hp, j * 128:(j + 1) * 128], QT[hsl, hp, I["ql0"] + off:I["ql0"] + NQ], True, not diagblk)
                if diagblk:
                    mm(pS[:, off:off + 128], identb[:], negm[:], False, True)
                act(ebuf[:, k3, off:NQ], pS[:, off:NQ], AF.Exp, bias=sbias[:, hd:hd + 1], scale=0.125)
                act(spb[:, k3, off:NQ], ebuf[:, k3, off:NQ], AF.Ln, bias=1.0)
                if not I["last"]:
                    ln_ = (i + 1) % 3
                    le = "dve" if SMP else "pool"
                    if off > 0:
                        mset(le, lacc[:, ln_, 0:off], 0.0)
                    if I["first"]:
                        cp(le, lacc[:, ln_, off:NQ], spb[:, k3, off:NQ])
                    else:
                        tt(le, lacc[:, ln_, off:NQ], lacc[:, i % 3, off:NQ], spb[:, k3, off:NQ], ALU.add)

            def stageB(I):
                i, off, NQ, hp, hh, j = I["i"], I["off"], I["NQ"], I["hp"], I["hh"], I["j"]
                pA = banks[4] if SMP else banks[4 + i % 2]
                k3 = i % 3
                k2 = i % 2
                mm(pA[:, off:NQ], tinc[:], spb[:, k3, off:NQ], True, I["first"])
                if not I["first"]:
                    mm(pA[:, off:NQ], onesb[:], lacc[:, i % 3, off:NQ], False, True)
                act(xbuf[:, k2, off:NQ], pA[:, off:NQ], AF.Exp, scale=-1.0)
                tt("dve", wb[:, k2, off:NQ], ebuf[:, k3, off:NQ], xbuf[:, k2, off:NQ], ALU.mult)

            def stageC(I):
                i, off, NQ, hp, hh, j = I["i"], I["off"], I["NQ"], I["hp"], I["hh"], I["j"]
                hsl = slice(hh * 64, hh * 64 + 64)
                hd = 2 * hp + hh
                pB = banks[5] if SMP else banks[6 + (hp % 2)]
                k2 = i % 2
                mm(pB[hsl, off:NQ], Vall[:, j, hd * 64:(hd + 1) * 64], wb[:, k2, off:NQ], I["first"], I["last"],
                   tp=(0, 64) if hh == 1 else None, sgc=True)
                if I["last"] and hh == 1:
                    cp("dve", cat[:, 4 + hp, I["ql0"]:I["ql0"] + NQ], pB[:, 0:NQ])

            dec_hook = None
            if SMP:
                sc0 = TP
                o = DEC0
                qblk = mixv(o, 4 * 16 * 16, BF16).rearrange("p (c s x) -> p c s x", c=4, s=16); o += 2048
                kring = mixv(o, 8 * 512, BF16).rearrange("p (a f) -> p a f", a=8); o += 8192
                ktp = mixv(o, 2 * 512, BF16).rearrange("p (a f) -> p a f", a=2); o += 2048
                vring = mixv(o, 8 * 512, BF16).rearrange("p (a f) -> p a f", a=8)
                eN = mixv(o, 1024, F32)
                o += 8192
                zbb = mixv(o, 2 * 512, F32).rearrange("p (a n) -> p a n", a=2)
                xdN = mixv(o, 1024, F32)
                o += 4096
                xd = mixv(o, 512, F32); o += 2048
                spdb = mixv(o, 2 * 512, BF16).rearrange("p (a n) -> p a n", a=2); o += 2048
                wd = mixv(o, 512, BF16); o += 1024
                spN = mixv(o, 1024, BF16); o += 2048
                wN = mixv(o, 1024, BF16); o += 2048
                sufa = mixv(o, 512, BF16); o += 1024
                sufb = mixv(o, 512, BF16); o += 1024
                sphs = mixv(o, 64, BF16); o += 128
                assert o <= MIXB, o
                ksem = [P.dsem() for _ in range(8)]
                vsem = [P.dsem() for _ in range(8)]
                mset("pool", qblk[:], 0.0)
                for hp in range(4):
                    cp("pool", qblk[0:64, hp, :, 0:8], QT[0:64, hp, sc0:sc0 + 128].rearrange("p (s t) -> p s t", t=8))
                    cp("pool", qblk[64:128, hp, :, 8:16], QT[64:128, hp, sc0:sc0 + 128].rearrange("p (s t) -> p s t", t=8))
                pTv2 = [banks[0][:].bitcast(BF16), banks[3][:].bitcast(BF16)]
                pSd = banks[1]
                pAd = banks[6]
                pBd = banks[7]
                pS2 = [banks[1], banks[2]]
                pA2 = [banks[3], banks[4]]
                for hb in range(2):
                    for hd in range(4 * hb, 4 * hb + 4):
                        hp, hh = hd // 2, hd % 2
                        hsl = slice(hh * 64, hh * 64 + 64)
                        dstp = pS2[hb][:, (hd % 4) * 128:(hd % 4 + 1) * 128]
                        mm(dstp, KT[hsl, hp, SEQ:SEQ + 128], QT[hsl, hp, sc0:sc0 + 128], hd % 4 == 0, False, sgc=True)
                        mm(dstp, identb[:], maskN[:], False, True, sgc=True)
                    for hd in range(4 * hb, 4 * hb + 4):
                        dstp = pS2[hb][:, (hd % 4) * 128:(hd % 4 + 1) * 128]
                        act(eN[:, hd * 128:(hd + 1) * 128], dstp, AF.Exp, bias=sbias[:, hd:hd + 1], scale=0.125)
                act(spN, eN, AF.Ln, bias=1.0)
                for bq in range(2):
                    mm(pA2[bq][:], tinc[:], spN[:, bq * 512:(bq + 1) * 512], True, True)
                    act(xdN[:, bq * 512:(bq + 1) * 512], pA2[bq][:], AF.Exp, scale=-1.0)
                tt("dve", wN, eN, xdN, ALU.mult)
                spN3 = spN.rearrange("p (h c) -> p h c", h=8)
                wN3 = wN.rearrange("p (h c) -> p h c", h=8)
                NU = 2 * NSEQ

                def pages(u):
                    return (u // 2, 8 if u % 2 == 0 else 0)

                def M1(u):
                    s_, p0 = pages(u)
                    for jl in range(8):
                        col = s_ * NPAGE + p0 + jl
                        kdst = kring[:, jl, :]
                        P.add("pool", lambda e, kdst=kdst, col=col: e.indirect_dma_start(
                            out=kdst, out_offset=None, in_=ck,
                            in_offset=bass.IndirectOffsetOnAxis(ap=pidx[:, col:col + 1], axis=0)),
                            [V(pidx[:, col:col + 1])], [V(kdst)], dsem=ksem[jl])

                def MV(u):
                    s_, p0 = pages(u)
                    for jl in range(8):
                        col = s_ * NPAGE + p0 + jl
                        vdst = vring[:, jl, :]
                        P.add("pool", lambda e, vdst=vdst, col=col: e.indirect_dma_start(
                            out=vdst, out_offset=None, in_=cv,
                            in_offset=bass.IndirectOffsetOnAxis(ap=pidx[:, col:col + 1], axis=0)),
                            [V(pidx[:, col:col + 1])], [V(vdst)], dsem=vsem[jl])

                def M2(u):
                    s_, p0 = pages(u)

                    def smm(jl):
                        half = jl % 2
                        for hp in range(4):
                            mm(pSd[:, jl * 64 + hp * 16:jl * 64 + hp * 16 + 16],
                               ktp[:, half, hp * 128:(hp + 1) * 128], qblk[:, hp, s_, :], True, True)
                    for jl in range(8):
                        half = jl % 2
                        pTv = pTv2[half]
                        for c in range(4):
                            tr(pTv[:, c * 128:(c + 1) * 128], kring[:, jl, c * 128:(c + 1) * 128], identb[:])
                        cp("dve", ktp[:, half, :], pTv[:, 0:512])
                        if jl >= 1:
                            smm(jl - 1)
                    smm(7)

                def M3a(u):
                    hf = u % 2
                    zb = zbb[:, hf, :]
                    spd = spdb[:, hf, :]
                    stt("dve", zb.rearrange("p (j c) -> p j c", c=64), pSd[:].rearrange("p (j c) -> p j c", c=64),
                        0.125, biasd[:].unsqueeze(1).broadcast_to([128, 8, 64]), ALU.mult, ALU.add)
                    act(zb, zb, AF.Exp)
                    act(spd, zb, AF.Ln, bias=1.0)
                    s3 = spd.rearrange("p (j c) -> p j c", c=64)
                    t1 = sufa.rearrange("p (j c) -> p j c", c=64)
                    t2 = sufb.rearrange("p (j c) -> p j c", c=64)
                    tt("dve", t1[:, 0:7, :], s3[:, 0:7, :], s3[:, 1:8, :], ALU.add)
                    cp("dve", t1[:, 7:8, :], s3[:, 7:8, :])
                    tt("dve", t2[:, 0:6, :], t1[:, 0:6, :], t1[:, 2:8, :], ALU.add)
                    cp("dve", t2[:, 6:8, :], t1[:, 6:8, :])
                    tt("dve", t1[:, 0:4, :], t2[:, 0:4, :], t2[:, 4:8, :], ALU.add)
                    cp("dve", t1[:, 4:8, :], t2[:, 4:8, :])
                    if hf == 0:
                        cp("dve", sphs, t1[:, 0, :])

                def M3b(u):
                    s_, p0 = pages(u)
                    hf = u % 2
                    spd = spdb[:, hf, :]
                    t1 = sufa.rearrange("p (j c) -> p j c", c=64)
                    mm(pAd[:], tinc[:], spd, True, False)
                    mm(pAd[:, 0:7 * 64].rearrange("p (j c) -> p j c", c=64), onesb[:], t1[:, 1:8, :], False, False)
                    if hf == 1:
                        mm(pAd[:].rearrange("p (j c) -> p j c", c=64), onesb[:],
                           sphs.unsqueeze(1).broadcast_to([128, 8, 64]), False, False)
                    newc = spN3[:, :, 8 * s_:8 * s_ + 8].unsqueeze(1).broadcast_to([128, 8, 8, 8])
                    mm(pAd[:].rearrange("p (j h q) -> p j h q", j=8, h=8), onesb[:], newc, False, True)

                def M4a(u):
                    hf = u % 2
                    zb = zbb[:, hf, :]
                    act(xd, pAd[:], AF.Exp, scale=-1.0)
                    tt("dve", wd, zb, xd, ALU.mult)

                def M4b(u):
                    s_, p0 = pages(u)
                    hf = u % 2
                    for jl in range(8):
                        for hp in range(4):
                            mm(pBd[:, hp * 16:hp * 16 + 16], vring[:, jl, hp * 128:(hp + 1) * 128],
                               wd[:, jl * 64 + hp * 16:jl * 64 + hp * 16 + 16], hf == 0 and jl == 0 and hp == 0, False, sgc=True)
                    if hf == 1:
                        for hp in range(4):
                            mm(pBd[:, hp * 16:hp * 16 + 16].rearrange("p (a q) -> p a q", a=2), vs_tok[:, hp * 128:(hp + 1) * 128],
                               wN3[:, 2 * hp:2 * hp + 2, 8 * s_:8 * s_ + 8], False, hp == 3, sgc=True)
                        pbv4 = pBd[:, 0:64].rearrange("p (c x) -> p c x", c=4)
                        cp("dve", cat[0:64, 4:8, sc0 + 8 * s_:sc0 + 8 * s_ + 8], pbv4[0:64, :, 0:8])
                        cp("dve", cat[64:128, 4:8, sc0 + 8 * s_:sc0 + 8 * s_ + 8], pbv4[64:128, :, 8:16])

                NMS = NU + 4
                IPM = max(5, -(-(len(its) + 2) // NMS))

                def micro(m, part):
                    if part == 0:
                        if 0 <= m - 2 < NU:
                            M3b(m - 2)
                    elif part == 1:
                        if 0 <= m - 3 < NU:
                            M4b(m - 3)
                        if 0 <= m - 2 < NU:
                            MV(m - 2)
                    elif part == 2:
                        if 0 <= m - 1 < NU:
                            M2(m - 1)
                        if m < NU:
                            M1(m)
                    elif part == 3:
                        if 0 <= m - 2 < NU:
                            M4a(m - 2)
                    elif part == 4:
                        if 0 <= m - 1 < NU:
                            M3a(m - 1)

                dstate = {"m": 0, "part": 0}

                def dec_hook(step, flush=False):
                    while dstate["m"] < NMS:
                        m, part = dstate["m"], dstate["part"]
                        due = m * IPM + min(part, IPM - 1)
                        if not flush and due > step:
                            break
                        micro(m, part)
                        if part == 4:
                            dstate["m"] += 1
                            dstate["part"] = 0
                        else:
                            dstate["part"] += 1

            n_it = len(its)
            for step in range(n_it + 2):
                if dec_hook is not None:
                    dec_hook(step)
                if step < n_it:
                    stageA(its[step])
                if 0 <= step - 1 < n_it:
                    stageB(its[step - 1])
                if 0 <= step - 2 < n_it:
                    stageC(its[step - 2])
            if dec_hook is not None:
                dec_hook(0, flush=True)

            chk('L0 decode gi=%d' % gi)
            residual_proj(8, cat)
            ffn(0)

            chk('L0 ffn gi=%d' % gi)
            norm_T(2)
            o = 0
            gbb = mixv(o, 4 * T, BF16).rearrange("p (c t) -> p c t", c=4); o += 4 * T * 2
            cxp = mixv(o, 4 * (2 + TP), BF16).rearrange("p (c t) -> p c t", c=4); o += 4 * (2 + TP) * 2
            upx = mixv(o, 4 * (16 + TP), BF16).rearrange("p (c t) -> p c t", c=4)[:, :, 1:16 + TP]; o += 4 * (16 + TP) * 2
            if G["sample"]:
                cxs = mixv(o, 4 * 16 * 10, BF16).rearrange("p (c s t) -> p c s t", c=4, s=16); o += 4 * 160 * 2
                usx = mixv(o, 4 * 16 * 24, BF16).rearrange("p (c s t) -> p c s t", c=4, s=16)[:, :, :, 1:24]; o += 4 * 16 * 24 * 2
            gct = mixv(o, 2 * 512, BF16).rearrange("p (a n) -> p a n", a=2); o += 2048
            acc = mixv(o, 16 + T, F32); o += (16 + T) * 4
            acc2 = mixv(o, 16 + T, F32); o += (16 + T) * 4
            pooled = mixv(o, T, BF16); o += T * 2
            sts = mixv(o, 2 * 512, F32).rearrange("p (a n) -> p a n", a=2); o += 4096
            assert o <= MIXB, o
            if gi == 0:
                mset("pool", cxp[:, :, 0:2], 0.0)
                mset("pool", upx[:, :, 0:15], 0.0)
            else:
                cp("pool", cxp[:, :, 0:2], hist_c[:])
                cp("pool", upx[:, :, 0:15], hist_u[:])
                lload(sts[0:32, 0, :], st_sc)
                pb = nbank()
                for c in range(4):
                    tr(pb[:, c * 128:c * 128 + 32], sts[0:32, 0, c * 128:(c + 1) * 128], identf[0:32, 0:32])
                cp("dve", cxs[:, :, :, 0:2], pb[:].rearrange("p (c x) -> p c x", c=4)[:, :, 0:32].rearrange("p c (s r) -> p c s r", s=16))
                for i in range(2):
                    lload(sts[0:120, 1, :], st_pool[i * 120:(i + 1) * 120, :])
                    pb = nbank()
                    for c in range(4):
                        tr(pb[:, c * 128:c * 128 + 120], sts[0:120, 1, c * 128:(c + 1) * 128], identf[0:120, 0:120])
                    cp("dve", usx[:, :, 8 * i:8 * i + 8, 0:15],
                       pb[:].rearrange("p (c x) -> p c x", c=4)[:, :, 0:120].rearrange("p c (s r) -> p c s r", s=8))
                P.add("sp", lambda e: e.dma_start(out=pool_s[:, 0:7, :], in_=st_pool.rearrange("(s r) f -> s r f", r=15)[:, 8:15, :]),
                      [], [], dsem=P.dsem(out=True))

            chk('L1 norm gi=%d' % gi)
            for b in range(2):
                wv = wget(8)
                for fc in range(2):
                    c = 2 * b + fc
                    for (c0, n) in col_tiles(0, T):
                        pa = nbank()
                        pg = nbank()
                        fm_mm(pa[:, 0:n], wv, fc * 128, c0, n)
                        fm_mm(pg[:, 0:n], wv, 256 + fc * 128, c0, n)
                        k = bank_rr["n"] % 2
                        cp("act", gct[:, k, 0:n], pa[:, 0:n])
                        npp = max(0, min(n, TP - c0))
                        if npp > 0:
                            tt("dve", cxp[:, c, 2 + c0:2 + c0 + npp], pg[:, 0:npp], gct[:, k, 0:npp], ALU.mult)
                        if npp < n:
                            tt("dve", cxs[:, c, :, 2:10], pg[:, npp:n].rearrange("p (s t) -> p s t", t=8),
                               gct[:, k, npp:n].rearrange("p (s t) -> p s t", t=8), ALU.mult)
                if gi == 1:
                    for lt, dst in ((npt - 1, "p"), (npt, "s")):
                        pa = nbank()
                        tm_mm(pa[:], wv, 0, 512, lt)
                        si = stage_slot()
                        cp("act", stage[:, si, 0:256], pa[:, 0:256])
                        tt("dve", stage[:, si, 256:512], pa[:, 256:512], stage[:, si, 0:256], ALU.mult)
                        if dst == "p":
                            dma("sp", sc_p[:, b * 256:(b + 1) * 256], stage[126:128, si, 256:512], osem[si], rd=[stage[:, si, 256:512]])
                        else:
                            for s_ in range(NSEQ):
                                dma("sp", sc_s[s_, :, b * 256:(b + 1) * 256], stage[8 * s_ + 6:8 * s_ + 8, si, 256:512], osem[si],
                                    rd=[stage[:, si, 256:512]])
            for b in range(2):
                wv = wget(8)
                for fc in range(2):
                    c = 2 * b + fc
                    for (c0, n) in col_tiles(0, T):
                        pa = nbank()
                        pu = nbank()
                        fm_mm(pa[:, 0:n], wv, fc * 128, c0, n)
                        fm_mm(pu[:, 0:n], wv, 256 + fc * 128, c0, n)
                        cp("act", gbb[:, c, c0:c0 + n], pa[:, 0:n])
                        npp = max(0, min(n, TP - c0))
                        if npp > 0:
                            cp("dve", upx[:, c, 15 + c0:15 + c0 + npp], pu[:, 0:npp])
                        if npp < n:
                            cp("dve", usx[:, c, :, 15:23], pu[:, npp:n].rearrange("p (s t) -> p s t", t=8))
                if gi == 1:
                    for lt, dst in ((npt - 1, "p"), (npt, "s")):
                        pa = nbank()
                        tm_mm(pa[:, 0:256], wv, 256, 256, lt)
                        si = stage_slot()
                        cp("dve", stage[:, si, 0:256], pa[:, 0:256])
                        if dst == "p":
                            dma("sp", pool_p[:, b * 256:(b + 1) * 256], stage[113:128, si, 0:256], osem[si], rd=[stage[:, si, 0:256]])
                        else:
                            for s_ in range(NSEQ):
                                dma("sp", pool_s[s_, 7:15, b * 256:(b + 1) * 256], stage[8 * s_:8 * s_ + 8, si, 0:256], osem[si],
                                    rd=[stage[:, si, 0:256]])
            if gi == 0:
                cp("pool", hist_c[:], cxp[:, :, TP:TP + 2])
                cp("pool", hist_u[:], upx[:, :, TP:TP + 15])

            chk('L1 proj gi=%d' % gi)
            for c in range(4):
                a_ = acc[:, 0:TP]
                ts("dve", a_, cxp[:, c, 0:TP], chv[:, c, 34:35], ALU.mult)
                stt("dve", a_, cxp[:, c, 1:1 + TP], chv[:, c, 35:36], a_, ALU.mult, ALU.add)
                stt("dve", a_, cxp[:, c, 2:2 + TP], chv[:, c, 36:37], a_, ALU.mult, ALU.add)
                tt("dve", cat[:, c, 0:TP], a_, gbb[:, c, 0:TP], ALU.mult)
                if G["sample"]:
                    a3 = acc2[:, 0:128].rearrange("p (s t) -> p s t", t=8)
                    ts("dve", a3, cxs[:, c, :, 0:8], chv[:, c, 34:35], ALU.mult)
                    stt("dve", a3, cxs[:, c, :, 1:9], chv[:, c, 35:36], a3, ALU.mult, ALU.add)
                    stt("dve", a3, cxs[:, c, :, 2:10], chv[:, c, 36:37], a3, ALU.mult, ALU.add)
                    tt("dve", cat[:, c, TP:T], acc2[:, 0:128], gbb[:, c, TP:T], ALU.mult)
            for g, w in enumerate((2, 4, 8, 16)):
                L = 15 + TP
                src = upx[:, g, 0:L]
                cur, nxt = acc[:, 0:L], acc2[:, 0:L]
                sh = 1
                first = True
                while sh < w:
                    a_in = src if first else cur
                    eng = "dve"
                    if not first:
                        cp(eng, nxt[:, 0:sh], a_in[:, 0:sh])
                    tt(eng, nxt[:, sh:L], a_in[:, sh:L], a_in[:, 0:L - sh], ALU.add)
                    cur, nxt = nxt, cur
                    first = False
                    sh *= 2
                stt("dve", pooled[:, 0:TP], cur[:, 15:15 + TP], 1.0 / w, upx[:, g, 15:15 + TP], ALU.mult, ALU.subtract)
                if gi == 0:
                    tt("dve", nxt[:, 0:16], cur[:, 15:31], rc[:, g, :], ALU.mult)
                    tt("dve", pooled[:, 0:16], nxt[:, 0:16], upx[:, g, 15:31], ALU.subtract)
                if G["sample"]:
                    s3 = usx[:, g, :, :]
                    c3 = acc[:, 0:16 * 23].rearrange("p (s t) -> p s t", t=23)
                    n3 = acc2[:, 0:16 * 23].rearrange("p (s t) -> p s t", t=23)
                    sh = 1
                    first = True
                    while sh < w:
                        a_in = s3 if first else c3
                        if not first:
                            cp("pool", n3[:, :, 0:sh], a_in[:, :, 0:sh])
                        tt("pool", n3[:, :, sh:23], a_in[:, :, sh:23], a_in[:, :, 0:23 - sh], ALU.add)
                        c3, n3 = n3, c3
                        first = False
                        sh *= 2
                    stt("dve", pooled[:, TP:T].rearrange("p (s t) -> p s t", t=8), c3[:, :, 15:23], 1.0 / w, usx[:, g, :, 15:23],
                        ALU.mult, ALU.subtract)
                for (c0, n) in col_tiles(0, T):
                    pb = nbank()
                    mm(pb[:, 0:n], poolw[:, g, :], pooled[:, c0:c0 + n], True, True)
                    act(cat[:, 4 + g, c0:c0 + n], pb[:, 0:n], AF.Copy, scale=chv[:, g, 37:38])

            residual_proj(8, cat)
            ffn(1)

            chk('L1 mixer+ffn gi=%d' % gi)
            gslot = load_g(4)
            c0 = stat_col(3 * ntl)
            for lt in range(ntl):
                act(xn[:, lt % 2, :], h[:, lt, :], AF.Square, accum=stats[:, c0 + lt:c0 + lt + 1])
            act(stats[:, c0 + ntl:c0 + 2 * ntl], stats[:, c0:c0 + ntl], AF.Ln, bias=epst[:, 0:1], scale=1.0 / D)
            act(stats[:, c0 + 2 * ntl:c0 + 3 * ntl], stats[:, c0 + ntl:c0 + 2 * ntl], AF.Exp, scale=-0.5)
            for lt in range(ntl):
                hv = h[:, lt, :]
                si = stage_slot()
                stt("dve", stage[:, si, :], hv, stats[:, c0 + 2 * ntl + lt:c0 + 2 * ntl + lt + 1], gb[:, gslot, :], ALU.mult, ALU.mult)
                dst = yp[ptiles[lt] * 128:(ptiles[lt] + 1) * 128, :] if lt < npt else ys
                dma("sp", dst, stage[:, si, :], osem[si], rd=[stage[:, si, :]])

    try:
        chk('consts')
        run_groups()
    except _Stop as e_:
        print('STOPPED at', e_)
    if limit is None:
        assert wstate["cur"] == len(WSPECS), (wstate, len(WSPECS))
    P.emit()
    es.close()
    return nc


def make_in_maps(inp, compact_cache=False):
    f = lambda a: np.ascontiguousarray(np.asarray(a))
    gvec = f(np.stack([inp["norm_mix_g"][0], inp["norm_ffn_g"][0], inp["norm_mix_g"][1], inp["norm_ffn_g"][1],
                       inp["norm_final_g"]]))
    chv = f(np.concatenate([inp["conv_a_w"][0], inp["conv_a_b"], inp["ln_a_g"], inp["ln_a_b"], inp["conv_c_w"][0],
                            inp["pool_scale"]], axis=0))
    shared = {
        "gvec": gvec, "chv": chv, "sbb": f(inp["sb_bias"]).reshape(1, 8),
        "poolw": f(inp["pool_w"]).reshape(512, 128),
        "w_in_e": f(inp["w_in_even"][0]), "w_out_e": f(inp["w_out_even"][0]),
        "w_in_o": f(inp["w_in_odd"][0]), "w_out_o": f(inp["w_out_odd"][0]),
        "w_ffn_in": f(inp["w_ffn_in"]).reshape(2 * D, 2 * FFN), "w_ffn_out": f(inp["w_ffn_out"]).reshape(2 * FFN, D),
    }
    ck_full = np.asarray(inp["cache_k"])[0].reshape(-1, 512)
    cv_full = np.asarray(inp["cache_v"])[0].reshape(-1, 512)
    maps = []
    for c in range(NCORES):
        sl = slice(NSEQ * c, NSEQ * (c + 1))
        m = dict(shared)
        m["xp"] = f(inp["x_prompt"][c])
        m["xs"] = f(inp["x_sample"][sl]).reshape(128, D)
        m["st_conf"] = f(inp["state_conformer"][0, sl]).reshape(NSEQ * 30, 512)
        m["st_sc"] = f(inp["state_shortconv"][0, sl]).reshape(NSEQ * 2, 512)
        m["st_pool"] = f(inp["state_pool"][0, sl]).reshape(NSEQ * 15, 512)
        ptc = np.asarray(inp["page_table"])[sl].astype(np.int32)
        if compact_cache:
            pages = ptc.reshape(-1)
            m["ck"] = f(ck_full.reshape(-1, 128, 512)[pages]).reshape(-1, 512)
            m["cv"] = f(cv_full.reshape(-1, 128, 512)[pages]).reshape(-1, 512)
            m["pt"] = np.arange(256, dtype=np.int32).reshape(NSEQ, NPAGE)
        else:
            m["ck"] = ck_full
            m["cv"] = cv_full
            m["pt"] = f(ptc)
        maps.append(m)
    return maps


def gather_outputs(res):
    R = res.results
    cat = lambda k: np.stack([R[c][k] for c in range(NCORES)])
    y_p = cat("yp")
    y_s = np.concatenate([R[c]["ys"].reshape(NSEQ, DSEQ, D) for c in range(NCORES)])
    k_p = cat("kp").reshape(1, NCORES, SEQ, 8, 64)
    v_p = cat("vp").reshape(1, NCORES, SEQ, 8, 64)
    k_s = np.concatenate([R[c]["ks"].reshape(NSEQ, DSEQ, 8, 64) for c in range(NCORES)])[None]
    v_s = np.concatenate([R[c]["vs"].reshape(NSEQ, DSEQ, 8, 64) for c in range(NCORES)])[None]
    conv_p = cat("conv_p")[None]
    conv_s = np.concatenate([R[c]["conv_s"] for c in range(NCORES)])[None]
    sc_p = cat("sc_p")[None]
    sc_s = np.concatenate([R[c]["sc_s"] for c in range(NCORES)])[None]
    pool_p = cat("pool_p")[None]
    pool_s = np.concatenate([R[c]["pool_s"] for c in range(NCORES)])[None]
    outs = (y_p, y_s, k_p, v_p, k_s, v_s, conv_p, conv_s, sc_p, sc_s, pool_p, pool_s)
    return tuple(np.ascontiguousarray(o, dtype=np.float32) for o in outs)


def kernel(**inputs):
    npool = int(np.asarray(inputs["cache_k"]).shape[1])
    nc = build_nc(npool)
    in_maps = make_in_maps(inputs)
    res = run_bass_kernel_spmd(nc, in_maps, core_ids=list(range(NCORES)))
    return gather_outputs(res)
```

```python
import numpy as np
from contextlib import ExitStack
import concourse.bass as bass
import concourse.mybir as mybir
from concourse.bass_utils import run_bass_kernel_spmd

F32, BF16, I32 = mybir.dt.float32, mybir.dt.bfloat16, mybir.dt.int32
AF = mybir.ActivationFunctionType
ALU = mybir.AluOpType
ESZ = {F32: 4, BF16: 2, I32: 4}

NCORES = 8
D = 1024
SEQ = 2048
NSEQ = 16
DSEQ = 8
NPAGE = 16
FFN = 2816
NEG = -240000.0
EPS = 1e-6


class V:
    __slots__ = ("ap", "key", "lo", "hi")

    def __init__(self, ap):
        self.ap = ap
        dims = ap.ap
        pstep = dims[0][0]
        off = ap.offset % pstep if pstep > 0 else ap.offset
        ext = 1
        for st, cnt in dims[1:]:
            ext += (cnt - 1) * abs(st)
        es = ESZ[ap.dtype]
        self.key = ap.tensor.name
        self.lo = off * es
        self.hi = (off + ext) * es
        if str(ap.space) == "PSUM":
            self.lo, self.hi = 0, 2048


class DSem:
    def __init__(self, sem, group=False):
        self.sem = sem
        self.count = 0
        self.group = group


class Op:
    __slots__ = ("eng", "fn", "deps", "sig", "cnt", "isdma", "dsem", "dval", "idx")


class Prog:
    ENGS = ("pe", "act", "dve", "pool", "sp")

    def __init__(self, nc, es):
        self.nc = nc
        self.es = es
        self.q = {e: [] for e in self.ENGS}
        self.recs = {}
        self.esem = {e: es.enter_context(nc.semaphore("sem_" + e)) for e in ("pe", "act", "dve", "pool")}
        self.nsem = 0
        self.out_dsems = []

    def dsem(self, group=False, out=False):
        self.nsem += 1
        d = DSem(self.es.enter_context(self.nc.semaphore("dsem%d" % self.nsem)), group)
        self.out_dsems.append(d)
        return d

    def _access(self, v, op, is_write):
        L = self.recs.get(v.key)
        if L is None:
            L = self.recs[v.key] = []
        raw, other = [], []
        newL = []
        lo, hi = v.lo, v.hi
        cov = []
        for rec in L:
            rlo, rhi, w, rd = rec
            if rhi <= lo or rlo >= hi:
                newL.append(rec)
                continue
            if w is not None:
                (other if is_write else raw).append(w)
            if is_write:
                other.extend(rd.values())
            if rlo < lo:
                newL.append([rlo, lo, w, dict(rd)])
            if rhi > hi:
                newL.append([hi, rhi, w, dict(rd)])
            if not is_write:
                nrd = dict(rd)
                nrd[("d", id(op)) if op.isdma else op.eng] = op
                a, b = max(rlo, lo), min(rhi, hi)
                newL.append([a, b, w, nrd])
                cov.append((a, b))
        if is_write:
            newL.append([lo, hi, op, {}])
        else:
            cov.sort()
            cur = lo
            k = ("d", id(op)) if op.isdma else op.eng
            for a, b in cov:
                if a > cur:
                    newL.append([cur, a, None, {k: op}])
                cur = max(cur, b)
            if cur < hi:
                newL.append([cur, hi, None, {k: op}])
        self.recs[v.key] = newL
        return raw, other

    def add(self, eng, fn, reads=(), writes=(), dsem=None):
        op = Op()
        op.eng = eng
        op.fn = fn
        op.isdma = dsem is not None
        op.sig = False
        op.cnt = 0
        op.idx = len(self.q[eng])
        raw, other = [], []
        for v in reads:
            r, o = self._access(v, op, v.key.startswith("ps"))
            raw += r
            other += o
        for v in writes:
            r, o = self._access(v, op, True)
            raw += r
            other += o
        deps = {}
        for lst, is_raw in ((raw, True), (other, False)):
            for d in lst:
                if d is op:
                    continue
                if d.isdma:
                    deps[("dsem", id(d.dsem))] = (d, d.dsem.count)
                    continue
                if (not op.isdma) and d.eng == eng:
                    if eng == "pe":
                        continue
                k = d.eng
                if k not in deps or deps[k].idx < d.idx:
                    deps[k] = d
        op.deps = [x if isinstance(x, tuple) else (x, None) for x in deps.values()]
        for d, _ in op.deps:
            if not d.isdma:
                d.sig = True
        if op.isdma:
            dsem.count += 16
            op.dsem = dsem
            op.dval = dsem.count
        self.q[eng].append(op)
        return op

    def emit(self):
        nc = self.nc
        for eng, L in self.q.items():
            c = 0
            for op in L:
                if (not op.isdma) and op.sig:
                    c += 1
                    op.cnt = c
        engobj = {"pe": "tensor", "act": "scalar", "dve": "vector", "pool": "gpsimd", "sp": "sync"}
        with nc.Block() as block:
            for eng in self.ENGS:
                def body(e, eng=eng):
                    waited = {}
                    for op in self.q[eng]:
                        for d, dv in op.deps:
                            if d.isdma:
                                sem = d.dsem.sem
                                val = d.dsem.count if d.dsem.group else dv
                            else:
                                sem = self.esem[d.eng]
                                val = d.cnt
                            if waited.get(sem.num, 0) < val:
                                e.wait_ge(sem, val)
                                waited[sem.num] = val
                        ins = op.fn(e)
                        if op.isdma:
                            ins.then_inc(op.dsem.sem, 16)
                        elif op.sig:
                            ins.then_inc(self.esem[eng], 1)
                    if eng == "sp":
                        for d in self.out_dsems:
                            if d.count > 0:
                                e.wait_ge(d.sem, d.count)
                getattr(block, engobj[eng])(body)


class _Stop(Exception):
    pass


def build_nc(npool, limit=None):
    nc = bass.Bass("TRN2", target_bir_lowering=False)
    es = ExitStack()
    P = Prog(nc, es)

    def din(name, shape, dt=F32):
        return nc.dram_tensor(name, list(shape), dt, kind="ExternalInput").ap()

    def dout(name, shape, dt=F32):
        return nc.dram_tensor(name, list(shape), dt, kind="ExternalOutput").ap()

    xp = din("xp", [SEQ, D])
    xs = din("xs", [128, D])
    ck = din("ck", [npool * 128, 512])
    cv = din("cv", [npool * 128, 512])
    st_conf = din("st_conf", [NSEQ * 30, 512])
    st_sc = din("st_sc", [NSEQ * 2, 512])
    st_pool = din("st_pool", [NSEQ * 15, 512])
    pt = din("pt", [NSEQ, NPAGE], I32)
    gvec = din("gvec", [5, D])
    chvd = din("chv", [38, 512])
    sbb = din("sbb", [1, 8])
    poolwd = din("poolw", [512, 128])
    w_in_e = din("w_in_e", [D, 2560])
    w_out_e = din("w_out_e", [D, D])
    w_in_o = din("w_in_o", [D, 2048])
    w_out_o = din("w_out_o", [D, D])
    w_ffn_in = din("w_ffn_in", [2 * D, 2 * FFN])
    w_ffn_out = din("w_ffn_out", [2 * FFN, D])

    yp = dout("yp", [SEQ, D])
    ys = dout("ys", [128, D])
    kp = dout("kp", [SEQ, 512])
    vp = dout("vp", [SEQ, 512])
    ks = dout("ks", [128, 512])
    vs = dout("vs", [128, 512])
    conv_p = dout("conv_p", [30, 512])
    conv_s = dout("conv_s", [NSEQ, 30, 512])
    sc_p = dout("sc_p", [2, 512])
    sc_s = dout("sc_s", [NSEQ, 2, 512])
    pool_p = dout("pool_p", [15, 512])
    pool_s = dout("pool_s", [NSEQ, 15, 512])

    def sb(name, shape, dt):
        return es.enter_context(nc.sbuf_tensor(name, list(shape), dt))

    def ps(name):
        return es.enter_context(nc.psum_tensor(name, [128, 512], F32))

    TMAX = 1152
    h = sb("h", [128, 9, D], F32)
    actA = sb("actA", [128, 8, TMAX], BF16)
    KT = sb("KT", [128, 4, SEQ + 128], BF16)
    Vall = sb("Vall", [128, 16, 512], BF16)
    wslot = sb("wslot", [128, 2, 6144], BF16)
    gb = sb("gb", [128, 2, D], F32)
    stage = sb("stage", [128, 2, D], F32)
    xn = sb("xn", [128, 2, D], BF16)
    identb = sb("identb", [128, 128], BF16)
    identf = sb("identf", [128, 128], F32)
    tinc = sb("tinc", [128, 128], BF16)
    onesb = sb("onesb", [128, 128], BF16)
    negm = sb("negm", [128, 128], BF16)
    onesf = sb("onesf", [128, 128], F32)
    chv = sb("chvs", [128, 4, 38], F32)
    chraw = sb("chraw", [38, 512], F32)
    sbias = sb("sbias", [128, 8], F32)
    biasd = sb("biasd", [128, 64], F32)
    epst = sb("epst", [128, 1], F32)
    poolw = sb("poolws", [128, 4, 128], BF16)
    rc = sb("rc", [128, 4, 16], F32)
    stats = sb("stats", [128, 64], F32)
    ptb = sb("ptb", [128, NSEQ * NPAGE], I32)
    pidx = sb("pidx", [128, NSEQ * NPAGE], I32)
    iot = sb("iot", [128, 1], F32)
    hist_a = sb("hist_a", [128, 4, 30], BF16)
    hist_c = sb("hist_c", [128, 4, 2], BF16)
    hist_u = sb("hist_u", [128, 4, 15], BF16)
    MIXB = 66 * 1024
    mix = sb("mix", [128, MIXB // 2], BF16)
    banks = [ps("ps%d" % i) for i in range(8)]

    maskN = sb("maskN", [128, 128], BF16)
    ebl = sb("ebl", [16, 128], F32)
    eblb = sb("eblb", [16, 128], BF16)

    def mixv(off_bytes, nelem, dt):
        assert off_bytes % 4 == 0
        assert off_bytes + nelem * ESZ[dt] <= MIXB, (off_bytes, nelem, MIXB)
        a = mix[:, off_bytes // 2: off_bytes // 2 + nelem * ESZ[dt] // 2]
        return a if dt == BF16 else a.bitcast(dt)

    def rv(x):
        return x if isinstance(x, V) else V(x)

    def act(out, in_, func, bias=0.0, scale=1.0, accum=None):
        out, in_ = rv(out), rv(in_)
        rd = [in_]
        wr = [out]
        kw = {}
        if hasattr(bias, "ap"):
            bias = rv(bias)
            rd.append(bias)
            kw["bias"] = bias.ap
        else:
            kw["bias"] = float(bias)
        if hasattr(scale, "ap"):
            scale = rv(scale)
            rd.append(scale)
            kw["scale"] = scale.ap
        else:
            kw["scale"] = float(scale)
        if accum is not None:
            accum = rv(accum)
            wr.append(accum)
            kw["accum_out"] = accum.ap
        return P.add("act", lambda e: e.activation(out=out.ap, in_=in_.ap, func=func, **kw), rd, wr)

    def tt(eng, out, in0, in1, op):
        out, in0, in1 = rv(out), rv(in0), rv(in1)
        return P.add(eng, lambda e: e.tensor_tensor(out=out.ap, in0=in0.ap, in1=in1.ap, op=op), [in0, in1], [out])

    def ts(eng, out, in0, s1, op0, s2=None, op1=None):
        out, in0 = rv(out), rv(in0)
        rd = [in0]
        a1 = s1
        if hasattr(s1, "ap"):
            s1 = rv(s1)
            rd.append(s1)
            a1 = s1.ap
        a2 = s2
        if s2 is not None and hasattr(s2, "ap"):
            s2 = rv(s2)
            rd.append(s2)
            a2 = s2.ap
        if op1 is None:
            return P.add(eng, lambda e: e.tensor_scalar(out=out.ap, in0=in0.ap, scalar1=a1, scalar2=None, op0=op0), rd, [out])
        return P.add(eng, lambda e: e.tensor_scalar(out=out.ap, in0=in0.ap, scalar1=a1, scalar2=a2, op0=op0, op1=op1), rd, [out])

    def stt(eng, out, in0, s, in1, op0, op1):
        out, in0, in1 = rv(out), rv(in0), rv(in1)
        rd = [in0, in1]
        a = s
        if hasattr(s, "ap"):
            s = rv(s)
            rd.append(s)
            a = s.ap
        return P.add(eng, lambda e: e.scalar_tensor_tensor(out=out.ap, in0=in0.ap, scalar=a, in1=in1.ap, op0=op0, op1=op1), rd, [out])

    def cp(eng, out, in_):
        out, in_ = rv(out), rv(in_)
        if eng == "act":
            return P.add("act", lambda e: e.copy(out=out.ap, in_=in_.ap), [in_], [out])
        return P.add(eng, lambda e: e.tensor_copy(out=out.ap, in_=in_.ap), [in_], [out])

    def mset(eng, out, val):
        out = rv(out)
        return P.add(eng, lambda e: e.memset(out.ap, val), [], [out])

    def asel(out, in_, pattern, op, base, cm):
        out, in_ = rv(out), rv(in_)
        return P.add("pool", lambda e: e.affine_select(out=out.ap, in_=in_.ap, pattern=pattern, compare_op=op, fill=0.0,
                                                       base=base, channel_multiplier=cm), [in_], [out])

    def mm(out, lhsT, rhs, start, stop, tp=None, sgc=False):
        out, lhsT, rhs = rv(out), rv(lhsT), rv(rhs)
        kw = {}
        if tp is not None:
            kw["tile_position"] = tp
        if sgc:
            kw["skip_group_check"] = True
        rd = [lhsT, rhs] + ([] if start else [out])
        return P.add("pe", lambda e: e.matmul(out.ap, lhsT=lhsT.ap, rhs=rhs.ap, start=start, stop=stop, **kw), rd, [out])

    def tr(out, in_, ident):
        out, in_, ident = rv(out), rv(in_), rv(ident)
        return P.add("pe", lambda e: e.transpose(out.ap, in_.ap, ident.ap), [in_, ident], [out])

    def dma(q, out, in_, dsem, rd=(), wr=()):
        return P.add(q, lambda e: e.dma_start(out=out, in_=in_), [rv(x) for x in rd], [rv(x) for x in wr], dsem=dsem)

    ckstate = {"n": 0}

    def chk(name):
        ckstate["n"] += 1
        if limit is not None and ckstate["n"] > limit:
            raise _Stop(name)

    cdsem = P.dsem(group=True)

    def cload(out_ap, in_ap):
        dma("sp", out_ap, in_ap, cdsem, wr=[out_ap])

    def lload(out_ap, in_ap):
        dma("sp", out_ap, in_ap, P.dsem(), wr=[out_ap])

    wsems = [[P.dsem(), P.dsem()], [P.dsem(), P.dsem()]]
    WSPECS = []

    def wspec_layer0():
        for b in range(2):
            WSPECS.append([(w_in_e[:, b * 256:(b + 1) * 256], 8, 0, 256, 512),
                           (w_in_e[:, 512 + b * 256:512 + (b + 1) * 256], 8, 256, 256, 512)])
        for q0 in (1024, 1536, 2048):
            WSPECS.append([(w_in_e[:, q0:q0 + 512], 8, 0, 512, 512)])

    def wspec_out(w):
        for ch in range(2):
            WSPECS.append([(w[:, ch * 512:(ch + 1) * 512], 8, 0, 512, 512)])

    def wspec_ffn(layer):
        wi = w_ffn_in[layer * D:(layer + 1) * D, :]
        wo = w_ffn_out[layer * FFN:(layer + 1) * FFN, :]
        for (b0, b1) in ((0, 6), (6, 11)):
            for b in range(b0, b1):
                WSPECS.append([(wi[:, b * 256:(b + 1) * 256], 8, 0, 256, 512),
                               (wi[:, FFN + b * 256:FFN + (b + 1) * 256], 8, 256, 256, 512)])
            nch = 2 * (b1 - b0)
            for ch in range(2):
                WSPECS.append([(wo[b0 * 256:b1 * 256, ch * 512:(ch + 1) * 512], nch, 0, 512, 512)])

    def wspec_layer1():
        for b in range(2):
            WSPECS.append([(w_in_o[:, b * 256:(b + 1) * 256], 8, 0, 256, 512),
                           (w_in_o[:, 1536 + b * 256:1536 + (b + 1) * 256], 8, 256, 256, 512)])
        for b in range(2):
            WSPECS.append([(w_in_o[:, 512 + b * 256:512 + (b + 1) * 256], 8, 0, 256, 512),
                           (w_in_o[:, 1024 + b * 256:1024 + (b + 1) * 256], 8, 256, 256, 512)])

    for _g in range(2):
        wspec_layer0()
        wspec_out(w_out_e)
        wspec_ffn(0)
        wspec_layer1()
        wspec_out(w_out_o)
        wspec_ffn(1)
    wstate = {"cur": 0, "issued": 0}

    def wissue(i):
        s = i % 2
        for pi, (src, nk, c0, ncols, stride) in enumerate(WSPECS[i]):
            dst = wslot[:, s, 0:nk * stride].rearrange("p (k n) -> p k n", k=nk)[:, :, c0:c0 + ncols]
            srcv = src.rearrange("(k p) n -> p k n", p=128)
            P.add("pool", lambda e, dst=dst, srcv=srcv: e.dma_start(out=dst, in_=srcv), [], [V(dst)], dsem=wsems[s][pi])

    def wget(nk, stride=512):
        i = wstate["cur"]
        wstate["cur"] += 1
        while wstate["issued"] <= min(i + 1, len(WSPECS) - 1):
            wissue(wstate["issued"])
            wstate["issued"] += 1
        assert WSPECS[i][0][1] == nk, (i, nk, WSPECS[i][0][1])
        return wslot[:, i % 2, 0:nk * stride].rearrange("p (k n) -> p k n", k=nk)

    wissue(0)
    wissue(1)
    wstate["issued"] = 2

    cload(chraw[0:38, :], chvd)
    cload(sbias[:], sbb.rearrange("a b -> (a b)").partition_broadcast(128))
    cload(ptb[:], pt.rearrange("a b -> (a b)").partition_broadcast(128))
    pw32 = mixv(0, 512, F32).rearrange("p (g d) -> p g d", g=4)
    cload(pw32, poolwd.rearrange("(g c) d -> c g d", c=128))
    cp("dve", poolw[:], pw32)

    mset("pool", onesf[:], 1.0)
    mset("pool", epst[:], EPS)
    asel(identf[:], onesf[:], [[-1, 128]], ALU.is_equal, 0, 1)
    tincf = mixv(4096, 128, F32)
    asel(tincf, onesf[:], [[-1, 128]], ALU.is_ge, 0, 1)
    cp("pool", identb[:], identf[:])
    cp("pool", tinc[:], tincf)
    cp("pool", onesb[:], onesf[:])
    ts("pool", negm[:], tincf, NEG, ALU.mult)
    P.add("pool", lambda e: e.iota(iot[:], [[0, 1]], base=0, channel_multiplier=1, allow_small_or_imprecise_dtypes=True),
          [], [V(iot[:])])
    ts("dve", pidx[:], ptb[:], 128.0, ALU.mult, iot[:, 0:1], ALU.add)
    cp("dve", biasd[:].rearrange("p (h q) -> p h q", q=8), sbias[:].unsqueeze(2).broadcast_to([128, 8, 8]))
    for c in range(4):
        tr(banks[0][0:128, c * 64: c * 64 + 38], chraw[0:38, c * 128:(c + 1) * 128], identf[0:38, 0:38])
    cp("dve", chv[:], banks[0][:, 0:256].rearrange("p (c r) -> p c r", c=4)[:, :, 0:38])
    rcf = mixv(8192, 16, F32)
    P.add("pool", lambda e: e.iota(rcf, [[1, 16]], base=1, channel_multiplier=0, allow_small_or_imprecise_dtypes=True),
          [], [V(rcf)])
    for g, w in enumerate((2, 4, 8, 16)):
        ts("dve", rc[:, g, :], rcf, float(w), ALU.min)
    P.add("dve", lambda e: e.reciprocal(out=rc[:], in_=rc[:]), [V(rc[:])], [V(rc[:])])
    mset("pool", ebl[:], 1.0)
    asel(ebl[:], ebl[:], [[1, 128]], ALU.is_ge, 0, -8)
    asel(ebl[:], ebl[:], [[-1, 128]], ALU.is_ge, 7, 8)
    cp("pool", eblb[:], ebl[:])
    mm(banks[1][:, 0:128], eblb[0:16, :], eblb[0:16, :], True, True)
    blk = mixv(12288, 128, F32)
    cp("dve", blk, banks[1][:, 0:128])
    asel(blk, blk, [[1, 128]], ALU.is_gt, 0, -1)
    ts("dve", maskN[:], blk, -NEG, ALU.mult, NEG, ALU.add)

    groups = [
        dict(ptiles=list(range(0, 9)), sample=False),
        dict(ptiles=list(range(9, 16)), sample=True),
    ]
    xsems = [P.dsem() for _ in range(9)]
    gsem = [P.dsem(), P.dsem()]
    osem = [P.dsem(out=True), P.dsem(out=True)]
    ostate = {"n": 0}

    def stage_slot():
        i = ostate["n"] % 2
        ostate["n"] += 1
        return i

    gstate = {"n": 0}

    def load_g(row):
        i = gstate["n"] % 2
        gstate["n"] += 1
        dma("sp", gb[:, i, :], gvec[row].partition_broadcast(128), gsem[i], wr=[gb[:, i, :]])
        return i

    statc = {"n": 0}

    def stat_col(n=3):
        c = statc["n"]
        if c + n > 64:
            c = 0
        statc["n"] = c + n
        return c

    def col_tiles(lo, hi, step=512):
        r = []
        c = lo
        while c < hi:
            r.append((c, min(step, hi - c)))
            c += step
        return r

    bank_rr = {"n": 0}

    def nbank(lo=2, hi=8):
        b = lo + bank_rr["n"] % (hi - lo)
        bank_rr["n"] += 1
        return banks[b]

    def run_groups():
        for gi, G in enumerate(groups):
            ptiles = G["ptiles"]
            npt = len(ptiles)
            ntl = npt + (1 if G["sample"] else 0)
            TP = npt * 128
            T = ntl * 128
            first_tok = ptiles[0] * 128
            hnT = actA
            cat = actA

            for lt, gt in enumerate(ptiles):
                dma("sp", h[:, lt, :], xp[gt * 128:(gt + 1) * 128, :], xsems[lt], wr=[h[:, lt, :]])
            if G["sample"]:
                dma("sp", h[:, npt, :], xs, xsems[npt], wr=[h[:, npt, :]])

            def norm_T(grow):
                gslot = load_g(grow)
                c0 = stat_col(3 * ntl)
                halves = [(0, (ntl + 1) // 2), ((ntl + 1) // 2, ntl)]
                for (t0_, t1_) in halves:
                    for lt in range(t0_, t1_):
                        act(xn[:, lt % 2, :], h[:, lt, :], AF.Square, accum=stats[:, c0 + lt:c0 + lt + 1])
                    act(stats[:, c0 + ntl + t0_:c0 + ntl + t1_], stats[:, c0 + t0_:c0 + t1_], AF.Ln, bias=epst[:, 0:1], scale=1.0 / D)
                    act(stats[:, c0 + 2 * ntl + t0_:c0 + 2 * ntl + t1_], stats[:, c0 + ntl + t0_:c0 + ntl + t1_], AF.Exp, scale=-0.5)
                for lt in range(ntl):
                    hv = h[:, lt, :]
                    xs_ = lt % 2
                    stt("dve", xn[:, xs_, :], hv, stats[:, c0 + 2 * ntl + lt:c0 + 2 * ntl + lt + 1], gb[:, gslot, :], ALU.mult, ALU.mult)
                    pb = banks[lt % 2]
                    pbv = pb[:].bitcast(BF16)
                    for kc in range(8):
                        tr(pbv[:, kc * 128:(kc + 1) * 128], xn[:, xs_, kc * 128:(kc + 1) * 128], identb[:])
                    cp("dve" if lt % 2 == 0 else "act", hnT[:, :, lt * 128:(lt + 1) * 128],
                       pbv.rearrange("p (k n) -> p k n", k=8))

            def fm_mm(psv, wv, wc0, c0, n):
                for kc in range(8):
                    mm(psv, wv[:, kc, wc0:wc0 + 128], hnT[:, kc, c0:c0 + n], kc == 0, kc == 7)

            def tm_mm(psv, wv, wc0, ncols, lt):
                for kc in range(8):
                    mm(psv, hnT[:, kc, lt * 128:(lt + 1) * 128], wv[:, kc, wc0:wc0 + ncols], kc == 0, kc == 7)

            def residual_proj(nk_total, src):
                for ch in range(2):
                    wv = wget(nk_total)
                    for lt in range(ntl):
                        pb = nbank()
                        for kc in range(nk_total):
                            mm(pb[:], src[:, kc, lt * 128:(lt + 1) * 128], wv[:, kc, :], kc == 0, kc == nk_total - 1)
                        tt("dve", h[:, lt, ch * 512:(ch + 1) * 512], h[:, lt, ch * 512:(ch + 1) * 512], pb[:], ALU.add)

            def ffn(layer):
                hid = mixv(0, 12 * T, BF16).rearrange("p (c t) -> p c t", c=12)
                sg = mixv(12 * T * 2, 2 * 512, BF16).rearrange("p (a n) -> p a n", a=2)
                norm_T(1 + 2 * layer)
                for half, (b0, b1) in enumerate(((0, 6), (6, 11))):
                    nch = 2 * (b1 - b0)
                    for b in range(b0, b1):
                        wv = wget(8)
                        for fc in range(2):
                            lc = 2 * (b - b0) + fc
                            for (c0, n) in col_tiles(0, T):
                                pg = nbank()
                                pu = nbank()
                                fm_mm(pg[:, 0:n], wv, fc * 128, c0, n)
                                fm_mm(pu[:, 0:n], wv, 256 + fc * 128, c0, n)
                                k = bank_rr["n"] % 2
                                act(sg[:, k, 0:n], pg[:, 0:n], AF.Silu)
                                tt("dve", hid[:, lc, c0:c0 + n], pu[:, 0:n], sg[:, k, 0:n], ALU.mult)
                    residual_proj(nch, hid)

            norm_T(0)
            o = 0
            QT = mixv(o, 4 * T, BF16).rearrange("p (c t) -> p c t", c=4); o += 4 * T * 2
            sgt = mixv(o, 2 * 512, F32).rearrange("p (a n) -> p a n", a=2)
            vs_tok = mixv(o, 512, BF16)
            lntmp = mixv(o + 1024, 3 * 512, BF16).rearrange("p (a n) -> p a n", a=3)
            o += 4096
            R1 = o
            extp = mixv(o, 4 * (30 + TP), BF16).rearrange("p (c t) -> p c t", c=4); o += 4 * (30 + TP) * 2
            if G["sample"]:
                exts = mixv(o, 4 * 16 * 38, BF16).rearrange("p (c s t) -> p c s t", c=4, s=16); o += 4 * 16 * 38 * 2
            diag2 = mixv(o, 2 * 31 * 128, BF16).rearrange("p (d j n) -> p d j n", d=2, j=31); o += 2 * 31 * 128 * 2
            a16o = o
            a16 = mixv(o, 4 * T, BF16).rearrange("p (c t) -> p c t", c=4); o += 4 * T * 2
            a2 = mixv(o, 4 * T, BF16).rearrange("p (c t) -> p c t", c=4); o += 4 * T * 2
            lnt = mixv(o, 4 * 512, F32).rearrange("p (a n) -> p a n", a=4); o += 4 * 2048
            assert o <= MIXB, o

            if gi == 0:
                mset("pool", extp[:, :, 0:30], 0.0)
            else:
                cp("pool", extp[:, :, 0:30], hist_a[:])
                stt_ = mixv(a16o, 4 * 512, F32).rearrange("p (a n) -> p a n", a=4)
                for i in range(4):
                    lload(stt_[0:120, i, :], st_conf[i * 120:(i + 1) * 120, :])
                for i in range(4):
                    pb = nbank()
                    for c in range(4):
                        tr(pb[:, c * 128:c * 128 + 120], stt_[0:120, i, c * 128:(c + 1) * 128], identf[0:120, 0:120])
                    cp("dve", exts[:, :, 4 * i:4 * i + 4, 0:30],
                       pb[:].rearrange("p (c x) -> p c x", c=4)[:, :, 0:120].rearrange("p c (s r) -> p c s r", s=4))
                P.add("sp", lambda e: e.dma_start(out=conv_s[:, 0:22, :], in_=st_conf.rearrange("(s r) f -> s r f", r=30)[:, 8:30, :]),
                      [], [], dsem=P.dsem(out=True))

            chk('L0 norm gi=%d' % gi)
            for b in range(2):
                wv = wget(8)
                for fc in range(2):
                    c = 2 * b + fc
                    for (c0, n) in col_tiles(0, T):
                        pa = nbank()
                        pg = nbank()
                        fm_mm(pa[:, 0:n], wv, fc * 128, c0, n)
                        fm_mm(pg[:, 0:n], wv, 256 + fc * 128, c0, n)
                        k = bank_rr["n"] % 2
                        act(sgt[:, k, 0:n], pg[:, 0:n], AF.Sigmoid)
                        npp = max(0, min(n, TP - c0))
                        if npp > 0:
                            tt("dve", extp[:, c, 30 + c0:30 + c0 + npp], pa[:, 0:npp], sgt[:, k, 0:npp], ALU.mult)
                        if npp < n:
                            assert n - npp == 128
                            tt("dve", exts[:, c, :, 30:38], pa[:, npp:n].rearrange("p (s t) -> p s t", t=8),
                               sgt[:, k, npp:n].rearrange("p (s t) -> p s t", t=8), ALU.mult)
                if gi == 1:
                    for lt, dst in ((npt - 1, "p"), (npt, "s")):
                        pa = nbank()
                        tm_mm(pa[:], wv, 0, 512, lt)
                        si = stage_slot()
                        act(stage[:, si, 0:256], pa[:, 256:512], AF.Sigmoid)
                        tt("dve", stage[:, si, 256:512], pa[:, 0:256], stage[:, si, 0:256], ALU.mult)
                        if dst == "p":
                            dma("sp", conv_p[:, b * 256:(b + 1) * 256], stage[98:128, si, 256:512], osem[si], rd=[stage[:, si, 256:512]])
                        else:
                            for s_ in range(NSEQ):
                                dma("sp", conv_s[s_, 22:30, b * 256:(b + 1) * 256], stage[8 * s_:8 * s_ + 8, si, 256:512], osem[si],
                                    rd=[stage[:, si, 256:512]])
            chk('L0 glu gi=%d' % gi)
            wv = wget(8)
            for c in range(4):
                for (c0, n) in col_tiles(0, T):
                    pb = nbank()
                    fm_mm(pb[:, 0:n], wv, c * 128, c0, n)
                    cp("act", QT[:, c, c0:c0 + n], pb[:, 0:n])
            wv = wget(8)
            for c in range(4):
                for (c0, n) in col_tiles(0, T):
                    pb = nbank()
                    fm_mm(pb[:, 0:n], wv, c * 128, c0, n)
                    npp = max(0, min(n, TP - c0))
                    if npp > 0:
                        cp("act", KT[:, c, first_tok + c0:first_tok + c0 + npp], pb[:, 0:npp])
                    if npp < n:
                        cp("act", KT[:, c, SEQ:SEQ + 128], pb[:, npp:n])
            for lt in range(ntl):
                pb = nbank()
                tm_mm(pb[:], wv, 0, 512, lt)
                si = stage_slot()
                cp("dve", stage[:, si, 0:512], pb[:])
                dst = kp[ptiles[lt] * 128:(ptiles[lt] + 1) * 128, :] if lt < npt else ks
                dma("sp", dst, stage[:, si, 0:512], osem[si], rd=[stage[:, si, 0:512]])
            wv = wget(8)
            for lt in range(ntl):
                pb = nbank()
                tm_mm(pb[:], wv, 0, 512, lt)
                si = stage_slot()
                cp("dve", stage[:, si, 0:512], pb[:])
                if lt < npt:
                    cp("act", Vall[:, ptiles[lt], :], pb[:])
                    dst = vp[ptiles[lt] * 128:(ptiles[lt] + 1) * 128, :]
                else:
                    cp("act", vs_tok, pb[:])
                    dst = vs
                dma("sp", dst, stage[:, si, 0:512], osem[si], rd=[stage[:, si, 0:512]])

            chk('L0 qkv gi=%d' % gi)
            for c in range(4):
                diag = diag2[:, c % 2]
                for j in range(31):
                    ts("dve", diag[:, j, :], identb[:], chv[:, c, j:j + 1], ALU.mult)
                for (c0, n) in col_tiles(0, TP):
                    pb = nbank()
                    for j in range(31):
                        mm(pb[:, 0:n], diag[:, j, :], extp[:, c, c0 + j:c0 + j + n], j == 0, j == 30)
                    act(a16[:, c, c0:c0 + n], pb[:, 0:n], AF.Identity, bias=chv[:, c, 31:32])
                    act(a2[:, c, c0:c0 + n], pb[:, 0:n], AF.Square, bias=chv[:, c, 31:32])
                if G["sample"]:
                    pb = nbank()
                    for j in range(31):
                        mm(pb[:, 0:128].rearrange("p (s t) -> p s t", t=8), diag[:, j, :], exts[:, c, :, j:j + 8], j == 0, j == 30)
                    act(a16[:, c, TP:T], pb[:, 0:128], AF.Identity, bias=chv[:, c, 31:32])
                    act(a2[:, c, TP:T], pb[:, 0:128], AF.Square, bias=chv[:, c, 31:32])
            if gi == 0:
                cp("pool", hist_a[:], extp[:, :, TP:TP + 30])
            cts = col_tiles(0, T)
            stat_ps = []
            for ci, (c0, n) in enumerate(cts):
                pm = banks[2 + 2 * ci]
                pv_ = banks[3 + 2 * ci]
                for c in range(4):
                    mm(pm[:, 0:n], onesb[:], a16[:, c, c0:c0 + n], c == 0, c == 3)
                for c in range(4):
                    mm(pv_[:, 0:n], onesb[:], a2[:, c, c0:c0 + n], c == 0, c == 3)
                stat_ps.append((pm, pv_))
            ti = 0
            for ci, (c0, n) in enumerate(cts):
                pm, pv_ = stat_ps[ci]
                mean = lnt[:, 2 * (ci % 2), 0:n]
                rstd = lnt[:, 2 * (ci % 2) + 1, 0:n]
                ts("dve", mean, pm[:, 0:n], 1.0 / 512, ALU.mult)
                tt("pool", rstd, mean, mean, ALU.mult)
                stt("dve", rstd, pv_[:, 0:n], 1.0 / 512, rstd, ALU.mult, ALU.subtract)
                act(rstd, rstd, AF.Ln, bias=epst[:, 0:1])
                act(rstd, rstd, AF.Exp, scale=-0.5)
                for c in range(4):
                    tmp = lntmp[:, ti % 3, 0:n]
                    ti += 1
                    tt("dve", tmp, a16[:, c, c0:c0 + n], mean, ALU.subtract)
                    tt("dve", tmp, tmp, rstd, ALU.mult)
                    act(cat[:, c, c0:c0 + n], tmp, AF.Silu, bias=chv[:, c, 33:34], scale=chv[:, c, 32:33])
            chk('L0 conformer gi=%d' % gi)
            o = R1
            ebuf = mixv(o, 3 * 512, F32).rearrange("p (a n) -> p a n", a=3); o += 6144
            xbuf = mixv(o, 2 * 512, F32).rearrange("p (a n) -> p a n", a=2); o += 4096
            spb = mixv(o, 3 * 512, BF16).rearrange("p (a n) -> p a n", a=3); o += 3072
            wb = mixv(o, 2 * 512, BF16).rearrange("p (a n) -> p a n", a=2); o += 2048
            lacc = mixv(o, 3 * 512, BF16).rearrange("p (a n) -> p a n", a=3); o += 3072
            DEC0 = o
            SMP = G["sample"]
            its = []
            qgroups = [ptiles[i:i + 4] for i in range(0, npt, 4)]
            for qg in qgroups:
                gq0, gq1 = qg[0], qg[-1]
                NQ = len(qg) * 128
                ql0 = (gq0 - ptiles[0]) * 128
                for hp in range(4):
                    for hh in range(2):
                        for j in range(gq1, -1, -1):
                            its.append(dict(gq0=gq0, gq1=gq1, NQ=NQ, ql0=ql0, hp=hp, hh=hh, j=j,
                                            first=(j == gq1), last=(j == 0), i=len(its),
                                            off=(max(gq0, j) - gq0) * 128))

            def stageA(I):
                i, off, NQ, hp, hh, j = I["i"], I["off"], I["NQ"], I["hp"], I["hh"], I["j"]
                hsl = slice(hh * 64, hh * 64 + 64)
                hd = 2 * hp + hh
                pS = banks[2] if SMP else banks[2 + i % 2]
                k3 = i % 3
                diagblk = j >= I["gq0"]
                mm(pS[:, off:NQ], KT[hsl, hp, j * 128:(j + 1) * 128], QT[hsl, hp, I["ql0"] + off:I["ql0"] + NQ], True, not diagblk)
                if diagblk:
                    mm(pS[:, off:off + 128], identb[:], negm[:], False, True)
                act(ebuf[:, k3, off:NQ], pS[:, off:NQ], AF.Exp, bias=sbias[:, hd:hd + 1], scale=0.125)
                act(spb[:, k3, off:NQ], ebuf[:, k3, off:NQ], AF.Ln, bias=1.0)
                if not I["last"]:
                    ln_ = (i + 1) % 3
                    le = "dve" if SMP else "pool"
                    if off > 0:
                        mset(le, lacc[:, ln_, 0:off], 0.0)
                    if I["first"]:
                        cp(le, lacc[:, ln_, off:NQ], spb[:, k3, off:NQ])
                    else:
                        tt(le, lacc[:, ln_, off:NQ], lacc[:, i % 3, off:NQ], spb[:, k3, off:NQ], ALU.add)

            def stageB(I):
                i, off, NQ, hp, hh, j = I["i"], I["off"], I["NQ"], I["hp"], I["hh"], I["j"]
                pA = banks[4] if SMP else banks[4 + i % 2]
                k3 = i % 3
                k2 = i % 2
                mm(pA[:, off:NQ], tinc[:], spb[:, k3, off:NQ], True, I["first"])
                if not I["first"]:
                    mm(pA[:, off:NQ], onesb[:], lacc[:, i % 3, off:NQ], False, True)
                act(xbuf[:, k2, off:NQ], pA[:, off:NQ], AF.Exp, scale=-1.0)
                tt("dve", wb[:, k2, off:NQ], ebuf[:, k3, off:NQ], xbuf[:, k2, off:NQ], ALU.mult)

            def stageC(I):
                i, off, NQ, hp, hh, j = I["i"], I["off"], I["NQ"], I["hp"], I["hh"], I["j"]
                hsl = slice(hh * 64, hh * 64 + 64)
                hd = 2 * hp + hh
                pB = banks[5] if SMP else banks[6 + (hp % 2)]
                k2 = i % 2
                mm(pB[hsl, off:NQ], Vall[:, j, hd * 64:(hd + 1) * 64], wb[:, k2, off:NQ], I["first"], I["last"],
                   tp=(0, 64) if hh == 1 else None, sgc=True)
                if I["last"] and hh == 1:
                    cp("dve", cat[:, 4 + hp, I["ql0"]:I["ql0"] + NQ], pB[:, 0:NQ])

            dec_hook = None
            if SMP:
                sc0 = TP
                o = DEC0
                qblk = mixv(o, 4 * 16 * 16, BF16).rearrange("p (c s x) -> p c s x", c=4, s=16); o += 2048
                kring = mixv(o, 8 * 512, BF16).rearrange("p (a f) -> p a f", a=8); o += 8192
                ktp = mixv(o, 2 * 512, BF16).rearrange("p (a f) -> p a f", a=2); o += 2048
                vring = mixv(o, 8 * 512, BF16).rearrange("p (a f) -> p a f", a=8)
                eN = mixv(o, 1024, F32)
                o += 8192
                zbb = mixv(o, 2 * 512, F32).rearrange("p (a n) -> p a n", a=2)
                xdN = mixv(o, 1024, F32)
                o += 4096
                xd = mixv(o, 512, F32); o += 2048
                spdb = mixv(o, 2 * 512, BF16).rearrange("p (a n) -> p a n", a=2); o += 2048
                wd = mixv(o, 512, BF16); o += 1024
                spN = mixv(o, 1024, BF16); o += 2048
                wN = mixv(o, 1024, BF16); o += 2048
                sufa = mixv(o, 512, BF16); o += 1024
                sufb = mixv(o, 512, BF16); o += 1024
                sphs = mixv(o, 64, BF16); o += 128
                assert o <= MIXB, o
                ksem = [P.dsem() for _ in range(8)]
                vsem = [P.dsem() for _ in range(8)]
                mset("pool", qblk[:], 0.0)
                for hp in range(4):
                    cp("pool", qblk[0:64, hp, :, 0:8], QT[0:64, hp, sc0:sc0 + 128].rearrange("p (s t) -> p s t", t=8))
                    cp("pool", qblk[64:128, hp, :, 8:16], QT[64:128, hp, sc0:sc0 + 128].rearrange("p (s t) -> p s t", t=8))
                pTv2 = [banks[0][:].bitcast(BF16), banks[3][:].bitcast(BF16)]
                pSd = banks[1]
                pAd = banks[6]
                pBd = banks[7]
                pS2 = [banks[1], banks[2]]
                pA2 = [banks[3], banks[4]]
                for hb in range(2):
                    for hd in range(4 * hb, 4 * hb + 4):
                        hp, hh = hd // 2, hd % 2
                        hsl = slice(hh * 64, hh * 64 + 64)
                        dstp = pS2[hb][:, (hd % 4) * 128:(hd % 4 + 1) * 128]
                        mm(dstp, KT[hsl, hp, SEQ:SEQ + 128], QT[hsl, hp, sc0:sc0 + 128], hd % 4 == 0, False, sgc=True)
                        mm(dstp, identb[:], maskN[:], False, True, sgc=True)
                    for hd in range(4 * hb, 4 * hb + 4):
                        dstp = pS2[hb][:, (hd % 4) * 128:(hd % 4 + 1) * 128]
                        act(eN[:, hd * 128:(hd + 1) * 128], dstp, AF.Exp, bias=sbias[:, hd:hd + 1], scale=0.125)
                act(spN, eN, AF.Ln, bias=1.0)
                for bq in range(2):
                    mm(pA2[bq][:], tinc[:], spN[:, bq * 512:(bq + 1) * 512], True, True)
                    act(xdN[:, bq * 512:(bq + 1) * 512], pA2[bq][:], AF.Exp, scale=-1.0)
                tt("dve", wN, eN, xdN, ALU.mult)
                spN3 = spN.rearrange("p (h c) -> p h c", h=8)
                wN3 = wN.rearrange("p (h c) -> p h c", h=8)
                NU = 2 * NSEQ

                def pages(u):
                    return (u // 2, 8 if u % 2 == 0 else 0)

                def M1(u):
                    s_, p0 = pages(u)
                    for jl in range(8):
                        col = s_ * NPAGE + p0 + jl
                        kdst = kring[:, jl, :]
                        P.add("pool", lambda e, kdst=kdst, col=col: e.indirect_dma_start(
                            out=kdst, out_offset=None, in_=ck,
                            in_offset=bass.IndirectOffsetOnAxis(ap=pidx[:, col:col + 1], axis=0)),
                            [V(pidx[:, col:col + 1])], [V(kdst)], dsem=ksem[jl])

                def MV(u):
                    s_, p0 = pages(u)
                    for jl in range(8):
                        col = s_ * NPAGE + p0 + jl
                        vdst = vring[:, jl, :]
                        P.add("pool", lambda e, vdst=vdst, col=col: e.indirect_dma_start(
                            out=vdst, out_offset=None, in_=cv,
                            in_offset=bass.IndirectOffsetOnAxis(ap=pidx[:, col:col + 1], axis=0)),
                            [V(pidx[:, col:col + 1])], [V(vdst)], dsem=vsem[jl])

                def M2(u):
                    s_, p0 = pages(u)

                    def smm(jl):
                        half = jl % 2
                        for hp in range(4):
                            mm(pSd[:, jl * 64 + hp * 16:jl * 64 + hp * 16 + 16],
                               ktp[:, half, hp * 128:(hp + 1) * 128], qblk[:, hp, s_, :], True, True)
                    for jl in range(8):
                        half = jl % 2
                        pTv = pTv2[half]
                        for c in range(4):
                            tr(pTv[:, c * 128:(c + 1) * 128], kring[:, jl, c * 128:(c + 1) * 128], identb[:])
                        cp("dve", ktp[:, half, :], pTv[:, 0:512])
                        if jl >= 1:
                            smm(jl - 1)
                    smm(7)

                def M3a(u):
                    hf = u % 2
                    zb = zbb[:, hf, :]
                    spd = spdb[:, hf, :]
                    stt("dve", zb.rearrange("p (j c) -> p j c", c=64), pSd[:].rearrange("p (j c) -> p j c", c=64),
                        0.125, biasd[:].unsqueeze(1).broadcast_to([128, 8, 64]), ALU.mult, ALU.add)
                    act(zb, zb, AF.Exp)
                    act(spd, zb, AF.Ln, bias=1.0)
                    s3 = spd.rearrange("p (j c) -> p j c", c=64)
                    t1 = sufa.rearrange("p (j c) -> p j c", c=64)
                    t2 = sufb.rearrange("p (j c) -> p j c", c=64)
                    tt("dve", t1[:, 0:7, :], s3[:, 0:7, :], s3[:, 1:8, :], ALU.add)
                    cp("dve", t1[:, 7:8, :], s3[:, 7:8, :])
                    tt("dve", t2[:, 0:6, :], t1[:, 0:6, :], t1[:, 2:8, :], ALU.add)
                    cp("dve", t2[:, 6:8, :], t1[:, 6:8, :])
                    tt("dve", t1[:, 0:4, :], t2[:, 0:4, :], t2[:, 4:8, :], ALU.add)
                    cp("dve", t1[:, 4:8, :], t2[:, 4:8, :])
                    if hf == 0:
                        cp("dve", sphs, t1[:, 0, :])

                def M3b(u):
                    s_, p0 = pages(u)
                    hf = u % 2
                    spd = spdb[:, hf, :]
                    t1 = sufa.rearrange("p (j c) -> p j c", c=64)
                    mm(pAd[:], tinc[:], spd, True, False)
                    mm(pAd[:, 0:7 * 64].rearrange("p (j c) -> p j c", c=64), onesb[:], t1[:, 1:8, :], False, False)
                    if hf == 1:
                        mm(pAd[:].rearrange("p (j c) -> p j c", c=64), onesb[:],
                           sphs.unsqueeze(1).broadcast_to([128, 8, 64]), False, False)
                    newc = spN3[:, :, 8 * s_:8 * s_ + 8].unsqueeze(1).broadcast_to([128, 8, 8, 8])
                    mm(pAd[:].rearrange("p (j h q) -> p j h q", j=8, h=8), onesb[:], newc, False, True)

                def M4a(u):
                    hf = u % 2
                    zb = zbb[:, hf, :]
                    act(xd, pAd[:], AF.Exp, scale=-1.0)
                    tt("dve", wd, zb, xd, ALU.mult)

                def M4b(u):
                    s_, p0 = pages(u)
                    hf = u % 2
                    for jl in range(8):
                        for hp in range(4):
                            mm(pBd[:, hp * 16:hp * 16 + 16], vring[:, jl, hp * 128:(hp + 1) * 128],
                               wd[:, jl * 64 + hp * 16:jl * 64 + hp * 16 + 16], hf == 0 and jl == 0 and hp == 0, False, sgc=True)
                    if hf == 1:
                        for hp in range(4):
                            mm(pBd[:, hp * 16:hp * 16 + 16].rearrange("p (a q) -> p a q", a=2), vs_tok[:, hp * 128:(hp + 1) * 128],
                               wN3[:, 2 * hp:2 * hp + 2, 8 * s_:8 * s_ + 8], False, hp == 3, sgc=True)
                        pbv4 = pBd[:, 0:64].rearrange("p (c x) -> p c x", c=4)
                        cp("dve", cat[0:64, 4:8, sc0 + 8 * s_:sc0 + 8 * s_ + 8], pbv4[0:64, :, 0:8])
                        cp("dve", cat[64:128, 4:8, sc0 + 8 * s_:sc0 + 8 * s_ + 8], pbv4[64:128, :, 8:16])

                NMS = NU + 4
                IPM = max(5, -(-(len(its) + 2) // NMS))

                def micro(m, part):
                    if part == 0:
                        if 0 <= m - 2 < NU:
                            M3b(m - 2)
                    elif part == 1:
                        if 0 <= m - 3 < NU:
                            M4b(m - 3)
                        if 0 <= m - 2 < NU:
                            MV(m - 2)
                    elif part == 2:
                        if 0 <= m - 1 < NU:
                            M2(m - 1)
                        if m < NU:
                            M1(m)
                    elif part == 3:
                        if 0 <= m - 2 < NU:
                            M4a(m - 2)
                    elif part == 4:
                        if 0 <= m - 1 < NU:
                            M3a(m - 1)

                dstate = {"m": 0, "part": 0}

                def dec_hook(step, flush=False):
                    while dstate["m"] < NMS:
                        m, part = dstate["m"], dstate["part"]
                        due = m * IPM + min(part, IPM - 1)
                        if not flush and due > step:
                            break
                        micro(m, part)
                        if part == 4:
                            dstate["m"] += 1
                            dstate["part"] = 0
                        else:
                            dstate["part"] += 1

            n_it = len(its)
            for step in range(n_it + 2):
                if dec_hook is not None:
                    dec_hook(step)
                if step < n_it:
                    stageA(its[step])
                if 0 <= step - 1 < n_it:
                    stageB(its[step - 1])
                if 0 <= step - 2 < n_it:
                    stageC(its[step - 2])
            if dec_hook is not None:
                dec_hook(0, flush=True)

            chk('L0 decode gi=%d' % gi)
            residual_proj(8, cat)
            ffn(0)

            chk('L0 ffn gi=%d' % gi)
            norm_T(2)
            o = 0
            gbb = mixv(o, 4 * T, BF16).rearrange("p (c t) -> p c t", c=4); o += 4 * T * 2
            cxp = mixv(o, 4 * (2 + TP), BF16).rearrange("p (c t) -> p c t", c=4); o += 4 * (2 + TP) * 2
            upx = mixv(o, 4 * (16 + TP), BF16).rearrange("p (c t) -> p c t", c=4)[:, :, 1:16 + TP]; o += 4 * (16 + TP) * 2
            if G["sample"]:
                cxs = mixv(o, 4 * 16 * 10, BF16).rearrange("p (c s t) -> p c s t", c=4, s=16); o += 4 * 160 * 2
                usx = mixv(o, 4 * 16 * 24, BF16).rearrange("p (c s t) -> p c s t", c=4, s=16)[:, :, :, 1:24]; o += 4 * 16 * 24 * 2
            gct = mixv(o, 2 * 512, BF16).rearrange("p (a n) -> p a n", a=2); o += 2048
            acc = mixv(o, 16 + T, F32); o += (16 + T) * 4
            acc2 = mixv(o, 16 + T, F32); o += (16 + T) * 4
            pooled4 = mixv(o, 4 * T, BF16).rearrange("p (g t) -> p g t", g=4); o += 4 * T * 2
            sts = mixv(o, 2 * 512, F32).rearrange("p (a n) -> p a n", a=2); o += 4096
            assert o <= MIXB, o
            if gi == 0:
                mset("pool", cxp[:, :, 0:2], 0.0)
                mset("pool", upx[:, :, 0:15], 0.0)
            else:
                cp("pool", cxp[:, :, 0:2], hist_c[:])
                cp("pool", upx[:, :, 0:15], hist_u[:])
                lload(sts[0:32, 0, :], st_sc)
                pb = nbank()
                for c in range(4):
                    tr(pb[:, c * 128:c * 128 + 32], sts[0:32, 0, c * 128:(c + 1) * 128], identf[0:32, 0:32])
                cp("dve", cxs[:, :, :, 0:2], pb[:].rearrange("p (c x) -> p c x", c=4)[:, :, 0:32].rearrange("p c (s r) -> p c s r", s=16))
                for i in range(2):
                    lload(sts[0:120, 1, :], st_pool[i * 120:(i + 1) * 120, :])
                    pb = nbank()
                    for c in range(4):
                        tr(pb[:, c * 128:c * 128 + 120], sts[0:120, 1, c * 128:(c + 1) * 128], identf[0:120, 0:120])
                    cp("dve", usx[:, :, 8 * i:8 * i + 8, 0:15],
                       pb[:].rearrange("p (c x) -> p c x", c=4)[:, :, 0:120].rearrange("p c (s r) -> p c s r", s=8))
                P.add("sp", lambda e: e.dma_start(out=pool_s[:, 0:7, :], in_=st_pool.rearrange("(s r) f -> s r f", r=15)[:, 8:15, :]),
                      [], [], dsem=P.dsem(out=True))

            chk('L1 norm gi=%d' % gi)
            for b in range(2):
                wv = wget(8)
                for fc in range(2):
                    c = 2 * b + fc
                    for (c0, n) in col_tiles(0, T):
                        pa = nbank()
                        pu = nbank()
                        fm_mm(pa[:, 0:n], wv, fc * 128, c0, n)
                        fm_mm(pu[:, 0:n], wv, 256 + fc * 128, c0, n)
                        cp("act", gbb[:, c, c0:c0 + n], pa[:, 0:n])
                        npp = max(0, min(n, TP - c0))
                        if npp > 0:
                            cp("dve", upx[:, c, 15 + c0:15 + c0 + npp], pu[:, 0:npp])
                        if npp < n:
                            cp("dve", usx[:, c, :, 15:23], pu[:, npp:n].rearrange("p (s t) -> p s t", t=8))
                if gi == 1:
                    for lt, dst in ((npt - 1, "p"), (npt, "s")):
                        pa = nbank()
                        tm_mm(pa[:, 0:256], wv, 256, 256, lt)
                        si = stage_slot()
                        cp("dve", stage[:, si, 0:256], pa[:, 0:256])
                        if dst == "p":
                            dma("sp", pool_p[:, b * 256:(b + 1) * 256], stage[113:128, si, 0:256], osem[si], rd=[stage[:, si, 0:256]])
                        else:
                            for s_ in range(NSEQ):
                                dma("sp", pool_s[s_, 7:15, b * 256:(b + 1) * 256], stage[8 * s_:8 * s_ + 8, si, 0:256], osem[si],
                                    rd=[stage[:, si, 0:256]])
            def pool_sums(g):
                w = (2, 4, 8, 16)[g]
                pooled = pooled4[:, g, :]
                L = 15 + TP
                src = upx[:, g, 0:L]
                cur, nxt = acc[:, 0:L], acc2[:, 0:L]
                sh = 1
                first = True
                while sh < w:
                    a_in = src if first else cur
                    eng = "dve"
                    if not first:
                        cp(eng, nxt[:, 0:sh], a_in[:, 0:sh])
                    tt(eng, nxt[:, sh:L], a_in[:, sh:L], a_in[:, 0:L - sh], ALU.add)
                    cur, nxt = nxt, cur
                    first = False
                    sh *= 2
                stt("dve", pooled[:, 0:TP], cur[:, 15:15 + TP], 1.0 / w, upx[:, g, 15:15 + TP], ALU.mult, ALU.subtract)
                if gi == 0:
                    tt("dve", nxt[:, 0:16], cur[:, 15:31], rc[:, g, :], ALU.mult)
                    tt("dve", pooled[:, 0:16], nxt[:, 0:16], upx[:, g, 15:31], ALU.subtract)
                if G["sample"]:
                    s3 = usx[:, g, :, :]
                    c3 = acc[:, 0:16 * 23].rearrange("p (s t) -> p s t", t=23)
                    n3 = acc2[:, 0:16 * 23].rearrange("p (s t) -> p s t", t=23)
                    sh = 1
                    first = True
                    while sh < w:
                        a_in = s3 if first else c3
                        if not first:
                            cp("pool", n3[:, :, 0:sh], a_in[:, :, 0:sh])
                        tt("pool", n3[:, :, sh:23], a_in[:, :, sh:23], a_in[:, :, 0:23 - sh], ALU.add)
                        c3, n3 = n3, c3
                        first = False
                        sh *= 2
                    stt("dve", pooled[:, TP:T].rearrange("p (s t) -> p s t", t=8), c3[:, :, 15:23], 1.0 / w, usx[:, g, :, 15:23],
                        ALU.mult, ALU.subtract)

            for b in range(2):
                wv = wget(8)
                for fc in range(2):
                    c = 2 * b + fc
                    for (c0, n) in col_tiles(0, T):
                        pa = nbank()
                        pg = nbank()
                        fm_mm(pa[:, 0:n], wv, fc * 128, c0, n)
                        fm_mm(pg[:, 0:n], wv, 256 + fc * 128, c0, n)
                        k = bank_rr["n"] % 2
                        cp("act", gct[:, k, 0:n], pa[:, 0:n])
                        npp = max(0, min(n, TP - c0))
                        if npp > 0:
                            tt("dve", cxp[:, c, 2 + c0:2 + c0 + npp], pg[:, 0:npp], gct[:, k, 0:npp], ALU.mult)
                        if npp < n:
                            tt("dve", cxs[:, c, :, 2:10], pg[:, npp:n].rearrange("p (s t) -> p s t", t=8),
                               gct[:, k, npp:n].rearrange("p (s t) -> p s t", t=8), ALU.mult)
                    pool_sums(c)
                if gi == 1:
                    for lt, dst in ((npt - 1, "p"), (npt, "s")):
                        pa = nbank()
                        tm_mm(pa[:], wv, 0, 512, lt)
                        si = stage_slot()
                        cp("act", stage[:, si, 0:256], pa[:, 0:256])
                        tt("dve", stage[:, si, 256:512], pa[:, 256:512], stage[:, si, 0:256], ALU.mult)
                        if dst == "p":
                            dma("sp", sc_p[:, b * 256:(b + 1) * 256], stage[126:128, si, 256:512], osem[si], rd=[stage[:, si, 256:512]])
                        else:
                            for s_ in range(NSEQ):
                                dma("sp", sc_s[s_, :, b * 256:(b + 1) * 256], stage[8 * s_ + 6:8 * s_ + 8, si, 256:512], osem[si],
                                    rd=[stage[:, si, 256:512]])
            if gi == 0:
                cp("pool", hist_c[:], cxp[:, :, TP:TP + 2])
                cp("pool", hist_u[:], upx[:, :, TP:TP + 15])

            chk('L1 proj gi=%d' % gi)
            for c in range(4):
                a_ = acc[:, 0:TP]
                ts("dve", a_, cxp[:, c, 0:TP], chv[:, c, 34:35], ALU.mult)
                stt("dve", a_, cxp[:, c, 1:1 + TP], chv[:, c, 35:36], a_, ALU.mult, ALU.add)
                stt("dve", a_, cxp[:, c, 2:2 + TP], chv[:, c, 36:37], a_, ALU.mult, ALU.add)
                tt("dve", cat[:, c, 0:TP], a_, gbb[:, c, 0:TP], ALU.mult)
                if G["sample"]:
                    a3 = acc2[:, 0:128].rearrange("p (s t) -> p s t", t=8)
                    ts("dve", a3, cxs[:, c, :, 0:8], chv[:, c, 34:35], ALU.mult)
                    stt("dve", a3, cxs[:, c, :, 1:9], chv[:, c, 35:36], a3, ALU.mult, ALU.add)
                    stt("dve", a3, cxs[:, c, :, 2:10], chv[:, c, 36:37], a3, ALU.mult, ALU.add)
                    tt("dve", cat[:, c, TP:T], acc2[:, 0:128], gbb[:, c, TP:T], ALU.mult)
            for g in range(4):
                for (c0, n) in col_tiles(0, T):
                    pb = nbank()
                    mm(pb[:, 0:n], poolw[:, g, :], pooled4[:, g, c0:c0 + n], True, True)
                    act(cat[:, 4 + g, c0:c0 + n], pb[:, 0:n], AF.Copy, scale=chv[:, g, 37:38])

            residual_proj(8, cat)
            ffn(1)

            chk('L1 mixer+ffn gi=%d' % gi)
            gslot = load_g(4)
            c0 = stat_col(3 * ntl)
            for lt in range(ntl):
                act(xn[:, lt % 2, :], h[:, lt, :], AF.Square, accum=stats[:, c0 + lt:c0 + lt + 1])
            act(stats[:, c0 + ntl:c0 + 2 * ntl], stats[:, c0:c0 + ntl], AF.Ln, bias=epst[:, 0:1], scale=1.0 / D)
            act(stats[:, c0 + 2 * ntl:c0 + 3 * ntl], stats[:, c0 + ntl:c0 + 2 * ntl], AF.Exp, scale=-0.5)
            for lt in range(ntl):
                hv = h[:, lt, :]
                si = stage_slot()
                stt("dve", stage[:, si, :], hv, stats[:, c0 + 2 * ntl + lt:c0 + 2 * ntl + lt + 1], gb[:, gslot, :], ALU.mult, ALU.mult)
                dst = yp[ptiles[lt] * 128:(ptiles[lt] + 1) * 128, :] if lt < npt else ys
                dma("sp", dst, stage[:, si, :], osem[si], rd=[stage[:, si, :]])

    try:
        chk('consts')
        run_groups()
    except _Stop as e_:
        print('STOPPED at', e_)
    if limit is None:
        assert wstate["cur"] == len(WSPECS), (wstate, len(WSPECS))
    P.emit()
    es.close()
    return nc


def make_in_maps(inp, compact_cache=False):
    f = lambda a: np.ascontiguousarray(np.asarray(a))
    gvec = f(np.stack([inp["norm_mix_g"][0], inp["norm_ffn_g"][0], inp["norm_mix_g"][1], inp["norm_ffn_g"][1],
                       inp["norm_final_g"]]))
    chv = f(np.concatenate([inp["conv_a_w"][0], inp["conv_a_b"], inp["ln_a_g"], inp["ln_a_b"], inp["conv_c_w"][0],
                            inp["pool_scale"]], axis=0))
    shared = {
        "gvec": gvec, "chv": chv, "sbb": f(inp["sb_bias"]).reshape(1, 8),
        "poolw": f(inp["pool_w"]).reshape(512, 128),
        "w_in_e": f(inp["w_in_even"][0]), "w_out_e": f(inp["w_out_even"][0]),
        "w_in_o": f(inp["w_in_odd"][0]), "w_out_o": f(inp["w_out_odd"][0]),
        "w_ffn_in": f(inp["w_ffn_in"]).reshape(2 * D, 2 * FFN), "w_ffn_out": f(inp["w_ffn_out"]).reshape(2 * FFN, D),
    }
    ck_full = np.asarray(inp["cache_k"])[0].reshape(-1, 512)
    cv_full = np.asarray(inp["cache_v"])[0].reshape(-1, 512)
    maps = []
    for c in range(NCORES):
        sl = slice(NSEQ * c, NSEQ * (c + 1))
        m = dict(shared)
        m["xp"] = f(inp["x_prompt"][c])
        m["xs"] = f(inp["x_sample"][sl]).reshape(128, D)
        m["st_conf"] = f(inp["state_conformer"][0, sl]).reshape(NSEQ * 30, 512)
        m["st_sc"] = f(inp["state_shortconv"][0, sl]).reshape(NSEQ * 2, 512)
        m["st_pool"] = f(inp["state_pool"][0, sl]).reshape(NSEQ * 15, 512)
        ptc = np.asarray(inp["page_table"])[sl].astype(np.int32)
        if compact_cache:
            pages = ptc.reshape(-1)
            m["ck"] = f(ck_full.reshape(-1, 128, 512)[pages]).reshape(-1, 512)
            m["cv"] = f(cv_full.reshape(-1, 128, 512)[pages]).reshape(-1, 512)
            m["pt"] = np.arange(256, dtype=np.int32).reshape(NSEQ, NPAGE)
        else:
            m["ck"] = ck_full
            m["cv"] = cv_full
            m["pt"] = f(ptc)
        maps.append(m)
    return maps


def gather_outputs(res):
    R = res.results
    cat = lambda k: np.stack([R[c][k] for c in range(NCORES)])
    y_p = cat("yp")
    y_s = np.concatenate([R[c]["ys"].reshape(NSEQ, DSEQ, D) for c in range(NCORES)])
    k_p = cat("kp").reshape(1, NCORES, SEQ, 8, 64)
    v_p = cat("vp").reshape(1, NCORES, SEQ, 8, 64)
    k_s = np.concatenate([R[c]["ks"].reshape(NSEQ, DSEQ, 8, 64) for c in range(NCORES)])[None]
    v_s = np.concatenate([R[c]["vs"].reshape(NSEQ, DSEQ, 8, 64) for c in range(NCORES)])[None]
    conv_p = cat("conv_p")[None]
    conv_s = np.concatenate([R[c]["conv_s"] for c in range(NCORES)])[None]
    sc_p = cat("sc_p")[None]
    sc_s = np.concatenate([R[c]["sc_s"] for c in range(NCORES)])[None]
    pool_p = cat("pool_p")[None]
    pool_s = np.concatenate([R[c]["pool_s"] for c in range(NCORES)])[None]
    outs = (y_p, y_s, k_p, v_p, k_s, v_s, conv_p, conv_s, sc_p, sc_s, pool_p, pool_s)
    return tuple(np.ascontiguousarray(o, dtype=np.float32) for o in outs)


def kernel(**inputs):
    npool = int(np.asarray(inputs["cache_k"]).shape[1])
    nc = build_nc(npool)
    in_maps = make_in_maps(inputs)
    res = run_bass_kernel_spmd(nc, in_maps, core_ids=list(range(NCORES)))
    return gather_outputs(res)
```
